# Optimizing a Trainium2 kernel written in Bass

```python
import math
import jax, jax.numpy as jnp
from jax import lax
import numpy as np

D_MODEL = 1024
BATCH = 4
SEQ = 8192
DEPTH = 4

GRID_W = 64
CTX_LEN = 256
N_MOD = 9
D_FF = 2816
MACARON = 0.5
A_HEADS = 8
QK_NOPE = 64
QK_ROPE = 32
V_DIM = 64
Q_RANK = 384
KV_RANK = 256
A_WIDTH = A_HEADS * V_DIM
ATTN_SCALE = 1.0 / math.sqrt(QK_NOPE + QK_ROPE)
ROPE_BASE = 10000.0
Q_BLOCK = 128
CHUNK = 128
B_GROUPS = 4
B_GROUP_CH = 128
B_WIDTH = B_GROUPS * B_GROUP_CH
OFF_KV = Q_RANK
OFF_KR = Q_RANK + KV_RANK
OFF_B = OFF_KR + QK_ROPE
IN_E = OFF_B + 2 * B_WIDTH
CONV_W = 31
LN_EPS = 1e-5
RMS_EPS = 1e-6
ALPHA = (2 * DEPTH) ** 0.25
BETA = (8 * DEPTH) ** -0.25
N_EVEN = (DEPTH + 1) // 2
N_ODD = DEPTH // 2

kernel_name = "hybrid_mla_gmlp_conformer_dit"


def layer_norm(x, g, b):
    xf = x.astype(jnp.float32)
    mu = jnp.mean(xf, axis=-1, keepdims=True)
    var = jnp.mean(jnp.square(xf - mu), axis=-1, keepdims=True)
    return ((xf - mu) * lax.rsqrt(var + LN_EPS)).astype(x.dtype) * g + b


def rms_norm(x, g):
    xf = x.astype(jnp.float32)
    return (xf * lax.rsqrt(jnp.mean(xf * xf, axis=-1, keepdims=True) + RMS_EPS)).astype(x.dtype) * g


def adaln(cond, w_mod, b_mod):
    m = jax.nn.silu(cond) @ w_mod + b_mod
    return m.reshape(cond.shape[:-1] + (N_MOD, D_MODEL))


def modulate(h, mod, idx):
    return h * (1 + mod[:, 3 * idx + 1][:, None]) + mod[:, 3 * idx][:, None]


def residual(h, y, mod, idx, g, b, weight):
    return layer_norm(ALPHA * h + weight * mod[:, 3 * idx + 2][:, None] * y, g, b)


def swiglu(h, w13, w2):
    gt, up = jnp.split(h @ w13, 2, axis=-1)
    return (jax.nn.silu(gt) * up) @ w2


def ffn_sublayer(h, mod, idx, w13, w2, g, b):
    return residual(h, swiglu(modulate(h, mod, idx), w13, w2), mod, idx, g, b, MACARON)


def axial_rope(rows, cols):
    half = QK_ROPE // 2
    inv = 1.0 / (ROPE_BASE ** (jnp.arange(0, half, 2, dtype=jnp.float32) / half))
    ang = jnp.concatenate([rows[:, None] * inv, cols[:, None] * inv], axis=-1)
    return jnp.cos(ang), jnp.sin(ang)


def apply_rope(x, cos, sin):
    xp = x.reshape(x.shape[:-1] + (QK_ROPE // 2, 2))
    x0, x1 = xp[..., 0], xp[..., 1]
    extra = (1,) * (x.ndim - 3)
    c = cos.reshape(cos.shape[:1] + extra + cos.shape[1:]).astype(x.dtype)
    s = sin.reshape(sin.shape[:1] + extra + sin.shape[1:]).astype(x.dtype)
    return jnp.stack([x0 * c - x1 * s, x0 * s + x1 * c], axis=-1).reshape(x.shape)


def mla_queries(cq, q_norm_g, w_q_up):
    q = (rms_norm(cq, q_norm_g) @ w_q_up).reshape(cq.shape[:2] + (A_HEADS, QK_NOPE + QK_ROPE))
    return q[..., :QK_NOPE], q[..., QK_NOPE:]


def mla_keys_values(ckv, kv_norm_g, w_kv_up):
    kv = (rms_norm(ckv, kv_norm_g) @ w_kv_up).reshape(ckv.shape[:2] + (A_HEADS, QK_NOPE + V_DIM))
    return kv[..., :QK_NOPE], kv[..., QK_NOPE:]


def mla_attend(q_nope, q_rope, k_nope, k_rope, v):
    s = (jnp.einsum('bqhd,bkhd->bhqk', q_nope, k_nope)
         + jnp.einsum('bqhr,bkr->bhqk', q_rope, k_rope)) * ATTN_SCALE
    p = jax.nn.softmax(s.astype(jnp.float32), axis=-1).astype(v.dtype)
    return jnp.einsum('bhqk,bkhd->bqhd', p, v)


def mla_latent_attention(q_nope, q_rope, k_nope, k_rope, v):
    bn, length = q_nope.shape[:2]
    nb = length // Q_BLOCK

    def to_blocks(t):
        return jnp.moveaxis(t.reshape((bn, nb, Q_BLOCK) + t.shape[2:]), 1, 0)

    out = lax.map(lambda qb: mla_attend(qb[0], qb[1], k_nope, k_rope, v),
                  (to_blocks(q_nope), to_blocks(q_rope)))
    return jnp.moveaxis(out, 0, 1).reshape(bn, length, A_WIDTH)


def spatial_gating(zb, sgu_g, sgu_b, w_s, b_s):
    u, v = jnp.split(jax.nn.gelu(zb, approximate=False), 2, axis=-1)
    v = layer_norm(v, sgu_g, sgu_b)
    bn, length = v.shape[:2]
    vc = v.reshape(bn, length // CHUNK, CHUNK, B_GROUPS, B_GROUP_CH)
    mixed = jnp.einsum('gij,bnjgc->bnigc', w_s, vc) + b_s.T[:, :, None]
    return u * mixed.reshape(bn, length, B_WIDTH)


def even_mixer(h_lat, h_ctx, ctx_out, cos, sin, w_in, q_norm_g, kv_norm_g, w_q_up, w_kv_up,
               sgu_g, sgu_b, w_s, b_s, w_out):
    cq, ckv, kr, zb = jnp.split(h_lat @ w_in, [OFF_KV, OFF_KR, OFF_B], axis=-1)
    if ctx_out:
        cq_c, ckv_c, kr_c, zb_c = jnp.split(h_ctx @ w_in, [OFF_KV, OFF_KR, OFF_B], axis=-1)
    else:
        ckv_c, kr_c = jnp.split(h_ctx @ w_in[:, OFF_KV:OFF_B], [KV_RANK], axis=-1)
    kn_c, v_c = mla_keys_values(ckv_c, kv_norm_g, w_kv_up)
    kn_l, v_l = mla_keys_values(ckv, kv_norm_g, w_kv_up)
    kr_l = apply_rope(kr, cos, sin)
    qn_l, qr_l = mla_queries(cq, q_norm_g, w_q_up)
    qr_l = apply_rope(qr_l, cos, sin)
    k_nope = jnp.concatenate([kn_c, kn_l], axis=1)
    k_rope = jnp.concatenate([kr_c, kr_l], axis=1)
    v_all = jnp.concatenate([v_c, v_l], axis=1)
    a_lat = mla_latent_attention(qn_l, qr_l, k_nope, k_rope, v_all)
    b_lat = spatial_gating(zb, sgu_g, sgu_b, w_s, b_s)
    y_lat = jnp.concatenate([a_lat, b_lat], axis=-1) @ w_out
    if not ctx_out:
        return y_lat, None
    qn_c, qr_c = mla_queries(cq_c, q_norm_g, w_q_up)
    a_ctx = mla_attend(qn_c, qr_c, kn_c, kr_c, v_c).reshape(h_ctx.shape[0], h_ctx.shape[1], A_WIDTH)
    b_ctx = spatial_gating(zb_c, sgu_g, sgu_b, w_s, b_s)
    y_ctx = jnp.concatenate([a_ctx, b_ctx], axis=-1) @ w_out
    return y_lat, y_ctx


def conv_mixer(h, w_pw1, b_pw1, w_dw, b_dw, cg, cb, w_out, b_out):
    a, gt = jnp.split(h @ w_pw1 + b_pw1, 2, axis=-1)
    y = a * jax.nn.sigmoid(gt)
    y = lax.conv_general_dilated(y, w_dw[:, None, :], window_strides=(1,), padding='SAME',
                                 dimension_numbers=('NWC', 'WIO', 'NWC'),
                                 feature_group_count=D_MODEL) + b_dw
    y = jax.nn.silu(layer_norm(y, cg, cb))
    return y @ w_out + b_out


def _ctx_needed(l):
    return any(j % 2 == 0 for j in range(l, DEPTH))


def setup_inputs(seed: int = 0) -> dict:
    key = jax.random.key(seed)
    ks = iter(jax.random.split(key, 40))

    def nrm(shape, s):
        return jax.random.normal(next(ks), shape, jnp.float32) * s

    D = D_MODEL
    kv_scale = jnp.concatenate([jnp.ones((QK_NOPE,), jnp.float32), jnp.full((V_DIM,), BETA, jnp.float32)])
    w_kv_up = (nrm((N_EVEN, KV_RANK, A_HEADS, QK_NOPE + V_DIM), KV_RANK ** -0.5) * kv_scale
               ).reshape(N_EVEN, KV_RANK, A_HEADS * (QK_NOPE + V_DIM))
    return {
        "x": nrm((BATCH, SEQ, D), 1.0),
        "c": nrm((BATCH, D), 1.0),
        "ctx": nrm((BATCH, CTX_LEN, D), 1.0),
        "c_ctx": nrm((D,), 1.0),
        "w_mod": nrm((DEPTH, D, N_MOD * D), 0.5 * D ** -0.5),
        "b_mod": nrm((DEPTH, N_MOD * D), 0.02),
        "ln_g": 1.0 + nrm((DEPTH, 3, D), 0.02),
        "ln_b": nrm((DEPTH, 3, D), 0.02),
        "ffn_w13": nrm((DEPTH, 2, D, 2 * D_FF), D ** -0.5),
        "ffn_w2": nrm((DEPTH, 2, D_FF, D), BETA * D_FF ** -0.5),
        "e_w_in": nrm((N_EVEN, D, IN_E), D ** -0.5),
        "e_q_norm": 1.0 + nrm((N_EVEN, Q_RANK), 0.02),
        "e_kv_norm": 1.0 + nrm((N_EVEN, KV_RANK), 0.02),
        "e_w_q_up": nrm((N_EVEN, Q_RANK, A_HEADS * (QK_NOPE + QK_ROPE)), Q_RANK ** -0.5),
        "e_w_kv_up": w_kv_up,
        "e_sgu_g": 1.0 + nrm((N_EVEN, B_WIDTH), 0.02),
        "e_sgu_b": nrm((N_EVEN, B_WIDTH), 0.02),
        "e_w_s": nrm((N_EVEN, B_GROUPS, CHUNK, CHUNK), CHUNK ** -0.5),
        "e_b_s": 1.0 + nrm((N_EVEN, B_GROUPS, CHUNK), 0.02),
        "e_w_out": nrm((N_EVEN, D, D), BETA * D ** -0.5),
        "o_w_pw1": nrm((N_ODD, D, 2 * D), D ** -0.5),
        "o_b_pw1": nrm((N_ODD, 2 * D), 0.02),
        "o_w_dw": nrm((N_ODD, CONV_W, D), CONV_W ** -0.5),
        "o_b_dw": nrm((N_ODD, D), 0.02),
        "o_ln_g": 1.0 + nrm((N_ODD, D), 0.02),
        "o_ln_b": nrm((N_ODD, D), 0.02),
        "o_w_out": nrm((N_ODD, D, D), BETA * D ** -0.5),
        "o_b_out": nrm((N_ODD, D), 0.02),
    }


def reference(x, c, ctx, c_ctx, w_mod, b_mod, ln_g, ln_b, ffn_w13, ffn_w2,
              e_w_in, e_q_norm, e_kv_norm, e_w_q_up, e_w_kv_up, e_sgu_g, e_sgu_b, e_w_s, e_b_s, e_w_out,
              o_w_pw1, o_b_pw1, o_w_dw, o_b_dw, o_ln_g, o_ln_b, o_w_out, o_b_out):
    length = x.shape[1]
    ROWS = length // GRID_W
    rows = jnp.repeat(jnp.arange(ROWS, dtype=jnp.float32), GRID_W)
    cols = jnp.tile(jnp.arange(GRID_W, dtype=jnp.float32), ROWS)
    cos, sin = axial_rope(rows, cols)

    lat, cx = x, ctx
    for l in range(DEPTH):
        ctx_in = _ctx_needed(l)
        ctx_out = _ctx_needed(l + 1)
        m_lat = adaln(c, w_mod[l], b_mod[l])
        lat = ffn_sublayer(lat, m_lat, 0, ffn_w13[l, 0], ffn_w2[l, 0], ln_g[l, 0], ln_b[l, 0])
        h_lat = modulate(lat, m_lat, 1)
        h_ctx = None
        if ctx_in:
            m_ctx = adaln(c_ctx[None], w_mod[l], b_mod[l])
            cx = ffn_sublayer(cx, m_ctx, 0, ffn_w13[l, 0], ffn_w2[l, 0], ln_g[l, 0], ln_b[l, 0])
            h_ctx = modulate(cx, m_ctx, 1)
        if l % 2 == 0:
            e = l // 2
            y_lat, y_ctx = even_mixer(h_lat, h_ctx, ctx_out, cos, sin, e_w_in[e], e_q_norm[e], e_kv_norm[e],
                                      e_w_q_up[e], e_w_kv_up[e], e_sgu_g[e], e_sgu_b[e], e_w_s[e], e_b_s[e],
                                      e_w_out[e])
        else:
            o = l // 2
            conv_args = (o_w_pw1[o], o_b_pw1[o], o_w_dw[o], o_b_dw[o], o_ln_g[o], o_ln_b[o], o_w_out[o], o_b_out[o])
            y_lat = conv_mixer(h_lat, *conv_args)
            y_ctx = conv_mixer(h_ctx, *conv_args) if ctx_out else None
        lat = residual(lat, y_lat, m_lat, 1, ln_g[l, 1], ln_b[l, 1], 1.0)
        lat = ffn_sublayer(lat, m_lat, 2, ffn_w13[l, 1], ffn_w2[l, 1], ln_g[l, 2], ln_b[l, 2])
        if ctx_out:
            cx = residual(cx, y_ctx, m_ctx, 1, ln_g[l, 1], ln_b[l, 1], 1.0)
            cx = ffn_sublayer(cx, m_ctx, 2, ffn_w13[l, 1], ffn_w2[l, 1], ln_g[l, 2], ln_b[l, 2])
    return lat
```

```python
import math
import contextlib
import numpy as np
import concourse.bass as bass
import concourse.mybir as mybir
from concourse.bass_utils import run_bass_kernel_spmd

F32 = mybir.dt.float32
BF16 = mybir.dt.bfloat16
AF = mybir.ActivationFunctionType
ALU = mybir.AluOpType
ENGS = ["sync", "scalar", "vector", "gpsimd", "tensor"]
NDMA = 56

D = 1024
NCTX = 256
DFF = 2816
NJ = 22
ALPHA = 8.0 ** 0.25
LN_EPS = 1e-5
RMS_EPS = 1e-6
ATT_SCALE = 1.0 / math.sqrt(96.0)
GRID_W = 64
WIN_COLS = 1728
X_OVERLAP = False


class Buf:
    __slots__ = ("name", "w", "r", "p")

    def __init__(self, name=""):
        self.name = name
        self.w = {}
        self.r = {}
        self.p = {}


def bufs(n, name=""):
    return [Buf("%s%d" % (name, i)) for i in range(n)]


class Op:
    __slots__ = ("eng", "fn", "waits", "flag", "seq", "semval", "dma", "cc")

    def __init__(self, eng, fn):
        self.eng = eng
        self.fn = fn
        self.waits = []
        self.flag = False
        self.semval = 0
        self.dma = None
        self.cc = None


class Prog:
    def __init__(self, nc):
        self.nc = nc
        self.streams = {e: [] for e in ENGS}
        self.waited = {e: {} for e in ENGS}
        self.dma_val = [0] * NDMA
        self.dma_rr = 0
        self.ncc = 0

    @staticmethod
    def _merge(dst, src):
        for k, v in src.items():
            if dst.get(k, -1) < v:
                dst[k] = v

    def _finish(self, op, deps, reads, writes, adds, ev_key, ev_val):
        eng = op.eng
        wd = self.waited[eng]
        for k, v in deps.items():
            if k[0] == "e" and k[1] == eng and eng in ("tensor", "sync"):
                continue
            if wd.get(k, -1) >= v:
                continue
            wd[k] = v
            op.waits.append((k, v))
            if k[0] == "e":
                self.streams[k[1]][v].flag = True
        for b in reads:
            if b.r.get(ev_key, -1) < ev_val:
                b.r[ev_key] = ev_val
        for b in writes:
            pg = dict(b.w)
            self._merge(pg, b.r)
            b.p = pg
            b.w = {ev_key: ev_val}
            b.r = {}
        for b in adds:
            if b.w.get(ev_key, -1) < ev_val:
                b.w[ev_key] = ev_val

    def _deps(self, reads, writes, adds):
        deps = {}
        for b in reads:
            self._merge(deps, b.w)
        for b in writes:
            self._merge(deps, b.w)
            self._merge(deps, b.r)
        for b in adds:
            self._merge(deps, b.r)
            self._merge(deps, b.p)
        return deps

    def op(self, eng, fn, reads=(), writes=(), adds=()):
        o = Op(eng, fn)
        st = self.streams[eng]
        o.seq = len(st)
        deps = self._deps(reads, writes, adds)
        st.append(o)
        self._finish(o, deps, reads, writes, adds, ("e", eng), o.seq)
        return o

    def dma(self, q, out, in_, reads=(), writes=(), adds=()):
        s = self.dma_rr
        self.dma_rr = (self.dma_rr + 1) % NDMA
        o = Op(q, lambda e: e.dma_start(out=out, in_=in_))
        st = self.streams[q]
        o.seq = len(st)
        deps = self._deps(reads, writes, adds)
        prev = self.dma_val[s]
        if prev > 0:
            k = ("d", s)
            if deps.get(k, -1) < prev:
                deps[k] = prev
        self.dma_val[s] = prev + 16
        o.dma = s
        st.append(o)
        self._finish(o, deps, reads, writes, adds, ("d", s), prev + 16)
        return o

    def coll(self, fn, reads=(), writes=()):
        idx = self.ncc
        self.ncc += 1
        o = Op("gpsimd", fn)
        st = self.streams["gpsimd"]
        o.seq = len(st)
        deps = self._deps(reads, writes, ())
        o.cc = idx
        o.dma = -1
        st.append(o)
        self._finish(o, deps, reads, writes, (), ("c", idx), 1)
        return o

    def barrier(self):
        deps = {}
        for e in ENGS:
            n = len(self.streams[e])
            if n and e != "sync":
                for i in range(n - 1, -1, -1):
                    if self.streams[e][i].dma is None and self.streams[e][i].fn is not None:
                        deps[("e", e)] = i
                        break
        for s in range(NDMA):
            if self.dma_val[s] > 0:
                deps[("d", s)] = self.dma_val[s]
        for i in range(self.ncc):
            deps[("c", i)] = 1
        for e in ENGS:
            o = Op(e, None)
            o.seq = len(self.streams[e])
            self.streams[e].append(o)
            self._finish(o, dict(deps), (), (), (), ("e", e), o.seq)

    def emit(self):
        nc = self.nc
        self.barrier()
        for e in ENGS:
            c = 0
            for o in self.streams[e]:
                if o.flag and o.dma is None:
                    c += 1
                    o.semval = c
        with contextlib.ExitStack() as es:
            esem = {e: es.enter_context(nc.semaphore("es_" + e)) for e in ENGS}
            dsem = [es.enter_context(nc.semaphore("ds_%d" % i)) for i in range(NDMA)]
            csem = [es.enter_context(nc.semaphore("cs_%d" % i)) for i in range(self.ncc)]
            block = es.enter_context(nc.Block())

            def run(engname):
                def body(eng):
                    for o in self.streams[engname]:
                        for k, v in o.waits:
                            if k[0] == "e":
                                eng.wait_ge(esem[k[1]], self.streams[k[1]][v].semval)
                            elif k[0] == "c":
                                eng.wait_ge(csem[k[1]], v)
                            else:
                                eng.wait_ge(dsem[k[1]], v)
                        if o.fn is None:
                            if o.flag:
                                eng.nop().then_inc(esem[engname], 1)
                            continue
                        ins = o.fn(eng)
                        if o.cc is not None:
                            ins.then_inc(csem[o.cc], 1)
                        elif o.dma is not None:
                            ins.then_inc(dsem[o.dma], 16)
                        elif o.flag:
                            ins.then_inc(esem[engname], 1)
                return body

            block.sync(run("sync"))
            block.scalar(run("scalar"))
            block.vector(run("vector"))
            block.gpsimd(run("gpsimd"))
            block.tensor(run("tensor"))


def stage_list():
    L = [("PRO",)]
    for l in range(4):
        ci = l <= 2
        co = l <= 1
        L.append(("F", l, 0, ci))
        if l % 2 == 0:
            L.append(("E2", l, ci, co))
            L.append(("X", l))
            L.append(("E4", l, co))
            L.append(("E5", l, co))
        else:
            L.append(("O1", l, co))
            L.append(("X", l))
            L.append(("O2", l, co))
        L.append(("F", l, 2, co))
    L.append(("EPI",))
    return L


def segments():
    segs = [[]]
    for st in stage_list():
        if st[0] == "X":
            segs.append([])
        else:
            segs[-1].append(st)
    return segs


class Builder:
    def __init__(self, nlat, stages, state_in, state_out, fused=False, ncores=8):
        self.fused = fused
        self.ncores = ncores
        self.nc = bass.Bass("TRN2", target_bir_lowering=False)
        self.p = Prog(self.nc)
        self.nlat = nlat
        self.N = nlat + NCTX
        self.nk = NCTX + 2 * nlat
        self.stages = stages
        self.state_in = set(state_in)
        self.state_out = set(state_out)
        self.dram = {}
        self.inputs = []
        self.outputs = []
        self.wdone = set()
        self.groups_lat = [(g * 512, 512, 0) for g in range(nlat // 512)]
        self.group_ctx = (nlat, NCTX, 1)
        self.res_written = set()
        self.x_done = set()
        self.x_pending = []

    def din(self, name, shape, dt=F32):
        if name not in self.dram:
            self.dram[name] = self.nc.dram_tensor(name, list(shape), dt, kind="ExternalInput").ap()
            self.inputs.append(name)
        return self.dram[name]

    def dtmp(self, name, shape, dt):
        if name not in self.dram:
            self.dram[name] = self.nc.dram_tensor(name, list(shape), dt).ap()
        return self.dram[name]

    def dout(self, name, shape, dt=F32):
        if name not in self.dram:
            self.dram[name] = self.nc.dram_tensor(name, list(shape), dt, kind="ExternalOutput").ap()
            self.outputs.append(name)
        return self.dram[name]

    def state(self, name, shape, dt, write):
        if write:
            if name in self.state_out:
                return self.dout(name + "_o", shape, dt)
            return self.dtmp(name + "_t", shape, dt)
        if (name + "_o") in self.dram:
            return self.dram[name + "_o"]
        if (name + "_t") in self.dram:
            return self.dram[name + "_t"]
        assert name in self.state_in, name
        return self.din(name + "_i", shape, dt)

    def sbuf(self, key):
        if key not in self.dbufs:
            self.dbufs[key] = Buf(str(key))
        return self.dbufs[key]

    def reset_arena(self):
        self.f_off = self.f_base

    def af(self, *shape):
        n = int(np.prod(shape[1:]))
        off = self.f_off
        self.f_off += n
        assert self.f_off <= self.f_size, ("arena overflow", self.f_off)
        ap = self.AF[:, off:off + n]
        if len(shape) == 3:
            ap = ap.rearrange("p (a b) -> p a b", a=shape[1])
        elif len(shape) == 4:
            ap = ap.rearrange("p (a b c) -> p a b c", a=shape[1], b=shape[2])
        return ap

    def ab(self, *shape):
        n = int(np.prod(shape[1:]))
        nf = (n + 1) // 2
        off = self.f_off
        self.f_off += nf
        assert self.f_off <= self.f_size, ("arena overflow", self.f_off)
        ap = self.AF[:, off:off + nf].bitcast(BF16)[:, 0:n]
        if len(shape) == 3:
            ap = ap.rearrange("p (a b) -> p a b", a=shape[1])
        elif len(shape) == 4:
            ap = ap.rearrange("p (a b c) -> p a b c", a=shape[1], b=shape[2])
        return ap

    def act(self, out, in_, func, R, W=(), A=(), bias=None, scale=1.0):
        if bias is None:
            self.p.op("scalar", lambda e: e.activation(out=out, in_=in_, func=func, scale=scale), R, W, A)
        else:
            self.p.op("scalar", lambda e: e.activation(out=out, in_=in_, func=func, bias=bias, scale=scale), R, W, A)

    def tt(self, eng, out, in0, in1, op, R, W=(), A=()):
        self.p.op(eng, lambda e: e.tensor_tensor(out=out, in0=in0, in1=in1, op=op), R, W, A)

    def ts(self, eng, out, in0, s1, s2, op0, op1, R, W=(), A=()):
        self.p.op(eng, lambda e: e.tensor_scalar(out=out, in0=in0, scalar1=s1, scalar2=s2, op0=op0, op1=op1), R, W, A)

    def ts1(self, eng, out, in0, s1, op0, R, W=(), A=()):
        self.p.op(eng, lambda e: e.tensor_single_scalar(out=out, in_=in0, scalar=s1, op=op0), R, W, A)

    def stt(self, eng, out, in0, scalar, in1, op0, op1, R, W=(), A=()):
        self.p.op(eng, lambda e: e.scalar_tensor_tensor(out=out, in0=in0, scalar=scalar, in1=in1, op0=op0, op1=op1), R, W, A)

    def cp(self, eng, out, in_, R, W=(), A=()):
        if eng == "scalar":
            self.p.op(eng, lambda e: e.copy(out=out, in_=in_), R, W, A)
        else:
            self.p.op(eng, lambda e: e.tensor_copy(out=out, in_=in_), R, W, A)

    def mm(self, out, lhsT, rhs, start, stop, R, W=(), A=()):
        self.p.op("tensor", lambda e: e.matmul(out, lhsT=lhsT, rhs=rhs, start=start, stop=stop), R, W, A)

    def mmacc(self, psi, out, pairs, R):
        n = len(pairs)
        for i, (l, r) in enumerate(pairs):
            if i == 0:
                self.mm(out, l, r, True, n == 1, R, W=[self.PB[psi]])
            else:
                self.mm(out, l, r, False, i == n - 1, R, A=[self.PB[psi]])

    def memset(self, eng, ap, val, W):
        self.p.op(eng, lambda e: e.memset(ap, val), (), W)

    def ld(self, out, in_, R, W=(), A=()):
        self.p.dma("sync", out, in_, R, W, A)

    def st(self, out, in_, R, W=(), A=()):
        self.p.dma(self.store_q, out, in_, R, W, A)

    def build(self):
        nc = self.nc
        self.store_q = "gpsimd"
        self.dbufs = {}
        with contextlib.ExitStack() as es:
            self.f_size = 51 * 1024 + 512
            self.AF = es.enter_context(nc.sbuf_tensor("arena_f", [128, self.f_size], F32))[:]
            self.PSD = [es.enter_context(nc.psum_tensor("psd%d" % i, [128, 1024], F32))[:] for i in range(2)]
            self.PS = [self.PSD[0][:, 0:512], self.PSD[0][:, 512:1024], self.PSD[1][:, 0:512], self.PSD[1][:, 512:1024]]
            self.PS += [es.enter_context(nc.psum_tensor("ps%d" % i, [128, 512], F32))[:] for i in range(4, 8)]
            self.PB = bufs(8, "ps")
            self.f_off = 0
            self.consts()
            self.f_base = self.f_off
            self.marks = []
            for stg in self.stages:
                self.reset_arena()
                getattr(self, "st_" + stg[0])(*stg[1:])
                self.p.barrier()
                self.marks.append((stg, sum(1 for o in self.p.streams["tensor"] if o.fn is not None)))
            self.p.emit()
        return nc

    def consts(self):
        self.ident = self.af(128, 128)
        self.Bc = Buf("consts")
        idn = self.din("ident", [128, 128])
        self.ld(self.ident, idn, (), [self.Bc])
        self.ones_b = self.ab(128, 128)
        self.memset("vector", self.ones_b, 1.0, [Buf()])
        self.ident_b = self.ab(128, 128)
        self.cp("vector", self.ident_b, self.ident, [self.Bc], A=[self.Bc])
        self.ones_f = self.af(128, 128)
        self.memset("vector", self.ones_f, 1.0, [Buf()])
        self.eps = self.af(128, 2)
        self.memset("vector", self.eps[:, 0:1], LN_EPS, [Buf()])
        self.memset("vector", self.eps[:, 1:2], RMS_EPS, [Buf()])
        self.p.barrier()
        self.vec = {}
        self.vecB = Buf("vec")
        self.layers_mod = set()
        self.layer_vec = set()

    def load_cols(self, rows_ap, R, dst, stgq):
        stg, stgB = stgq
        self.ld(stg[0:R, :], rows_ap, (), [stgB])
        self.mm(self.PS[7][:, 0:R], stg[0:R, :], self.ident[0:R, 0:R], True, True, [stgB, self.Bc], W=[self.PB[7]])
        self.cp("vector", dst, self.PS[7][:, 0:R], [self.PB[7]], A=[self.vecB])

    def need_layer(self, l):
        if l in self.layer_vec:
            return
        self.layer_vec.add(l)
        self.f_off = self.f_base
        V = {}
        lng = self.af(128, 24)
        lnb = self.af(128, 24)
        modt = self.af(128, 72, 2)
        sc1p = self.af(128, 48)
        sh = self.af(128, 48)
        gw = self.af(128, 48)
        V.update(lng=lng, lnb=lnb, sc1p=sc1p, sh=sh, gw=gw)
        if l % 2 == 0:
            V["qng"] = self.af(128, 3)
            V["kvng"] = self.af(128, 2)
            V["sg"] = self.af(128, 512)
            V["sb"] = self.af(128, 512)
            V["bs"] = self.af(128, 512)
            V["wst"] = self.ab(128, 512)
        else:
            V["bpw1"] = self.af(128, 16)
            V["wdw"] = self.af(128, 31 * 8)
            V["bdw"] = self.af(128, 8)
            V["olng"] = self.af(128, 8)
            V["olnb"] = self.af(128, 8)
            V["bout"] = self.af(128, 8)
            V["bg"] = self.af(128, 16)
        self.f_base = self.f_off
        self.vec[l] = V
        stg = self.af(128, 128)
        stgB = Buf("stg")
        sq = (stg, stgB)
        lnga = self.din("ln_g%d" % l, [24, 128])
        lnba = self.din("ln_b%d" % l, [24, 128])
        self.load_cols(lnga, 24, lng, sq)
        self.load_cols(lnba, 24, lnb, sq)
        if l % 2 == 0:
            self.load_cols(self.din("qn%d" % l, [3, 128]), 3, V["qng"], sq)
            self.load_cols(self.din("kvn%d" % l, [2, 128]), 2, V["kvng"], sq)
            sgr = self.din("sgu_g%d" % l, [1, 512])
            sbr = self.din("sgu_b%d" % l, [1, 512])
            bsr = self.din("b_s%d" % l, [1, 512])
            for nm, src in (("sg", sgr), ("sb", sbr), ("bs", bsr)):
                self.ld(V[nm], bass.AP(src.tensor, 0, [[0, 128], [1, 512]]), (), A=[self.vecB])
            wsr = self.din("w_s%d" % l, [4, 128, 128])
            wtmp = self.af(128, 4, 128)
            wB = Buf()
            self.ld(wtmp, wsr.rearrange("g i j -> i g j"), (), [wB])
            for g in range(4):
                self.mm(self.PS[6][:, g * 128:(g + 1) * 128], wtmp[:, g, :], self.ident, True, True, [wB, self.Bc],
                        **({"W": [self.PB[6]]} if g == 0 else {"A": [self.PB[6]]}))
            self.cp("vector", V["wst"], self.PS[6], [self.PB[6]], A=[self.vecB])
        else:
            self.load_cols(self.din("b_pw1%d" % l, [16, 128]), 16, V["bpw1"], sq)
            wd = self.din("w_dw%d" % l, [248, 128])
            self.load_cols(wd[0:128, :], 128, V["wdw"][:, 0:128], sq)
            self.load_cols(wd[128:248, :], 120, V["wdw"][:, 128:248], sq)
            self.load_cols(self.din("b_dw%d" % l, [8, 128]), 8, V["bdw"], sq)
            self.load_cols(self.din("o_ln_g%d" % l, [8, 128]), 8, V["olng"], sq)
            self.load_cols(self.din("o_ln_b%d" % l, [8, 128]), 8, V["olnb"], sq)
            self.load_cols(self.din("b_out%d" % l, [8, 128]), 8, V["bout"], sq)
        cc = self.din("cc", [16, 128])
        scT = self.af(128, 16)
        scB = Buf()
        self.ld(stg[0:16, :], cc, (), [stgB])
        self.mm(self.PS[7][:, 0:16], stg[0:16, :], self.ident[0:16, 0:16], True, True, [stgB, self.Bc], W=[self.PB[7]])
        self.act(scT, self.PS[7][:, 0:16], AF.Silu, [self.PB[7]], W=[scB])
        scK = self.af(128, 8, 2)
        self.cp("vector", scK, scT.rearrange("p (w k) -> p k w", w=2), [scB], W=[scB])
        wm = self.din("w_mod%d" % l, [1024, 9216]).rearrange("(k p) n -> p k n", p=128)
        bmr = self.din("b_mod%d" % l, [1, 9216])
        bmt = [self.af(128, 512) for _ in range(2)]
        bmB = bufs(2)
        wbuf = [self.af(128, 4, 512) for _ in range(4)]
        wbB = bufs(4)
        piece = self.af(128, 512)
        pB = Buf()
        mB = Buf()
        for n in range(18):
            self.ld(bmt[n % 2][0:1, :], bmr[:, n * 512:(n + 1) * 512], (), [bmB[n % 2]])
            for hk in range(2):
                wi = (n * 2 + hk) % 4
                self.ld(wbuf[wi], wm[:, hk * 4:(hk + 1) * 4, n * 512:(n + 1) * 512], (), [wbB[wi]])
                for k4 in range(4):
                    k = hk * 4 + k4
                    self.mm(self.PS[5][0:2, :], scK[:, k, :], wbuf[wi][:, k4, :], k == 0, False, [scB, wbB[wi]],
                            **({"W": [self.PB[5]]} if k == 0 else {"A": [self.PB[5]]}))
            self.mm(self.PS[5][0:2, :], self.ones_f[0:1, 0:2], bmt[n % 2][0:1, :], False, True, [bmB[n % 2]], A=[self.PB[5]])
            self.cp("vector", piece[0:2, :], self.PS[5][0:2, :], [self.PB[5]], W=[pB])
            for q in range(4):
                self.mm(self.PS[6][:, 2 * q:2 * q + 2], piece[0:2, q * 128:(q + 1) * 128], self.ident[0:2, 0:2], True, True,
                        [pB, self.Bc], **({"W": [self.PB[6]]} if q == 0 else {"A": [self.PB[6]]}))
            self.cp("vector", modt[:, n * 4:(n + 1) * 4, :], self.PS[6][:, 0:8].rearrange("p (q w) -> p q w", w=2), [self.PB[6]], A=[mB])
        for s in range(3):
            wt = 1.0 if s == 1 else 0.5
            for w in range(2):
                o = (s * 2 + w) * 8
                self.ts1("vector", sc1p[:, o:o + 8], modt[:, (3 * s + 1) * 8:(3 * s + 1) * 8 + 8, w], 1.0, ALU.add, [mB], A=[self.vecB])
                self.cp("vector", sh[:, o:o + 8], modt[:, (3 * s) * 8:(3 * s) * 8 + 8, w], [mB], A=[self.vecB])
                self.ts1("vector", gw[:, o:o + 8], modt[:, (3 * s + 2) * 8:(3 * s + 2) * 8 + 8, w], wt, ALU.mult, [mB], A=[self.vecB])
        self.p.barrier()
        if l % 2 == 1:
            for w in range(2):
                o = (1 * 2 + w) * 8
                self.tt("vector", V["bg"][:, w * 8:w * 8 + 8], V["bout"], gw[:, o:o + 8], ALU.mult, [self.vecB], A=[self.vecB])
        self.p.barrier()
        self.f_off = self.f_base

    def coef(self, l, name, s, w, c):
        o = (s * 2 + w) * 8 + c
        return self.vec[l][name][:, o:o + 1]

    def cvt_setup(self, nb=2):
        self.cv_f = [self.af(128, 2048) for _ in range(nb)]
        self.cv_b = [self.ab(128, 2048) for _ in range(nb)]
        self.cv_fB = bufs(nb)
        self.cv_bB = bufs(nb)
        self.cv_i = 0
        self.cv_n = nb

    def cvt_emit(self, piece, ldq="sync", stq=None, engs=("vector", "gpsimd"), phase=None):
        if phase == 1:
            piece, i, eng = piece
        else:
            i = self.cv_i % self.cv_n
            eng = engs[self.cv_i % len(engs)]
            self.cv_i += 1
        loads, casts, stores, wB = piece
        stq = stq or self.store_q
        f, b, fB, bB = self.cv_f[i], self.cv_b[i], self.cv_fB[i], self.cv_bB[i]
        if phase != 1:
            for k, (dfn, src) in enumerate(loads):
                self.p.dma(ldq, dfn(f), src, (), **({"writes": [fB]} if k == 0 else {"adds": [fB]}))
        if phase == 0:
            return (piece, i, eng)
        for k, (ofn, ifn) in enumerate(casts):
            self.cp(eng, ofn(b), ifn(f), [fB], **({"W": [bB]} if k == 0 else {"A": [bB]}))
        for (dst, sfn) in stores:
            self.p.dma(stq, dst, sfn(b), [bB], (), [wB])

    def nat_pieces(self, src, K, M, dst, wB):
        out = []
        for k in range(K):
            for c0 in range(0, M, 2048):
                w = min(2048, M - c0)
                out.append(([(lambda f, w=w: f[:, 0:w], src[k * 128:(k + 1) * 128, c0:c0 + w])],
                            [(lambda b, w=w: b[:, 0:w], lambda f, w=w: f[:, 0:w])],
                            [(dst[:, k * M + c0:k * M + c0 + w], lambda b, w=w: b[:, 0:w])], wB))
        return out

    def w_pieces(self, key):
        wB = self.sbuf(("w", key))
        P = []
        kind = key[0]
        if kind == "ffn":
            _, l, i = key
            w13 = self.din("w13_%d_%d" % (l, i), [1024, 2 * DFF]).rearrange("(k p) n -> p k n", p=128)
            w2 = self.din("w2_%d_%d" % (l, i), [DFF, 1024])
            d13 = self.dtmp("w13s_%d_%d" % (l, i), [NJ, 128, 2048], BF16)
            d2 = self.dtmp("w2s_%d_%d" % (l, i), [128, NJ * 1024], BF16)
            for j in range(NJ):
                P.append((
                    [(lambda f: f.rearrange("p (k t c) -> p k t c", k=8, t=2)[:, :, 0, :], w13[:, :, j * 128:(j + 1) * 128]),
                     (lambda f: f.rearrange("p (k t c) -> p k t c", k=8, t=2)[:, :, 1, :], w13[:, :, DFF + j * 128:DFF + (j + 1) * 128])],
                    [(lambda b: b, lambda f: f)],
                    [(d13[j], lambda b: b)], wB))
            P += self.nat_pieces(w2, NJ, 1024, d2, wB)
        elif kind == "even":
            _, l = key
            win = self.din("w_in%d" % l, [1024, 1696])
            dwin = self.dtmp("wins%d" % l, [128, 8 * WIN_COLS], BF16)
            for k in range(8):
                P.append((
                    [(lambda f: f[:, 0:1696], win[k * 128:(k + 1) * 128, :])],
                    [(lambda b: b[:, 0:672], lambda f: f[:, 0:672]),
                     (lambda b: b[:, 672:704].rearrange("p (i t) -> p i t", t=2)[:, :, 0], lambda f: f[:, 640:672].rearrange("p (i t) -> p i t", t=2)[:, :, 1]),
                     (lambda b: b[:, 672:704].rearrange("p (i t) -> p i t", t=2)[:, :, 1], lambda f: f[:, 640:672].rearrange("p (i t) -> p i t", t=2)[:, :, 0]),
                     (lambda b: b[:, 704:1728], lambda f: f[:, 672:1696])],
                    [(dwin[:, k * WIN_COLS:(k + 1) * WIN_COLS], lambda b: b[:, 0:WIN_COLS])], wB))
            wq = self.din("w_q%d" % l, [384, 768])
            dwq = self.dtmp("wqs%d" % l, [128, 3 * 1024], BF16)
            fv = lambda f: f[:, 0:768].rearrange("p (h d) -> p h d", h=8)
            for k in range(3):
                P.append((
                    [(lambda f: f[:, 0:768], wq[k * 128:(k + 1) * 128, :])],
                    [(lambda b: b[:, 0:512].rearrange("p (h d) -> p h d", h=8), lambda f: fv(f)[:, :, 0:64]),
                     (lambda b: b[:, 512:768].rearrange("p (h d) -> p h d", h=8), lambda f: fv(f)[:, :, 64:96]),
                     (lambda b: b[:, 768:1024].rearrange("p (h i t) -> p h i t", h=8, t=2)[:, :, :, 0],
                      lambda f: fv(f)[:, :, 64:96].rearrange("p h (i t) -> p h i t", t=2)[:, :, :, 1]),
                     (lambda b: b[:, 768:1024].rearrange("p (h i t) -> p h i t", h=8, t=2)[:, :, :, 1],
                      lambda f: fv(f)[:, :, 64:96].rearrange("p h (i t) -> p h i t", t=2)[:, :, :, 0])],
                    [(dwq[:, k * 1024:(k + 1) * 1024], lambda b: b[:, 0:1024])], wB))
            wkv = self.din("w_kv%d" % l, [256, 1024])
            dwkv = self.dtmp("wkvs%d" % l, [128, 2 * 1024], BF16)
            fv2 = lambda f: f[:, 0:1024].rearrange("p (h d) -> p h d", h=8)
            for k in range(2):
                P.append((
                    [(lambda f: f[:, 0:1024], wkv[k * 128:(k + 1) * 128, :])],
                    [(lambda b: b[:, 0:512].rearrange("p (h d) -> p h d", h=8), lambda f: fv2(f)[:, :, 0:64]),
                     (lambda b: b[:, 512:1024].rearrange("p (h d) -> p h d", h=8), lambda f: fv2(f)[:, :, 64:128])],
                    [(dwkv[:, k * 1024:(k + 1) * 1024], lambda b: b[:, 0:1024])], wB))
            wo = self.din("w_out%d" % l, [1024, 1024])
            P += self.nat_pieces(wo, 8, 1024, self.dtmp("wos%d" % l, [128, 8 * 1024], BF16), wB)
        elif kind == "odd":
            _, l = key
            P += self.nat_pieces(self.din("w_pw1%d" % l, [1024, 2048]), 8, 2048, self.dtmp("wp1s%d" % l, [128, 8 * 2048], BF16), wB)
            P += self.nat_pieces(self.din("w_out%d" % l, [1024, 1024]), 8, 1024, self.dtmp("wos%d" % l, [128, 8 * 1024], BF16), wB)
        return P

    def need_w(self, key):
        if key in self.wdone:
            return
        self.wdone.add(key)
        mark = self.f_off
        self.cvt_setup(3)
        for piece in self.w_pieces(key):
            self.cvt_emit(piece)
        self.p.barrier()
        self.f_off = mark

    def res_ap(self, write):
        return self.state("RES", [1024, self.N], F32, write).rearrange("(c p) n -> p c n", p=128)

    def res_src(self, col0):
        if col0 in self.res_written:
            return self.res_ap(True)
        return self.din("RES_i", [1024, self.N], F32).rearrange("(c p) n -> p c n", p=128)

    def res_buf(self, col0):
        return self.sbuf(("RES", col0))

    def load_r(self, r, rB, grp):
        col0, T, w = grp
        src = self.res_src(col0)
        self.ld(r[:, :, :T], src[:, :, col0:col0 + T], [self.res_buf(col0)], W=rB)

    def store_r(self, r, rB, grp):
        col0, T, w = grp
        dst = self.res_ap(True)
        self.res_written.add(col0)
        self.st(dst[:, :, col0:col0 + T], r[:, :, :T], rB, W=[self.res_buf(col0)])

    def modulate(self, l, s, grp, r, rB, h, hB):
        col0, T, w = grp
        for c in range(8):
            eng = "vector" if c % 2 == 0 else "gpsimd"
            self.ts(eng, h[:, c, :T], r[:, c, :T], self.coef(l, "sc1p", s, w, c), self.coef(l, "sh", s, w, c), ALU.mult, ALU.add,
                    [rB[c], self.vecB], W=[hB[c]])

    def stats(self, srcs, T, F, eps_col, want_mean, tmp):
        C = len(srcs)
        xb, sq = tmp["xb"], tmp["sq"]
        xbB, sqB = tmp["xbB"], tmp["sqB"]
        for c, (x, xB) in enumerate(srcs):
            if want_mean:
                self.cp("gpsimd" if c % 2 else "vector", xb[:, c, :T], x, [xB], W=[xbB[c]])
            self.act(sq[:, c, :T], x, AF.Square, [xB], W=[sqB[c]])
        pm, pq = tmp["pm"], tmp["pq"]
        if want_mean:
            self.mmacc(pm, self.PS[pm][:, :T], [(self.ones_b, xb[:, c, :T]) for c in range(C)], list(xbB[:C]))
        self.mmacc(pq, self.PS[pq][:, :T], [(self.ones_b, sq[:, c, :T]) for c in range(C)], list(sqB[:C]))
        sB = tmp["sB"]
        rstd, nmr, mean, m2 = tmp["rstd"], tmp["nmr"], tmp["mean"], tmp["m2"]
        if want_mean:
            self.act(mean[:, :T], self.PS[pm][:, :T], AF.Copy, [self.PB[pm]], W=[sB], scale=1.0 / F)
            self.tt("vector", m2[:, :T], mean[:, :T], mean[:, :T], ALU.mult, [sB], W=[tmp["m2B"]])
            self.stt("vector", m2[:, :T], self.PS[pq][:, :T], 1.0 / F, m2[:, :T], ALU.mult, ALU.subtract, [self.PB[pq], tmp["m2B"]], W=[tmp["m2B"]])
            self.act(rstd[:, :T], m2[:, :T], AF.Sqrt, [tmp["m2B"]], W=[tmp["rsB"]], bias=self.eps[:, eps_col:eps_col + 1])
        else:
            self.act(rstd[:, :T], self.PS[pq][:, :T], AF.Sqrt, [self.PB[pq]], W=[tmp["rsB"]], bias=self.eps[:, eps_col:eps_col + 1], scale=1.0 / F)
        self.p.op("vector", lambda e: e.reciprocal(out=rstd[:, :T], in_=rstd[:, :T]), [tmp["rsB"]], [tmp["rsB"]])
        if want_mean:
            self.stt("vector", nmr[:, :T], mean[:, :T], -1.0, rstd[:, :T], ALU.mult, ALU.mult, [sB, tmp["rsB"]], W=[tmp["nmB"]])

    def stats_tmp(self, C, pm, pq, xb=None, sq=None):
        t = dict(xb=xb[0] if xb else self.ab(128, C, 512), sq=sq[0] if sq else self.ab(128, C, 512),
                 xbB=xb[1] if xb else bufs(C), sqB=sq[1] if sq else bufs(C), pm=pm, pq=pq, sB=Buf(), m2B=Buf(), rsB=Buf(), nmB=Buf(),
                 rstd=self.af(128, 512), nmr=self.af(128, 512), mean=self.af(128, 512), m2=self.af(128, 512),
                 t1=[self.af(128, 512) for _ in range(2)], t1B=bufs(2))
        return t

    def ln_apply(self, x, xB, C, T, gcol, bcol, out, outB, tmp, func=AF.Identity):
        for c in range(C):
            t1, t1B = tmp["t1"][c % 2], tmp["t1B"][c % 2]
            self.tt("vector", t1[:, :T], x[:, c, :T], tmp["rstd"][:, :T], ALU.mult, [xB[c], tmp["rsB"]], W=[t1B])
            self.tt("gpsimd", t1[:, :T], t1[:, :T], tmp["nmr"][:, :T], ALU.add, [t1B, tmp["nmB"]], W=[t1B])
            self.act(out[:, c, :T], t1[:, :T], func, [t1B, self.vecB], W=[outB[c]], bias=bcol(c), scale=gcol(c))

    def resid_ln_store(self, l, s, grp, r, rB, tmp):
        col0, T, w = grp
        V = self.vec[l]
        self.stats([(r[:, c, :T], rB[c]) for c in range(8)], T, 1024.0, 0, True, tmp)
        self.ln_apply(r, rB, 8, T, lambda c: V["lng"][:, s * 8 + c:s * 8 + c + 1], lambda c: V["lnb"][:, s * 8 + c:s * 8 + c + 1], r, rB, tmp)
        self.store_r(r, rB, grp)

    def epilogue(self, psi, m, T, r, rB, gwcol, ytmp, bias=None):
        y, yB = ytmp
        self.act(y[:, :T], self.PS[psi][:, :T], AF.Identity if bias is not None else AF.Copy, [self.PB[psi], self.vecB], W=[yB], scale=gwcol,
                 **({"bias": bias} if bias is not None else {}))
        self.stt("vector", r[:, m, :T], r[:, m, :T], ALPHA, y[:, :T], ALU.mult, ALU.add, [rB[m], yB], W=[rB[m]])

    def st_PRO(self):
        x = self.din("x", [self.nlat, 1024])
        ctx = self.din("ctx", [NCTX, 1024])
        res = self.res_ap(True)
        tin = [self.af(128, 4, 1024) for _ in range(2)]
        tinB = bufs(2)
        tout = [self.af(128, 8, 512) for _ in range(2)]
        toutB = bufs(2)
        pc = 0
        for gi, grp in enumerate(list(self.groups_lat) + [self.group_ctx]):
            col0, T, w = grp
            nt = T // 128
            b = gi % 2
            src = x[col0:col0 + T, :] if w == 0 else ctx
            self.ld(tin[b][:, 0:nt, :], src.rearrange("(t p) d -> p t d", p=128), (), W=[tinB[b]])
            first = True
            for t in range(nt):
                for hh in range(2):
                    psi = pc % 8
                    pc += 1
                    for q in range(4):
                        c = hh * 4 + q
                        self.mm(self.PS[psi][:, q * 128:(q + 1) * 128], tin[b][:, t, c * 128:(c + 1) * 128], self.ident, True, True, [tinB[b], self.Bc],
                                **({"W": [self.PB[psi]]} if q == 0 else {"A": [self.PB[psi]]}))
                    self.cp("vector" if hh == 0 else "scalar", tout[b][:, hh * 4:(hh + 1) * 4, t * 128:(t + 1) * 128],
                            self.PS[psi].rearrange("p (q t) -> p q t", q=4), [self.PB[psi]], **({"W": [toutB[b]]} if first else {"A": [toutB[b]]}))
                    first = False
            self.res_written.add(col0)
            self.st(res[:, :, col0:col0 + T], tout[b][:, :, :T], [toutB[b]], W=[self.res_buf(col0)])

    def st_EPI(self):
        out = self.dout("out", [self.nlat, 1024])
        tin = [self.af(128, 8, 512) for _ in range(2)]
        tinB = bufs(2)
        tout = [self.af(128, 1024) for _ in range(4)]
        toutB = bufs(4)
        self.outB = Buf("out")
        pc = 0
        tc = 0
        for gi, grp in enumerate(self.groups_lat):
            col0, T, w = grp
            b = gi % 2
            self.ld(tin[b], self.res_src(col0)[:, :, col0:col0 + T], [self.res_buf(col0)], W=[tinB[b]])
            for t in range(T // 128):
                ob = tc % 4
                tc += 1
                for hh in range(2):
                    psi = pc % 8
                    pc += 1
                    for q in range(4):
                        c = hh * 4 + q
                        self.mm(self.PS[psi][:, q * 128:(q + 1) * 128], tin[b][:, c, t * 128:(t + 1) * 128], self.ident, True, True, [tinB[b], self.Bc],
                                **({"W": [self.PB[psi]]} if q == 0 else {"A": [self.PB[psi]]}))
                    self.cp("vector" if hh == 0 else "scalar", tout[ob][:, hh * 512:(hh + 1) * 512], self.PS[psi], [self.PB[psi]],
                            **({"W": [toutB[ob]]} if hh == 0 else {"A": [toutB[ob]]}))
                self.st(out[col0 + t * 128:col0 + (t + 1) * 128, :], tout[ob], [toutB[ob]], A=[self.outB])

    def st_F(self, l, s, with_ctx):
        i = 0 if s == 0 else 1
        self.need_layer(l)
        self.need_w(("ffn", l, i))
        V = self.vec[l]
        d13 = self.dram["w13s_%d_%d" % (l, i)]
        d2 = self.dram["w2s_%d_%d" % (l, i)]
        wB = self.sbuf(("w", ("ffn", l, i)))
        W2 = self.ab(128, NJ, 1024)
        W2B = Buf()

        def load_W2():
            for q in range(2):
                self.ld(W2[:, q * 11:(q + 1) * 11, :], d2[:, q * 11 * 1024:(q + 1) * 11 * 1024].rearrange("p (j m) -> p j m", j=11), [wB],
                        **({"W": [W2B]} if q == 0 else {"A": [W2B]}))
        w13 = [self.ab(128, 8, 256) for _ in range(2)]
        w13B = bufs(2)
        S = 2
        r = [self.af(128, 8, 512) for _ in range(3)]
        rB = [bufs(8) for _ in range(3)]
        h = [self.ab(128, 8, 512) for _ in range(S)]
        hB = [bufs(8) for _ in range(S)]
        actt = [self.ab(128, NJ, 512) for _ in range(S)]
        actB = [Buf() for _ in range(S)]
        sg = [self.af(128, 512) for _ in range(2)]
        sgB = bufs(2)
        ytmp = [(self.af(128, 512), Buf()) for _ in range(2)]
        tmp = self.stats_tmp(8, 0, 1, xb=(h[0], hB[0]), sq=(h[1], hB[1]))
        groups = list(self.groups_lat)
        passes = [groups[a:a + S] for a in range(0, len(groups), S)]
        if with_ctx:
            passes.append([self.group_ctx])
        ridx = {}
        for pi, ps_ in enumerate(passes):
            if pi == 0:
                for si in range(len(ps_)):
                    ridx[(pi, si)] = si
            else:
                used = [ridx[(pi - 1, si)] for si in range(len(passes[pi - 1]))]
                free = [x for x in range(3) if x not in used]
                ridx[(pi, 0)] = free[0]
                if len(ps_) > 1:
                    ridx[(pi, 1)] = ridx[(pi - 1, 0)]

        def ln_stats(grp, ri):
            col0, T, w = grp
            self.stats([(r[ri][:, c, :T], rB[ri][c]) for c in range(8)], T, 1024.0, 0, True, tmp)

        def ln_apply_store(grp, ri):
            col0, T, w = grp
            self.ln_apply(r[ri], rB[ri], 8, T, lambda c: V["lng"][:, s * 8 + c:s * 8 + c + 1], lambda c: V["lnb"][:, s * 8 + c:s * 8 + c + 1],
                          r[ri], rB[ri], tmp)
            self.store_r(r[ri], rB[ri], grp)

        for si, grp in enumerate(passes[0]):
            self.load_r(r[ridx[(0, si)]], rB[ridx[(0, si)]], grp)
            self.modulate(l, s, grp, r[ridx[(0, si)]], rB[ridx[(0, si)]], h[si], hB[si])
        cnt = 0
        for pi, ps_ in enumerate(passes):
            nxt = passes[pi + 1] if pi + 1 < len(passes) else []
            for j in range(NJ):
                wb = j % 2
                self.ld(w13[wb], d13[j].rearrange("p (k c) -> p k c", k=8), [wB], W=[w13B[wb]])
                if pi == 0 and j == 1:
                    load_W2()
                for si, grp in enumerate(ps_):
                    col0, T, w = grp
                    pg = (cnt % 3) * 2
                    pu = pg + 1
                    cnt += 1
                    self.mmacc(pg, self.PS[pg][:, :T], [(w13[wb][:, k, 0:128], h[si][:, k, :T]) for k in range(8)], [w13B[wb]] + hB[si])
                    self.mmacc(pu, self.PS[pu][:, :T], [(w13[wb][:, k, 128:256], h[si][:, k, :T]) for k in range(8)], [w13B[wb]] + hB[si])
                    sgi = cnt % 2
                    self.act(sg[sgi][:, :T], self.PS[pg][:, :T], AF.Silu, [self.PB[pg]], W=[sgB[sgi]])
                    self.tt("vector", actt[si][:, j, :T], self.PS[pu][:, :T], sg[sgi][:, :T], ALU.mult, [self.PB[pu], sgB[sgi]],
                            **({"W": [actB[si]]} if j == 0 else {"A": [actB[si]]}))
            for si, grp in enumerate(ps_):
                col0, T, w = grp
                ri = ridx[(pi, si)]
                for m in range(8):
                    py = 6 + (m % 2)
                    self.mmacc(py, self.PS[py][:, :T], [(W2[:, j, m * 128:(m + 1) * 128], actt[si][:, j, :T]) for j in range(NJ)], [W2B, actB[si]])
                    self.epilogue(py, m, T, r[ri], rB[ri], self.coef(l, "gw", s, w, m), ytmp[m % 2])
                    if si > 0 and m == 3:
                        pr = ridx[(pi, si - 1)]
                        ln_stats(ps_[si - 1], pr)
                        ln_apply_store(ps_[si - 1], pr)
                        if len(nxt) > 1:
                            rn = ridx[(pi + 1, 1)]
                            self.load_r(r[rn], rB[rn], nxt[1])
                if si == 0 and nxt:
                    rn = ridx[(pi + 1, 0)]
                    self.load_r(r[rn], rB[rn], nxt[0])
            last = len(ps_) - 1
            rl = ridx[(pi, last)]
            ln_stats(ps_[last], rl)
            for si, grp in enumerate(nxt):
                rn = ridx[(pi + 1, si)]
                if len(ps_) == 1 and si == 1:
                    self.load_r(r[rn], rB[rn], grp)
                self.modulate(l, s, grp, r[rn], rB[rn], h[si], hB[si])
            ln_apply_store(ps_[last], rl)

    def st_E2(self, l, with_ctx, ctx_out):
        self.need_layer(l)
        self.need_w(("even", l))
        V = self.vec[l]
        wB = self.sbuf(("w", ("even", l)))
        N = self.N
        WIN = self.ab(128, 8, WIN_COLS)
        WQ = self.ab(128, 3, 1024)
        WB_ = Buf()
        self.ld(WIN, self.dram["wins%d" % l].rearrange("p (k c) -> p k c", k=8), [wB], W=[WB_])
        self.ld(WQ, self.dram["wqs%d" % l].rearrange("p (k c) -> p k c", k=3), [wB], A=[WB_])
        ropec = self.din("ropec", [128, self.nlat])
        ropes = self.din("ropes", [128, self.nlat])
        QTN = self.state("QTN", [512, N], BF16, True).rearrange("(c p) n -> p c n", p=128)
        QTR = self.state("QTR", [256, N], BF16, True).rearrange("(c p) n -> p c n", p=128)
        BL = self.state("BL", [512, N], BF16, True).rearrange("(c p) n -> p c n", p=128)
        CKV = self.state("CKV", [256, N], BF16, True).rearrange("(c p) n -> p c n", p=128)
        KR = self.state("KR", [32, N], BF16, True)
        r2 = [self.af(128, 8, 512) for _ in range(2)]
        rB2 = [bufs(8) for _ in range(2)]
        h2 = [self.ab(128, 8, 512) for _ in range(2)]
        hB2 = [bufs(8) for _ in range(2)]
        cosT = self.af(128, 512)
        sinT = self.af(128, 512)
        rpB = Buf()
        cq = self.af(128, 3, 512)
        cqB = bufs(3)
        cqn = self.ab(128, 3, 512)
        cqnB = bufs(3)
        ckv = self.af(128, 2, 512)
        ckvB = bufs(2)
        ckvn = self.ab(128, 2, 512)
        ckvnB = bufs(2)
        qn = self.ab(128, 4, 512)
        qnB = Buf()
        qr = self.ab(128, 2, 512)
        qrB = Buf()
        krt = self.ab(128, 512)
        krB = Buf()
        u = self.af(128, 4, 512)
        uB = bufs(4)
        vg = [self.af(128, 512) for _ in range(2)]
        vgB = bufs(2)
        vb = [self.ab(128, 512) for _ in range(2)]
        vbB = bufs(2)
        bnst2 = [self.af(128, 8) for _ in range(2)]
        bnB2 = bufs(2)
        mx2 = [self.af(128, 512) for _ in range(2)]
        mxB2 = bufs(2)
        bl = self.ab(128, 4, 512)
        blB = Buf()
        t1 = [self.af(128, 512) for _ in range(2)]
        t1B = bufs(2)
        tmp = self.stats_tmp(3, 0, 1)
        groups = list(self.groups_lat) + ([self.group_ctx] if with_ctx else [])
        pc = 0

        def nextps():
            nonlocal pc
            v = 2 + (pc % 6)
            pc += 1
            return v

        def x_after_group(gi):
            col0, T, w = groups[gi]
            cw = min(self.nlat, 1024)
            self.x_flush()
            if w == 1:
                self.x_even_ctx(l)
            elif (col0 + T) % cw == 0:
                self.x_even_chunk(l, (col0 + T) // cw - 1)
            if gi == len(groups) - 1:
                self.x_flush()
                self.x_done.add(l)

        self.load_r(r2[0], rB2[0], groups[0])
        self.modulate(l, 1, groups[0], r2[0], rB2[0], h2[0], hB2[0])
        for gi, grp in enumerate(groups):
            col0, T, w = grp
            full = (w == 0) or ctx_out
            h, hB = h2[gi % 2], hB2[gi % 2]
            if gi + 1 < len(groups):
                nb = (gi + 1) % 2
                self.load_r(r2[nb], rB2[nb], groups[gi + 1])
                self.modulate(l, 1, groups[gi + 1], r2[nb], rB2[nb], h2[nb], hB2[nb])
            if w == 0:
                self.ld(cosT[:, :T], ropec[:, col0:col0 + T], (), W=[rpB])
                self.ld(sinT[:, :T], ropes[:, col0:col0 + T], (), A=[rpB])
            hR = [WB_] + hB
            for c in range(2):
                psi = nextps()
                self.mmacc(psi, self.PS[psi][:, :T], [(WIN[:, k, 384 + c * 128:384 + (c + 1) * 128], h[:, k, :T]) for k in range(8)], hR)
                self.cp("scalar", ckv[:, c, :T], self.PS[psi][:, :T], [self.PB[psi]], W=[ckvB[c]])
            if full:
                for c in range(3):
                    psi = nextps()
                    self.mmacc(psi, self.PS[psi][:, :T], [(WIN[:, k, c * 128:(c + 1) * 128], h[:, k, :T]) for k in range(8)], hR)
                    self.cp("scalar", cq[:, c, :T], self.PS[psi][:, :T], [self.PB[psi]], W=[cqB[c]])
            pa = nextps()
            self.mmacc(pa, self.PS[pa][0:32, :T], [(WIN[:, k, 640:672], h[:, k, :T]) for k in range(8)], hR)
            if w == 0:
                pb = nextps()
                self.mmacc(pb, self.PS[pb][0:32, :T], [(WIN[:, k, 672:704], h[:, k, :T]) for k in range(8)], hR)
                self.tt("vector", t1[0][0:32, :T], self.PS[pa][0:32, :T], cosT[0:32, :T], ALU.mult, [self.PB[pa], rpB], W=[t1B[0]])
                self.tt("vector", t1[1][0:32, :T], self.PS[pb][0:32, :T], sinT[0:32, :T], ALU.mult, [self.PB[pb], rpB], W=[t1B[1]])
                self.tt("gpsimd", krt[0:32, :T], t1[0][0:32, :T], t1[1][0:32, :T], ALU.add, [t1B[0], t1B[1]], W=[krB])
            else:
                self.cp("scalar", krt[0:32, :T], self.PS[pa][0:32, :T], [self.PB[pa]], W=[krB])
            self.st(KR[:, col0:col0 + T], krt[0:32, :T], [krB], W=[self.sbuf(("KR", col0))])
            if full:
                for c in range(4):
                    psi = nextps()
                    self.mmacc(psi, self.PS[psi][:, :T], [(WIN[:, k, 704 + c * 128:704 + (c + 1) * 128], h[:, k, :T]) for k in range(8)], hR)
                    self.act(u[:, c, :T], self.PS[psi][:, :T], AF.Gelu, [self.PB[psi]], W=[uB[c]])
            self.stats([(ckv[:, c, :T], ckvB[c]) for c in range(2)], T, 256.0, 1, False, tmp)
            for c in range(2):
                self.tt("vector", t1[c][:, :T], ckv[:, c, :T], tmp["rstd"][:, :T], ALU.mult, [ckvB[c], tmp["rsB"]], W=[t1B[c]])
                self.act(ckvn[:, c, :T], t1[c][:, :T], AF.Copy, [t1B[c], self.vecB], W=[ckvnB[c]], scale=V["kvng"][:, c:c + 1])
            self.st(CKV[:, :, col0:col0 + T], ckvn[:, :, :T], ckvnB, W=[self.sbuf(("CKV", col0))])
            if not full:
                if self.fused and X_OVERLAP:
                    x_after_group(gi)
                continue
            self.stats([(cq[:, c, :T], cqB[c]) for c in range(3)], T, 384.0, 1, False, tmp)
            for c in range(3):
                self.tt("vector", t1[c % 2][:, :T], cq[:, c, :T], tmp["rstd"][:, :T], ALU.mult, [cqB[c], tmp["rsB"]], W=[t1B[c % 2]])
                self.act(cqn[:, c, :T], t1[c % 2][:, :T], AF.Copy, [t1B[c % 2], self.vecB], W=[cqnB[c]], scale=V["qng"][:, c:c + 1])

            def q_part():
                qR = [WB_] + cqnB
                for c in range(4):
                    psi = nextps()
                    self.mmacc(psi, self.PS[psi][:, :T], [(WQ[:, k, c * 128:(c + 1) * 128], cqn[:, k, :T]) for k in range(3)], qR)
                    self.cp("scalar", qn[:, c, :T], self.PS[psi][:, :T], [self.PB[psi]], **({"W": [qnB]} if c == 0 else {"A": [qnB]}))
                self.st(QTN[:, :, col0:col0 + T], qn[:, :, :T], [qnB], W=[self.sbuf(("QTN", col0))])
                for c in range(2):
                    pa = nextps()
                    self.mmacc(pa, self.PS[pa][:, :T], [(WQ[:, k, 512 + c * 128:512 + (c + 1) * 128], cqn[:, k, :T]) for k in range(3)], qR)
                    wa = {"W": [qrB]} if c == 0 else {"A": [qrB]}
                    if w == 0:
                        pb = nextps()
                        self.mmacc(pb, self.PS[pb][:, :T], [(WQ[:, k, 768 + c * 128:768 + (c + 1) * 128], cqn[:, k, :T]) for k in range(3)], qR)
                        self.tt("vector", t1[0][:, :T], self.PS[pa][:, :T], cosT[:, :T], ALU.mult, [self.PB[pa], rpB], W=[t1B[0]])
                        self.tt("vector", t1[1][:, :T], self.PS[pb][:, :T], sinT[:, :T], ALU.mult, [self.PB[pb], rpB], W=[t1B[1]])
                        self.tt("gpsimd", qr[:, c, :T], t1[0][:, :T], t1[1][:, :T], ALU.add, [t1B[0], t1B[1]], **wa)
                    else:
                        self.cp("scalar", qr[:, c, :T], self.PS[pa][:, :T], [self.PB[pa]], **wa)
                self.st(QTR[:, :, col0:col0 + T], qr[:, :, :T], [qrB], W=[self.sbuf(("QTR", col0))])

            nsub = T // 128
            for ci in range(nsub):
                if ci == nsub // 2:
                    q_part()
                tk = slice(ci * 128, (ci + 1) * 128)
                b2 = ci % 2
                bn, bnB_, mx_, mxB_ = bnst2[b2], bnB2[b2], mx2[b2], mxB2[b2]
                psi = nextps()
                self.mmacc(psi, self.PS[psi], [(h[:, k, tk], WIN[:, k, 1216:1728]) for k in range(8)], hR)
                self.act(vg[b2], self.PS[psi], AF.Gelu, [self.PB[psi]], W=[vgB[b2]])
                self.p.op("vector", lambda e, b2=b2, bn=bn: e.bn_stats(out=bn[:, 0:6], in_=vg[b2]), [vgB[b2]], [bnB_])
                self.p.op("vector", lambda e, bn=bn: e.bn_aggr(out=bn[:, 6:8], in_=bn[:, 0:6]), [bnB_], [bnB_])
                self.act(bn[:, 7:8], bn[:, 7:8], AF.Sqrt, [bnB_], W=[bnB_], bias=self.eps[:, 0:1])
                self.p.op("vector", lambda e, bn=bn: e.reciprocal(out=bn[:, 7:8], in_=bn[:, 7:8]), [bnB_], [bnB_])
                self.ts("vector", vg[b2], vg[b2], bn[:, 6:7], bn[:, 7:8], ALU.subtract, ALU.mult, [vgB[b2], bnB_], W=[vgB[b2]])
                self.tt("gpsimd", vg[b2], vg[b2], V["sg"], ALU.mult, [vgB[b2], self.vecB], W=[vgB[b2]])
                self.tt("vector", vb[b2], vg[b2], V["sb"], ALU.add, [vgB[b2], self.vecB], W=[vbB[b2]])
                psm = nextps()
                for g in range(4):
                    self.mm(self.PS[psm][:, g * 128:(g + 1) * 128], vb[b2][:, g * 128:(g + 1) * 128], V["wst"][:, g * 128:(g + 1) * 128], True, True,
                            [vbB[b2], self.vecB], **({"W": [self.PB[psm]]} if g == 0 else {"A": [self.PB[psm]]}))
                self.tt("vector", mx_, self.PS[psm], V["bs"], ALU.add, [self.PB[psm], self.vecB], W=[mxB_])
                self.tt("gpsimd", bl[:, :, tk], u[:, :, tk], mx_.rearrange("p (g i) -> p g i", g=4), ALU.mult, uB + [mxB_],
                        **({"W": [blB]} if ci == 0 else {"A": [blB]}))
            self.st(BL[:, :, col0:col0 + T], bl[:, :, :T], [blB], W=[self.sbuf(("BL", col0))])
            if self.fused and X_OVERLAP:
                x_after_group(gi)

    def x_even_ctx(self, l):
        nlat, N, nk = self.nlat, self.N, self.nk
        CKV = self.state("CKV", [256, N], BF16, False)
        KR = self.state("KR", [32, N], BF16, False)
        CKVA = self.state("CKVA", [256, nk], BF16, True)
        KRA = self.state("KRA", [32, nk], BF16, True)
        Ba = self.sbuf(("KVA",))
        self.ld(CKVA[:, 0:NCTX], CKV[:, nlat:N], [self.sbuf(("CKV", nlat))], A=[Ba])
        self.ld(KRA[:, 0:NCTX], KR[:, nlat:N], [self.sbuf(("KR", nlat))], A=[Ba])

    def x_even_chunk(self, l, k):
        nlat, N, nk = self.nlat, self.N, self.nk
        pairs = [[2 * i, 2 * i + 1] for i in range(self.ncores // 2)]
        cw = min(nlat, 1024)
        CKV = self.state("CKV", [256, N], BF16, False)
        KR = self.state("KR", [32, N], BF16, False)
        CKVA = self.state("CKVA", [256, nk], BF16, True)
        KRA = self.state("KRA", [32, nk], BF16, True)
        Ba = self.sbuf(("KVA",))
        xin = self.dtmp("xin%d_%d" % (l, k), [288, cw], BF16)
        xout = self.dtmp("xout%d_%d" % (l, k), [576, cw], BF16)
        Bi, Bo = Buf(), Buf()
        srcs = [self.sbuf((nm, c0)) for nm in ("CKV", "KR") for c0 in range(k * cw, (k + 1) * cw, 512)]
        self.ld(xin[0:256, :], CKV[:, k * cw:(k + 1) * cw], srcs, W=[Bi])
        self.ld(xin[256:288, :], KR[:, k * cw:(k + 1) * cw], srcs, A=[Bi])
        self.p.coll(lambda e, xin=xin, xout=xout: e.collective_compute("AllGather", ALU.bypass, replica_groups=pairs, ins=[xin], outs=[xout]), [Bi], [Bo])

        def post():
            for r in range(2):
                c0 = NCTX + r * nlat + k * cw
                self.ld(CKVA[:, c0:c0 + cw], xout[r * 288:r * 288 + 256, :], [Bo], A=[Ba])
                self.ld(KRA[:, c0:c0 + cw], xout[r * 288 + 256:(r + 1) * 288, :], [Bo], A=[Ba])
        self.x_pending.append(post)

    def x_flush(self):
        while self.x_pending:
            self.x_pending.pop(0)()

    def st_X(self, l):
        nlat, N, nk = self.nlat, self.N, self.nk
        pairs = [[2 * i, 2 * i + 1] for i in range(self.ncores // 2)]
        if l in self.x_done:
            return
        if l % 2 == 0:
            self.x_even_ctx(l)
            for k in range(nlat // min(nlat, 1024)):
                self.x_even_chunk(l, k)
            self.x_flush()
        else:
            self.x_odd_pre(l)
            self.x_flush()

    def x_odd_pre(self, l):
        nlat, N, nk = self.nlat, self.N, self.nk
        pairs = [[2 * i, 2 * i + 1] for i in range(self.ncores // 2)]
        GLH = self.state("GLH", [1024, nlat + 30], BF16, False)
        GLC = self.state("GLC", [1024, NCTX + 30], BF16, False)
        xin = self.dtmp("xin%d" % l, [1024, 32], BF16)
        xout = self.dtmp("xout%d" % l, [2048, 32], BF16)
        hm = self.din("hmask", [128, 2])
        Bi, Bo, Ba = Buf(), Buf(), self.sbuf(("GLH",))
        srcs = [self.sbuf(("GL", 0)), self.sbuf(("GL", nlat - 512))]
        self.ld(xin[:, 0:15], GLH[:, 15:30], srcs, W=[Bi])
        self.ld(xin[:, 15:30], GLH[:, nlat:nlat + 15], srcs, A=[Bi])
        self.p.coll(lambda e: e.collective_compute("AllGather", ALU.bypass, replica_groups=pairs, ins=[xin], outs=[xout]), [Bi], [Bo])
        hl = self.ab(128, 8, 32)
        hmt = self.af(128, 2)
        h2 = self.ab(128, 8, 32)
        z = self.ab(128, 8, 16)

        def post():
            hB = Buf()
            self.ld(hmt, hm, (), W=[hB])
            xo = xout.rearrange("(r c p) n -> r p c n", r=2, p=128)
            self.ld(hl[:, :, 0:15], xo[0][:, :, 15:30], [Bo], A=[hB])
            self.ld(hl[:, :, 15:30], xo[1][:, :, 0:15], [Bo], A=[hB])
            h2B = Buf()
            self.ts1("vector", h2[:, :, 0:15], hl[:, :, 0:15], hmt[:, 0:1], ALU.mult, [hB], W=[h2B])
            self.ts1("vector", h2[:, :, 15:30], hl[:, :, 15:30], hmt[:, 1:2], ALU.mult, [hB], A=[h2B])
            zB = Buf()
            self.memset("vector", z, 0.0, [zB])
            GLHv = GLH.rearrange("(c p) n -> p c n", p=128)
            GLCv = GLC.rearrange("(c p) n -> p c n", p=128)
            self.st(GLHv[:, :, 0:15], h2[:, :, 0:15], [h2B], A=[Ba])
            self.st(GLHv[:, :, nlat + 15:nlat + 30], h2[:, :, 15:30], [h2B], A=[Ba])
            self.st(GLCv[:, :, 0:15], z[:, :, 0:15], [zB], A=[Ba])
            self.st(GLCv[:, :, NCTX + 15:NCTX + 30], z[:, :, 0:15], [zB], A=[Ba])
        self.x_pending.append(post)

    def st_E4(self, l, ctx_out):
        self.need_layer(l)
        self.need_w(("even", l))
        wB = self.sbuf(("w", ("even", l)))
        N, nk = self.N, self.nk
        nkt = nk // 128
        CKVA = self.state("CKVA", [256, nk], BF16, False).rearrange("(c p) n -> p c n", p=128)
        KRA = self.state("KRA", [32, nk], BF16, False)
        QTN = self.state("QTN", [512, N], BF16, False)
        QTR = self.state("QTR", [256, N], BF16, False)
        AT = self.state("AT", [512, N], BF16, True)
        KN = self.dtmp("KN%d" % l, [512, nk], BF16)
        Bin = self.sbuf(("KVA",))
        mark_b = self.f_off
        ck = self.ab(128, 2, nk)
        ck_end = self.f_off
        ckB = Buf()
        self.ld(ck, CKVA, [Bin], W=[ckB])
        WKV = self.ab(128, 2, 1024)
        WKVB = Buf()
        self.ld(WKV, self.dram["wkvs%d" % l].rearrange("p (k c) -> p k c", k=2), [wB], W=[WKVB])
        Vall = self.ab(128, 8, nkt, 65)
        VB = Buf()
        self.memset("gpsimd", Vall, 1.0, [VB])
        kst = [self.ab(128, 512) for _ in range(2)]
        kstB = bufs(2)
        KNB = Buf()
        cnt = 0
        for c in range(4):
            for k0 in range(0, nk, 512):
                kw = min(512, nk - k0)
                psi = cnt % 4
                b = cnt % 2
                cnt += 1
                self.mmacc(psi, self.PS[psi][:, :kw], [(WKV[:, k2, c * 128:(c + 1) * 128], ck[:, k2, k0:k0 + kw]) for k2 in range(2)], [WKVB, ckB])
                self.cp("scalar" if b else "vector", kst[b][:, :kw], self.PS[psi][:, :kw], [self.PB[psi]], W=[kstB[b]])
                self.st(KN[c * 128:(c + 1) * 128, k0:k0 + kw], kst[b][:, :kw], [kstB[b]], A=[KNB])
        for kt in range(nkt):
            psi = cnt % 4
            cnt += 1
            self.mmacc(psi, self.PS[psi], [(ck[:, k2, kt * 128:(kt + 1) * 128], WKV[:, k2, 512:1024]) for k2 in range(2)], [WKVB, ckB])
            self.cp("scalar" if kt % 2 else "vector", Vall[:, :, kt, 1:65], self.PS[psi].rearrange("p (h d) -> p h d", h=8), [self.PB[psi]], A=[VB])
        self.p.barrier()
        save = self.f_off
        self.f_off = mark_b
        KH = [self.ab(128, nk) for _ in range(2)]
        assert self.f_off <= ck_end
        self.f_off = save
        KHB = bufs(2)
        Q = [self.ab(128, 512) for _ in range(2)]
        QB = bufs(2)
        PT = [self.ab(128, 1024) for _ in range(3)]
        PTB = bufs(3)
        rec = self.af(128, 512)
        recB = Buf()
        recb = self.af(128, 512)
        recbB = Buf()
        an = [self.ab(128, 512) for _ in range(2)]
        anB = bufs(2)
        qgroups = list(self.groups_lat) + ([self.group_ctx] if ctx_out else [])
        blocks = [(hd, grp) for hd in range(8) for grp in qgroups]
        rounds = []
        for bi, (hd, grp) in enumerate(blocks):
            kts = list(range(nkt)) if grp[2] == 0 else list(range(NCTX // 128))
            assert len(kts) % 2 == 0
            for ii in range(0, len(kts), 2):
                rounds.append((bi, kts[ii], kts[ii + 1], ii == 0, ii + 2 == len(kts)))

        def load_K(hd):
            kb = hd % 2
            self.ld(KH[kb][0:64, :], KN[hd * 64:(hd + 1) * 64, :], [KNB], W=[KHB[kb]])
            self.ld(KH[kb][64:96, :], KRA, [Bin], A=[KHB[kb]])

        def load_Q(bi):
            hd, (col0, T, w) = blocks[bi]
            qb = bi % 2
            self.ld(Q[qb][0:64, :T], QTN[hd * 64:(hd + 1) * 64, col0:col0 + T], [self.sbuf(("QTN", col0))], W=[QB[qb]])
            self.ld(Q[qb][64:96, :T], QTR[hd * 32:(hd + 1) * 32, col0:col0 + T], [self.sbuf(("QTR", col0))], A=[QB[qb]])

        load_K(0)
        load_Q(0)
        BG_PLAN = {0: [("ffn", 0, 1), ("ffn", 1, 0), ("odd", 1), ("ffn", 1, 1), ("ffn", 2, 0), ("even", 2)],
                   2: [("ffn", 2, 1), ("ffn", 3, 0), ("odd", 3), ("ffn", 3, 1)]}
        jobs = []
        for key in BG_PLAN.get(l, []):
            if key not in self.wdone:
                self.wdone.add(key)
                jobs += self.w_pieces(key)
        self.cvt_setup(3)
        jobs.reverse()
        pend = []

        def hook():
            if jobs:
                pend.append(self.cvt_emit(jobs.pop(), ldq="sync", stq="sync", engs=("gpsimd",), phase=0))
            if pend and (len(pend) > 1 or not jobs):
                self.cvt_emit(pend.pop(0), ldq="sync", stq="sync", engs=("gpsimd",), phase=1)

        SK = 1
        SB = [Buf(), Buf()]
        nr = len(rounds)
        for idx in range(nr + SK):
            if idx < nr:
                bi, kt0, kt1, isfirst, islast = rounds[idx]
                hd, (col0, T, w) = blocks[bi]
                if isfirst and bi + 1 < len(blocks):
                    if blocks[bi + 1][0] != hd:
                        load_K(hd + 1)
                    load_Q(bi + 1)
                sp = idx % 2
                pt = idx % 3
                for half, kt in enumerate((kt0, kt1)):
                    self.mm(self.PSD[sp][:, half * 512:half * 512 + T], KH[hd % 2][0:96, kt * 128:(kt + 1) * 128], Q[bi % 2][0:96, :T], True, True,
                            [KHB[hd % 2], QB[bi % 2]], **({"W": [SB[sp]]} if half == 0 else {"A": [SB[sp]]}))
                if T == 512:
                    self.act(PT[pt], self.PSD[sp], AF.Exp, [SB[sp]], W=[PTB[pt]], scale=ATT_SCALE)
                else:
                    self.act(PT[pt].rearrange("p (a t) -> p a t", a=2)[:, :, :T], self.PSD[sp].rearrange("p (a t) -> p a t", a=2)[:, :, :T], AF.Exp,
                             [SB[sp]], W=[PTB[pt]], scale=ATT_SCALE)
                if idx % 8 == 2:
                    hook()
            j = idx - SK
            if j < 0:
                continue
            bi, kt0, kt1, isfirst, islast = rounds[j]
            hd, (col0, T, w) = blocks[bi]
            po = 4 + (bi % 2)
            pt = j % 3
            for half, kt in enumerate((kt0, kt1)):
                st_ = isfirst and half == 0
                self.mm(self.PS[po][0:65, :T], Vall[:, hd, kt, :], PT[pt][:, half * 512:half * 512 + T], st_, islast and half == 1, [VB, PTB[pt]],
                        **({"W": [self.PB[po]]} if st_ else {"A": [self.PB[po]]}))
            if not islast:
                continue
            self.p.op("vector", lambda e, po=po, T=T: e.reciprocal(out=rec[0:1, :T], in_=self.PS[po][0:1, :T]), [self.PB[po]], [recB])
            self.mm(self.PS[6][0:65, :T], self.ones_f[0:1, 0:65], rec[0:1, :T], True, True, [recB], W=[self.PB[6]])
            self.cp("scalar", recb[0:65, :T], self.PS[6][0:65, :T], [self.PB[6]], W=[recbB])
            ab_ = bi % 2
            self.tt("vector", an[ab_][0:65, :T], self.PS[po][0:65, :T], recb[0:65, :T], ALU.mult, [self.PB[po], recbB], W=[anB[ab_]])
            self.p.dma("sync", AT[hd * 64:(hd + 1) * 64, col0:col0 + T], an[ab_][1:65, :T], [anB[ab_]], (), [self.sbuf(("AT", col0))])
        while jobs or pend:
            hook()

    def st_E5(self, l, ctx_out):
        self.need_layer(l)
        self.need_w(("even", l))
        wB = self.sbuf(("w", ("even", l)))
        N = self.N
        AT = self.state("AT", [512, N], BF16, False).rearrange("(c p) n -> p c n", p=128)
        BL = self.state("BL", [512, N], BF16, False).rearrange("(c p) n -> p c n", p=128)
        WO = self.ab(128, 8, 1024)
        WOB = Buf()
        self.ld(WO, self.dram["wos%d" % l].rearrange("p (k c) -> p k c", k=8), [wB], W=[WOB])
        r = [self.af(128, 8, 512) for _ in range(2)]
        rB = [bufs(8) for _ in range(2)]
        ab_ = [self.ab(128, 8, 512) for _ in range(2)]
        abB = bufs(2)
        ytmp = [(self.af(128, 512), Buf()) for _ in range(2)]
        tmp = self.stats_tmp(8, 0, 1)
        groups = list(self.groups_lat) + ([self.group_ctx] if ctx_out else [])
        for gi, grp in enumerate(groups):
            col0, T, w = grp
            b = gi % 2
            self.load_r(r[b], rB[b], grp)
            self.ld(ab_[b][:, 0:4, :T], AT[:, :, col0:col0 + T], [self.sbuf(("AT", col0))], W=[abB[b]])
            self.ld(ab_[b][:, 4:8, :T], BL[:, :, col0:col0 + T], [self.sbuf(("BL", col0))], A=[abB[b]])
            for m in range(8):
                py = 2 + (m % 4)
                self.mmacc(py, self.PS[py][:, :T], [(WO[:, k, m * 128:(m + 1) * 128], ab_[b][:, k, :T]) for k in range(8)], [WOB, abB[b]])
                self.epilogue(py, m, T, r[b], rB[b], self.coef(l, "gw", 1, w, m), ytmp[m % 2])
            self.resid_ln_store(l, 1, grp, r[b], rB[b], tmp)

    def st_O1(self, l, ctx_out):
        self.need_layer(l)
        self.need_w(("odd", l))
        V = self.vec[l]
        wB = self.sbuf(("w", ("odd", l)))
        N = self.N
        if self.fused:
            GLHw = self.state("GLH", [1024, self.nlat + 30], BF16, True).rearrange("(c p) n -> p c n", p=128)
            GLCw = self.state("GLC", [1024, NCTX + 30], BF16, True).rearrange("(c p) n -> p c n", p=128)
        else:
            GL = self.state("GL", [1024, N], BF16, True).rearrange("(c p) n -> p c n", p=128)
        WP = self.ab(128, 8, 2048)
        WPB = Buf()
        self.ld(WP, self.dram["wp1s%d" % l].rearrange("p (k c) -> p k c", k=8), [wB], W=[WPB])
        r = [self.af(128, 8, 512)] * 2
        rB = [bufs(8)] * 2
        h = [self.ab(128, 8, 512) for _ in range(2)]
        hB = [bufs(8) for _ in range(2)]
        gl = [self.ab(128, 8, 512) for _ in range(2)]
        glB = bufs(2)
        sg = [self.af(128, 512) for _ in range(2)]
        sgB = bufs(2)
        groups = list(self.groups_lat) + ([self.group_ctx] if ctx_out else [])
        cnt = 0
        for gi, grp in enumerate(groups):
            col0, T, w = grp
            b = gi % 2
            self.load_r(r[b], rB[b], grp)
            self.modulate(l, 1, grp, r[b], rB[b], h[b], hB[b])
            for m in range(8):
                pa = (cnt % 4) * 2
                pg = pa + 1
                cnt += 1
                self.mmacc(pa, self.PS[pa][:, :T], [(WP[:, k, m * 128:(m + 1) * 128], h[b][:, k, :T]) for k in range(8)], [WPB] + hB[b])
                self.mmacc(pg, self.PS[pg][:, :T], [(WP[:, k, 1024 + m * 128:1024 + (m + 1) * 128], h[b][:, k, :T]) for k in range(8)], [WPB] + hB[b])
                si = cnt % 2
                self.act(sg[si][:, :T], self.PS[pg][:, :T], AF.Sigmoid, [self.PB[pg], self.vecB], W=[sgB[si]], bias=V["bpw1"][:, 8 + m:9 + m])
                self.stt("vector", gl[b][:, m, :T], self.PS[pa][:, :T], V["bpw1"][:, m:m + 1], sg[si][:, :T], ALU.add, ALU.mult,
                         [self.PB[pa], sgB[si], self.vecB], **({"W": [glB[b]]} if m == 0 else {"A": [glB[b]]}))
            if not self.fused:
                dstv = GL[:, :, col0:col0 + T]
            elif w == 0:
                dstv = GLHw[:, :, 15 + col0:15 + col0 + T]
            else:
                dstv = GLCw[:, :, 15:15 + T]
            self.st(dstv, gl[b][:, :, :T], [glB[b]], W=[self.sbuf(("GL", col0))])
            if self.fused and X_OVERLAP and w == 0 and col0 + T == self.nlat:
                self.x_odd_pre(l)
        if self.fused and X_OVERLAP:
            self.x_flush()
            self.x_done.add(l)

    def st_O2(self, l, ctx_out):
        self.need_layer(l)
        self.need_w(("odd", l))
        V = self.vec[l]
        wB = self.sbuf(("w", ("odd", l)))
        GLH = self.state("GLH", [1024, self.nlat + 30], BF16, False).rearrange("(c p) n -> p c n", p=128)
        Bin = self.sbuf(("GLH",))
        WO = self.ab(128, 8, 1024)
        WOB = Buf()
        self.ld(WO, self.dram["wos%d" % l].rearrange("p (k c) -> p k c", k=8), [wB], W=[WOB])
        xh = [self.ab(128, 8, 542) for _ in range(2)]
        xhB = bufs(2)
        diag = self.ab(128, 248, 128)
        dB = Buf()
        for wc in range(248):
            self.ts1("vector", diag[:, wc, :], self.ident_b, V["wdw"][:, wc:wc + 1], ALU.mult, [self.vecB, self.Bc],
                     **({"W": [dB]} if wc == 0 else {"A": [dB]}))
        acc2 = [self.af(128, 8, 512) for _ in range(2)]
        accB2 = [bufs(8) for _ in range(2)]
        hs2 = [self.ab(128, 8, 512)] * 2
        hsB2 = [bufs(8)] * 2
        r2 = [self.af(128, 8, 512)] * 2
        rB2 = [bufs(8)] * 2
        NPE = 8
        ytmp = [(self.af(128, 512), Buf()) for _ in range(2)]
        tmp = self.stats_tmp(8, 0, 1, xb=(hs2[0], hsB2[0]))
        groups = list(self.groups_lat)
        if ctx_out:
            GLC = self.state("GLC", [1024, NCTX + 30], BF16, False).rearrange("(c p) n -> p c n", p=128)
            groups.append(self.group_ctx)
        for gi, grp in enumerate(groups):
            col0, T, w = grp
            b = gi % 2
            if w == 0:
                self.ld(xh[b][:, :, :T + 30], GLH[:, :, col0:col0 + T + 30], [Bin], W=[xhB[b]])
            else:
                self.ld(xh[b][:, :, :T + 30], GLC[:, :, 0:T + 30], [Bin], W=[xhB[b]])
            acc, accB, hs, hsB, r, rB = acc2[b], accB2[b], hs2[b], hsB2[b], r2[b], rB2[b]
            self.load_r(r, rB, grp)
            for c in range(NPE, 8):
                self.ts("vector", acc[:, c, :T], xh[b][:, c, 0:T], V["wdw"][:, c:c + 1], V["bdw"][:, c:c + 1], ALU.mult, ALU.add,
                        [xhB[b], self.vecB], W=[accB[c]])
                for wi in range(1, 31):
                    self.stt("vector", acc[:, c, :T], xh[b][:, c, wi:wi + T], V["wdw"][:, wi * 8 + c:wi * 8 + c + 1], acc[:, c, :T], ALU.mult, ALU.add,
                             [xhB[b], accB[c], self.vecB], W=[accB[c]])
            for c in range(NPE):
                psi = 2 + (c % 4)
                self.mmacc(psi, self.PS[psi][:, :T], [(diag[:, wi * 8 + c, :], xh[b][:, c, wi:wi + T]) for wi in range(31)], [dB, xhB[b]])
                self.act(acc[:, c, :T], self.PS[psi][:, :T], AF.Identity, [self.PB[psi], self.vecB], W=[accB[c]], bias=V["bdw"][:, c:c + 1])
            self.stats([(acc[:, c, :T], accB[c]) for c in range(8)], T, 1024.0, 0, True, tmp)
            self.ln_apply(acc, accB, 8, T, lambda c: V["olng"][:, c:c + 1], lambda c: V["olnb"][:, c:c + 1], hs, hsB, tmp, func=AF.Silu)
            for m in range(8):
                py = 2 + (m % 4)
                self.mmacc(py, self.PS[py][:, :T], [(WO[:, k, m * 128:(m + 1) * 128], hs[:, k, :T]) for k in range(8)], [WOB] + hsB)
                self.epilogue(py, m, T, r, rB, self.coef(l, "gw", 1, w, m), ytmp[m % 2], bias=V["bg"][:, w * 8 + m:w * 8 + m + 1])
            self.resid_ln_store(l, 1, grp, r, rB, tmp)


STATE_SHAPES = None


def rope_tables(nlat, half):
    t = np.arange(half * nlat, (half + 1) * nlat)
    rows = (t // GRID_W).astype(np.float32)
    cols = (t % GRID_W).astype(np.float32)
    hf = 16
    inv = (1.0 / (np.float32(10000.0) ** (np.arange(0, hf, 2, dtype=np.float32) / np.float32(hf)))).astype(np.float32)
    ang = np.concatenate([rows[:, None] * inv, cols[:, None] * inv], axis=-1).astype(np.float32)
    c = np.cos(ang).astype(np.float32)
    s = np.sin(ang).astype(np.float32)
    C = np.repeat(c, 2, axis=1)
    S = np.repeat(s, 2, axis=1)
    S[:, 0::2] *= -1.0
    C = np.tile(C, (1, 4)).T.copy()
    S = np.tile(S, (1, 4)).T.copy()
    return C, S


def core_inputs(inp, core, nlat, names):
    b, half = core // 2, core % 2
    f = lambda a: np.ascontiguousarray(a, dtype=np.float32)
    d = {}
    for n in names:
        if n == "ident":
            d[n] = np.eye(128, dtype=np.float32)
        elif n == "hmask":
            d[n] = np.tile(np.array([[half, 1 - half]], np.float32), (128, 1))
        elif n == "x":
            d[n] = f(inp["x"][b, half * nlat:(half + 1) * nlat, :])
        elif n == "ctx":
            d[n] = f(inp["ctx"][b])
        elif n == "cc":
            d[n] = f(np.stack([inp["c"][b], inp["c_ctx"]]).reshape(16, 128))
        elif n == "ropec" or n == "ropes":
            C, S = rope_tables(nlat, half)
            d[n] = C if n == "ropec" else S
        elif n.startswith("w_mod"):
            d[n] = f(inp["w_mod"][int(n[5:])])
        elif n.startswith("b_mod"):
            d[n] = f(inp["b_mod"][int(n[5:])][None, :])
        elif n.startswith("ln_g"):
            d[n] = f(inp["ln_g"][int(n[4:])].reshape(24, 128))
        elif n.startswith("ln_b"):
            d[n] = f(inp["ln_b"][int(n[4:])].reshape(24, 128))
        elif n.startswith("w13_"):
            l, i = int(n[4]), int(n[6])
            d[n] = f(inp["ffn_w13"][l, i])
        elif n.startswith("w2_"):
            l, i = int(n[3]), int(n[5])
            d[n] = f(inp["ffn_w2"][l, i])
        elif n.startswith("w_in"):
            d[n] = f(inp["e_w_in"][int(n[4:]) // 2])
        elif n.startswith("w_q"):
            d[n] = f(inp["e_w_q_up"][int(n[3:]) // 2])
        elif n.startswith("w_kv"):
            d[n] = f(inp["e_w_kv_up"][int(n[4:]) // 2])
        elif n.startswith("w_out"):
            l = int(n[5:])
            d[n] = f(inp["e_w_out"][l // 2] if l % 2 == 0 else inp["o_w_out"][l // 2])
        elif n.startswith("qn"):
            d[n] = f(inp["e_q_norm"][int(n[2:]) // 2].reshape(3, 128))
        elif n.startswith("kvn"):
            d[n] = f(inp["e_kv_norm"][int(n[3:]) // 2].reshape(2, 128))
        elif n.startswith("sgu_g"):
            d[n] = f(inp["e_sgu_g"][int(n[5:]) // 2][None, :])
        elif n.startswith("sgu_b"):
            d[n] = f(inp["e_sgu_b"][int(n[5:]) // 2][None, :])
        elif n.startswith("b_s"):
            d[n] = f(inp["e_b_s"][int(n[3:]) // 2].reshape(1, 512))
        elif n.startswith("w_s"):
            d[n] = f(inp["e_w_s"][int(n[3:]) // 2])
        elif n.startswith("w_pw1"):
            d[n] = f(inp["o_w_pw1"][int(n[5:]) // 2])
        elif n.startswith("b_pw1"):
            d[n] = f(inp["o_b_pw1"][int(n[5:]) // 2].reshape(16, 128))
        elif n.startswith("w_dw"):
            d[n] = f(inp["o_w_dw"][int(n[4:]) // 2].reshape(248, 128))
        elif n.startswith("b_dw"):
            d[n] = f(inp["o_b_dw"][int(n[4:]) // 2].reshape(8, 128))
        elif n.startswith("o_ln_g"):
            d[n] = f(inp["o_ln_g"][int(n[6:]) // 2].reshape(8, 128))
        elif n.startswith("o_ln_b"):
            d[n] = f(inp["o_ln_b"][int(n[6:]) // 2].reshape(8, 128))
        elif n.startswith("b_out"):
            d[n] = f(inp["o_b_out"][int(n[5:]) // 2].reshape(8, 128))
        else:
            raise KeyError(n)
    return d


SEG_STATE = [
    ((), ("RES", "QTN", "QTR", "BL", "CKV", "KR")),
    (("RES", "QTN", "QTR", "BL", "CKVA", "KRA"), ("RES", "GL")),
    (("RES", "GLH", "GLC"), ("RES", "QTN", "QTR", "BL", "CKV", "KR")),
    (("RES", "QTN", "QTR", "BL", "CKVA", "KRA"), ("RES", "GL")),
    (("RES", "GLH"), ()),
]

_prog_cache = {}


def get_prog(nlat, si):
    key = (nlat, si)
    if key not in _prog_cache:
        segs = segments()
        b = Builder(nlat, segs[si], SEG_STATE[si][0], SEG_STATE[si][1])
        b.build()
        _prog_cache[key] = b
    return _prog_cache[key]


def exchange(si, states, nlat, ncores):
    new = [dict() for _ in range(ncores)]
    for c in range(ncores):
        st = states[c]
        mate = states[c ^ 1]
        half = c % 2
        lo, hi = (st, mate) if half == 0 else (mate, st)
        n = new[c]
        n["RES_i"] = st["RES_o"]
        if "CKV_o" in st:
            for nm in ("QTN", "QTR", "BL"):
                n[nm + "_i"] = st[nm + "_o"]
            n["CKVA_i"] = np.concatenate([st["CKV_o"][:, nlat:], lo["CKV_o"][:, :nlat], hi["CKV_o"][:, :nlat]], axis=1)
            n["KRA_i"] = np.concatenate([st["KR_o"][:, nlat:], lo["KR_o"][:, :nlat], hi["KR_o"][:, :nlat]], axis=1)
        if "GL_o" in st:
            gl = st["GL_o"]
            z = np.zeros((1024, 15), gl.dtype)
            left = z if half == 0 else mate["GL_o"][:, nlat - 15:nlat]
            right = mate["GL_o"][:, 0:15] if half == 0 else z
            n["GLH_i"] = np.concatenate([left, gl[:, :nlat], right], axis=1)
            n["GLC_i"] = np.concatenate([z, gl[:, nlat:], z], axis=1)
    return new


def run_all(inp, nlat, ncores, seg_range=None, debug=None):
    states = [dict() for _ in range(ncores)]
    nseg = len(SEG_STATE)
    rng = range(nseg) if seg_range is None else seg_range
    for si in rng:
        b = get_prog(nlat, si)
        in_maps = []
        for c in range(ncores):
            names = [n for n in b.inputs if not n.endswith("_i")]
            d = core_inputs(inp, c, nlat, names)
            for n in b.inputs:
                if n.endswith("_i"):
                    d[n] = np.ascontiguousarray(states[c][n])
            in_maps.append(d)
        res = run_bass_kernel_spmd(b.nc, in_maps, core_ids=list(range(ncores)))
        outs = res.results
        if debug is not None:
            debug.append(outs)
        if si == nseg - 1:
            return outs
        states = exchange(si, outs, nlat, ncores)
    return states


def get_fused(nlat, ncores=8):
    key = (nlat, "fused", ncores)
    if key not in _prog_cache:
        b = Builder(nlat, stage_list(), (), (), fused=True, ncores=ncores)
        b.build()
        _prog_cache[key] = b
    return _prog_cache[key]


def run_fused(inp, nlat, ncores):
    b = get_fused(nlat, ncores)
    in_maps = [core_inputs(inp, c, nlat, b.inputs) for c in range(ncores)]
    res = run_bass_kernel_spmd(b.nc, in_maps, core_ids=list(range(ncores)))
    return res.results


FUSED = True


def kernel(**inputs):
    inp = {k: np.asarray(v) for k, v in inputs.items()}
    B, S, _ = inp["x"].shape
    nlat = S // 2
    ncores = 2 * B
    outs = run_fused(inp, nlat, ncores) if FUSED else run_all(inp, nlat, ncores)
    out = np.empty((B, S, D), np.float32)
    for c in range(ncores):
        out[c // 2, (c % 2) * nlat:(c % 2 + 1) * nlat, :] = outs[c]["out"]
    return out
```

```python
import math
import contextlib
import numpy as np
import concourse.bass as bass
import concourse.mybir as mybir
from concourse.bass_utils import run_bass_kernel_spmd

F32 = mybir.dt.float32
BF16 = mybir.dt.bfloat16
AF = mybir.ActivationFunctionType
ALU = mybir.AluOpType
ENGS = ["sync", "scalar", "vector", "gpsimd", "tensor"]
NDMA = 56
NDMA_HW = 36

D = 1024
NCTX = 256
DFF = 2816
NJ = 22
ALPHA = 8.0 ** 0.25
LN_EPS = 1e-5
RMS_EPS = 1e-6
ATT_SCALE = 1.0 / math.sqrt(96.0)
GRID_W = 64
WIN_COLS = 1728
X_OVERLAP = False


class Buf:
    __slots__ = ("name", "w", "r", "p")

    def __init__(self, name=""):
        self.name = name
        self.w = {}
        self.r = {}
        self.p = {}


def bufs(n, name=""):
    return [Buf("%s%d" % (name, i)) for i in range(n)]


class Op:
    __slots__ = ("eng", "fn", "waits", "flag", "seq", "semval", "dma", "cc")

    def __init__(self, eng, fn):
        self.eng = eng
        self.fn = fn
        self.waits = []
        self.flag = False
        self.semval = 0
        self.dma = None
        self.cc = None


class Prog:
    def __init__(self, nc):
        self.nc = nc
        self.streams = {e: [] for e in ENGS}
        self.waited = {e: {} for e in ENGS}
        self.dma_val = [0] * NDMA
        self.dma_rr = {"sync": 0, "gpsimd": 0}
        self.dma_pool = {"sync": list(range(0, NDMA_HW)), "gpsimd": list(range(NDMA_HW, NDMA))}
        self.ncc = 0

    @staticmethod
    def _merge(dst, src):
        for k, v in src.items():
            if dst.get(k, -1) < v:
                dst[k] = v

    def _finish(self, op, deps, reads, writes, adds, ev_key, ev_val):
        eng = op.eng
        wd = self.waited[eng]
        for k, v in deps.items():
            if k[0] == "e" and k[1] == eng and eng in ("tensor", "sync"):
                continue
            if wd.get(k, -1) >= v:
                continue
            wd[k] = v
            op.waits.append((k, v))
            if k[0] == "e":
                self.streams[k[1]][v].flag = True
        for b in reads:
            if b.r.get(ev_key, -1) < ev_val:
                b.r[ev_key] = ev_val
        for b in writes:
            pg = dict(b.w)
            self._merge(pg, b.r)
            b.p = pg
            b.w = {ev_key: ev_val}
            b.r = {}
        for b in adds:
            if b.w.get(ev_key, -1) < ev_val:
                b.w[ev_key] = ev_val

    def _deps(self, reads, writes, adds):
        deps = {}
        for b in reads:
            self._merge(deps, b.w)
        for b in writes:
            self._merge(deps, b.w)
            self._merge(deps, b.r)
        for b in adds:
            self._merge(deps, b.r)
            self._merge(deps, b.p)
        return deps

    def op(self, eng, fn, reads=(), writes=(), adds=()):
        o = Op(eng, fn)
        st = self.streams[eng]
        o.seq = len(st)
        deps = self._deps(reads, writes, adds)
        st.append(o)
        self._finish(o, deps, reads, writes, adds, ("e", eng), o.seq)
        return o

    def dma(self, q, out, in_, reads=(), writes=(), adds=()):
        pool = self.dma_pool[q]
        s = pool[self.dma_rr[q] % len(pool)]
        self.dma_rr[q] += 1
        o = Op(q, lambda e: e.dma_start(out=out, in_=in_))
        st = self.streams[q]
        o.seq = len(st)
        deps = self._deps(reads, writes, adds)
        prev = self.dma_val[s]
        if prev > 0:
            k = ("d", s)
            if deps.get(k, -1) < prev:
                deps[k] = prev
        self.dma_val[s] = prev + 16
        o.dma = s
        st.append(o)
        self._finish(o, deps, reads, writes, adds, ("d", s), prev + 16)
        return o

    def coll(self, fn, reads=(), writes=()):
        idx = self.ncc
        self.ncc += 1
        o = Op("gpsimd", fn)
        st = self.streams["gpsimd"]
        o.seq = len(st)
        deps = self._deps(reads, writes, ())
        o.cc = idx
        o.dma = -1
        st.append(o)
        self._finish(o, deps, reads, writes, (), ("c", idx), 1)
        return o

    def barrier(self):
        deps = {}
        for e in ENGS:
            n = len(self.streams[e])
            if n and e != "sync":
                for i in range(n - 1, -1, -1):
                    if self.streams[e][i].dma is None and self.streams[e][i].fn is not None:
                        deps[("e", e)] = i
                        break
        for s in range(NDMA):
            if self.dma_val[s] > 0:
                deps[("d", s)] = self.dma_val[s]
        for i in range(self.ncc):
            deps[("c", i)] = 1
        for e in ENGS:
            o = Op(e, None)
            o.seq = len(self.streams[e])
            self.streams[e].append(o)
            self._finish(o, dict(deps), (), (), (), ("e", e), o.seq)

    def emit(self):
        nc = self.nc
        self.barrier()
        for e in ENGS:
            c = 0
            for o in self.streams[e]:
                if o.flag and o.dma is None:
                    c += 1
                    o.semval = c
        with contextlib.ExitStack() as es:
            esem = {e: es.enter_context(nc.semaphore("es_" + e)) for e in ENGS}
            dsem = [es.enter_context(nc.semaphore("ds_%d" % i)) for i in range(NDMA)]
            csem = [es.enter_context(nc.semaphore("cs_%d" % i)) for i in range(self.ncc)]
            block = es.enter_context(nc.Block())

            def run(engname):
                def body(eng):
                    for o in self.streams[engname]:
                        for k, v in o.waits:
                            if k[0] == "e":
                                eng.wait_ge(esem[k[1]], self.streams[k[1]][v].semval)
                            elif k[0] == "c":
                                eng.wait_ge(csem[k[1]], v)
                            else:
                                eng.wait_ge(dsem[k[1]], v)
                        if o.fn is None:
                            if o.flag:
                                eng.nop().then_inc(esem[engname], 1)
                            continue
                        ins = o.fn(eng)
                        if o.cc is not None:
                            ins.then_inc(csem[o.cc], 1)
                        elif o.dma is not None:
                            ins.then_inc(dsem[o.dma], 16)
                        elif o.flag:
                            ins.then_inc(esem[engname], 1)
                return body

            block.sync(run("sync"))
            block.scalar(run("scalar"))
            block.vector(run("vector"))
            block.gpsimd(run("gpsimd"))
            block.tensor(run("tensor"))


def stage_list():
    L = [("PRO",)]
    for l in range(4):
        ci = l <= 2
        co = l <= 1
        L.append(("F", l, 0, ci))
        if l % 2 == 0:
            L.append(("E2", l, ci, co))
            L.append(("X", l))
            L.append(("E4", l, co))
            L.append(("E5", l, co))
        else:
            L.append(("O1", l, co))
            L.append(("X", l))
            L.append(("O2", l, co))
        L.append(("F", l, 2, co))
    L.append(("EPI",))
    return L


def segments():
    segs = [[]]
    for st in stage_list():
        if st[0] == "X":
            segs.append([])
        else:
            segs[-1].append(st)
    return segs


class Builder:
    def __init__(self, nlat, stages, state_in, state_out, fused=False, ncores=8):
        self.fused = fused
        self.ncores = ncores
        self.nc = bass.Bass("TRN2", target_bir_lowering=False)
        self.p = Prog(self.nc)
        self.nlat = nlat
        self.N = nlat + NCTX
        self.nk = NCTX + 2 * nlat
        self.stages = stages
        self.state_in = set(state_in)
        self.state_out = set(state_out)
        self.dram = {}
        self.inputs = []
        self.outputs = []
        self.wdone = set()
        self.groups_lat = [(g * 512, 512, 0) for g in range(nlat // 512)]
        self.group_ctx = (nlat, NCTX, 1)
        self.res_written = set()
        self.x_done = set()
        self.x_pending = []

    def din(self, name, shape, dt=F32):
        if name not in self.dram:
            self.dram[name] = self.nc.dram_tensor(name, list(shape), dt, kind="ExternalInput").ap()
            self.inputs.append(name)
        return self.dram[name]

    def dtmp(self, name, shape, dt):
        if name not in self.dram:
            self.dram[name] = self.nc.dram_tensor(name, list(shape), dt).ap()
        return self.dram[name]

    def dout(self, name, shape, dt=F32):
        if name not in self.dram:
            self.dram[name] = self.nc.dram_tensor(name, list(shape), dt, kind="ExternalOutput").ap()
            self.outputs.append(name)
        return self.dram[name]

    def state(self, name, shape, dt, write):
        if write:
            if name in self.state_out:
                return self.dout(name + "_o", shape, dt)
            return self.dtmp(name + "_t", shape, dt)
        if (name + "_o") in self.dram:
            return self.dram[name + "_o"]
        if (name + "_t") in self.dram:
            return self.dram[name + "_t"]
        assert name in self.state_in, name
        return self.din(name + "_i", shape, dt)

    def sbuf(self, key):
        if key not in self.dbufs:
            self.dbufs[key] = Buf(str(key))
        return self.dbufs[key]

    def reset_arena(self):
        self.f_off = self.f_base

    def af(self, *shape):
        n = int(np.prod(shape[1:]))
        off = self.f_off
        self.f_off += n
        assert self.f_off <= self.f_size, ("arena overflow", self.f_off)
        ap = self.AF[:, off:off + n]
        if len(shape) == 3:
            ap = ap.rearrange("p (a b) -> p a b", a=shape[1])
        elif len(shape) == 4:
            ap = ap.rearrange("p (a b c) -> p a b c", a=shape[1], b=shape[2])
        return ap

    def ab(self, *shape):
        n = int(np.prod(shape[1:]))
        nf = (n + 1) // 2
        off = self.f_off
        self.f_off += nf
        assert self.f_off <= self.f_size, ("arena overflow", self.f_off)
        ap = self.AF[:, off:off + nf].bitcast(BF16)[:, 0:n]
        if len(shape) == 3:
            ap = ap.rearrange("p (a b) -> p a b", a=shape[1])
        elif len(shape) == 4:
            ap = ap.rearrange("p (a b c) -> p a b c", a=shape[1], b=shape[2])
        return ap

    def act(self, out, in_, func, R, W=(), A=(), bias=None, scale=1.0):
        if bias is None:
            self.p.op("scalar", lambda e: e.activation(out=out, in_=in_, func=func, scale=scale), R, W, A)
        else:
            self.p.op("scalar", lambda e: e.activation(out=out, in_=in_, func=func, bias=bias, scale=scale), R, W, A)

    def tt(self, eng, out, in0, in1, op, R, W=(), A=()):
        self.p.op(eng, lambda e: e.tensor_tensor(out=out, in0=in0, in1=in1, op=op), R, W, A)

    def ts(self, eng, out, in0, s1, s2, op0, op1, R, W=(), A=()):
        self.p.op(eng, lambda e: e.tensor_scalar(out=out, in0=in0, scalar1=s1, scalar2=s2, op0=op0, op1=op1), R, W, A)

    def ts1(self, eng, out, in0, s1, op0, R, W=(), A=()):
        self.p.op(eng, lambda e: e.tensor_single_scalar(out=out, in_=in0, scalar=s1, op=op0), R, W, A)

    def stt(self, eng, out, in0, scalar, in1, op0, op1, R, W=(), A=()):
        self.p.op(eng, lambda e: e.scalar_tensor_tensor(out=out, in0=in0, scalar=scalar, in1=in1, op0=op0, op1=op1), R, W, A)

    def cp(self, eng, out, in_, R, W=(), A=()):
        if eng == "scalar":
            self.p.op(eng, lambda e: e.copy(out=out, in_=in_), R, W, A)
        else:
            self.p.op(eng, lambda e: e.tensor_copy(out=out, in_=in_), R, W, A)

    def mm(self, out, lhsT, rhs, start, stop, R, W=(), A=()):
        self.p.op("tensor", lambda e: e.matmul(out, lhsT=lhsT, rhs=rhs, start=start, stop=stop), R, W, A)

    def mmacc(self, psi, out, pairs, R):
        n = len(pairs)
        for i, (l, r) in enumerate(pairs):
            if i == 0:
                self.mm(out, l, r, True, n == 1, R, W=[self.PB[psi]])
            else:
                self.mm(out, l, r, False, i == n - 1, R, A=[self.PB[psi]])

    def memset(self, eng, ap, val, W):
        self.p.op(eng, lambda e: e.memset(ap, val), (), W)

    def ld(self, out, in_, R, W=(), A=()):
        self.p.dma("sync", out, in_, R, W, A)

    def st(self, out, in_, R, W=(), A=()):
        self.p.dma(self.store_q, out, in_, R, W, A)

    def build(self):
        nc = self.nc
        self.store_q = "gpsimd"
        self.dbufs = {}
        with contextlib.ExitStack() as es:
            self.f_size = 51 * 1024 + 512
            self.AF = es.enter_context(nc.sbuf_tensor("arena_f", [128, self.f_size], F32))[:]
            self.PS = [es.enter_context(nc.psum_tensor("ps%d" % i, [128, 512], F32))[:] for i in range(8)]
            self.PB = bufs(8, "ps")
            self.f_off = 0
            self.consts()
            self.f_base = self.f_off
            self.marks = []
            for stg in self.stages:
                self.reset_arena()
                getattr(self, "st_" + stg[0])(*stg[1:])
                self.p.barrier()
                self.marks.append((stg, sum(1 for o in self.p.streams["tensor"] if o.fn is not None)))
            self.p.emit()
        return nc

    def consts(self):
        self.ident = self.af(128, 128)
        self.Bc = Buf("consts")
        idn = self.din("ident", [128, 128])
        self.ld(self.ident, idn, (), [self.Bc])
        self.ones_b = self.ab(128, 128)
        self.memset("vector", self.ones_b, 1.0, [Buf()])
        self.ident_b = self.ab(128, 128)
        self.cp("vector", self.ident_b, self.ident, [self.Bc], A=[self.Bc])
        self.ones_f = self.af(128, 128)
        self.memset("vector", self.ones_f, 1.0, [Buf()])
        self.eps = self.af(128, 2)
        self.memset("vector", self.eps[:, 0:1], LN_EPS, [Buf()])
        self.memset("vector", self.eps[:, 1:2], RMS_EPS, [Buf()])
        self.p.barrier()
        self.vec = {}
        self.vecB = Buf("vec")
        self.layers_mod = set()
        self.layer_vec = set()

    def load_cols(self, rows_ap, R, dst, stgq):
        stg, stgB = stgq
        self.ld(stg[0:R, :], rows_ap, (), [stgB])
        self.mm(self.PS[7][:, 0:R], stg[0:R, :], self.ident[0:R, 0:R], True, True, [stgB, self.Bc], W=[self.PB[7]])
        self.cp("vector", dst, self.PS[7][:, 0:R], [self.PB[7]], A=[self.vecB])

    def need_layer(self, l):
        if l in self.layer_vec:
            return
        self.layer_vec.add(l)
        self.f_off = self.f_base
        V = {}
        lng = self.af(128, 24)
        lnb = self.af(128, 24)
        modt = self.af(128, 72, 2)
        sc1p = self.af(128, 48)
        sh = self.af(128, 48)
        gw = self.af(128, 48)
        V.update(lng=lng, lnb=lnb, sc1p=sc1p, sh=sh, gw=gw)
        if l % 2 == 0:
            V["qng"] = self.af(128, 3)
            V["kvng"] = self.af(128, 2)
            V["sg"] = self.af(128, 512)
            V["sb"] = self.af(128, 512)
            V["bs"] = self.af(128, 512)
            V["wst"] = self.ab(128, 512)
        else:
            V["bpw1"] = self.af(128, 16)
            V["wdw"] = self.af(128, 31 * 8)
            V["bdw"] = self.af(128, 8)
            V["olng"] = self.af(128, 8)
            V["olnb"] = self.af(128, 8)
            V["bout"] = self.af(128, 8)
            V["bg"] = self.af(128, 16)
        self.f_base = self.f_off
        self.vec[l] = V
        stg = self.af(128, 128)
        stgB = Buf("stg")
        sq = (stg, stgB)
        lnga = self.din("ln_g%d" % l, [24, 128])
        lnba = self.din("ln_b%d" % l, [24, 128])
        self.load_cols(lnga, 24, lng, sq)
        self.load_cols(lnba, 24, lnb, sq)
        if l % 2 == 0:
            self.load_cols(self.din("qn%d" % l, [3, 128]), 3, V["qng"], sq)
            self.load_cols(self.din("kvn%d" % l, [2, 128]), 2, V["kvng"], sq)
            sgr = self.din("sgu_g%d" % l, [1, 512])
            sbr = self.din("sgu_b%d" % l, [1, 512])
            bsr = self.din("b_s%d" % l, [1, 512])
            for nm, src in (("sg", sgr), ("sb", sbr), ("bs", bsr)):
                self.ld(V[nm], bass.AP(src.tensor, 0, [[0, 128], [1, 512]]), (), A=[self.vecB])
            wsr = self.din("w_s%d" % l, [4, 128, 128])
            wtmp = self.af(128, 4, 128)
            wB = Buf()
            self.ld(wtmp, wsr.rearrange("g i j -> i g j"), (), [wB])
            for g in range(4):
                self.mm(self.PS[6][:, g * 128:(g + 1) * 128], wtmp[:, g, :], self.ident, True, True, [wB, self.Bc],
                        **({"W": [self.PB[6]]} if g == 0 else {"A": [self.PB[6]]}))
            self.cp("vector", V["wst"], self.PS[6], [self.PB[6]], A=[self.vecB])
        else:
            self.load_cols(self.din("b_pw1%d" % l, [16, 128]), 16, V["bpw1"], sq)
            wd = self.din("w_dw%d" % l, [248, 128])
            self.load_cols(wd[0:128, :], 128, V["wdw"][:, 0:128], sq)
            self.load_cols(wd[128:248, :], 120, V["wdw"][:, 128:248], sq)
            self.load_cols(self.din("b_dw%d" % l, [8, 128]), 8, V["bdw"], sq)
            self.load_cols(self.din("o_ln_g%d" % l, [8, 128]), 8, V["olng"], sq)
            self.load_cols(self.din("o_ln_b%d" % l, [8, 128]), 8, V["olnb"], sq)
            self.load_cols(self.din("b_out%d" % l, [8, 128]), 8, V["bout"], sq)
        cc = self.din("cc", [16, 128])
        scT = self.af(128, 16)
        scB = Buf()
        self.ld(stg[0:16, :], cc, (), [stgB])
        self.mm(self.PS[7][:, 0:16], stg[0:16, :], self.ident[0:16, 0:16], True, True, [stgB, self.Bc], W=[self.PB[7]])
        self.act(scT, self.PS[7][:, 0:16], AF.Silu, [self.PB[7]], W=[scB])
        scK = self.af(128, 8, 2)
        self.cp("vector", scK, scT.rearrange("p (w k) -> p k w", w=2), [scB], W=[scB])
        wm = self.din("w_mod%d" % l, [1024, 9216]).rearrange("(k p) n -> p k n", p=128)
        bmr = self.din("b_mod%d" % l, [1, 9216])
        bmt = [self.af(128, 512) for _ in range(2)]
        bmB = bufs(2)
        wbuf = [self.af(128, 4, 512) for _ in range(4)]
        wbB = bufs(4)
        piece = self.af(128, 512)
        pB = Buf()
        mB = Buf()
        for n in range(18):
            self.ld(bmt[n % 2][0:1, :], bmr[:, n * 512:(n + 1) * 512], (), [bmB[n % 2]])
            for hk in range(2):
                wi = (n * 2 + hk) % 4
                self.ld(wbuf[wi], wm[:, hk * 4:(hk + 1) * 4, n * 512:(n + 1) * 512], (), [wbB[wi]])
                for k4 in range(4):
                    k = hk * 4 + k4
                    self.mm(self.PS[5][0:2, :], scK[:, k, :], wbuf[wi][:, k4, :], k == 0, False, [scB, wbB[wi]],
                            **({"W": [self.PB[5]]} if k == 0 else {"A": [self.PB[5]]}))
            self.mm(self.PS[5][0:2, :], self.ones_f[0:1, 0:2], bmt[n % 2][0:1, :], False, True, [bmB[n % 2]], A=[self.PB[5]])
            self.cp("vector", piece[0:2, :], self.PS[5][0:2, :], [self.PB[5]], W=[pB])
            for q in range(4):
                self.mm(self.PS[6][:, 2 * q:2 * q + 2], piece[0:2, q * 128:(q + 1) * 128], self.ident[0:2, 0:2], True, True,
                        [pB, self.Bc], **({"W": [self.PB[6]]} if q == 0 else {"A": [self.PB[6]]}))
            self.cp("vector", modt[:, n * 4:(n + 1) * 4, :], self.PS[6][:, 0:8].rearrange("p (q w) -> p q w", w=2), [self.PB[6]], A=[mB])
        for s in range(3):
            wt = 1.0 if s == 1 else 0.5
            for w in range(2):
                o = (s * 2 + w) * 8
                self.ts1("vector", sc1p[:, o:o + 8], modt[:, (3 * s + 1) * 8:(3 * s + 1) * 8 + 8, w], 1.0, ALU.add, [mB], A=[self.vecB])
                self.cp("vector", sh[:, o:o + 8], modt[:, (3 * s) * 8:(3 * s) * 8 + 8, w], [mB], A=[self.vecB])
                self.ts1("vector", gw[:, o:o + 8], modt[:, (3 * s + 2) * 8:(3 * s + 2) * 8 + 8, w], wt, ALU.mult, [mB], A=[self.vecB])
        self.p.barrier()
        if l % 2 == 1:
            for w in range(2):
                o = (1 * 2 + w) * 8
                self.tt("vector", V["bg"][:, w * 8:w * 8 + 8], V["bout"], gw[:, o:o + 8], ALU.mult, [self.vecB], A=[self.vecB])
        self.p.barrier()
        self.f_off = self.f_base

    def coef(self, l, name, s, w, c):
        o = (s * 2 + w) * 8 + c
        return self.vec[l][name][:, o:o + 1]

    def cvt_setup(self, nb=2):
        self.cv_f = [self.af(128, 2048) for _ in range(nb)]
        self.cv_b = [self.ab(128, 2048) for _ in range(nb)]
        self.cv_fB = bufs(nb)
        self.cv_bB = bufs(nb)
        self.cv_i = 0
        self.cv_n = nb

    def cvt_emit(self, piece, ldq="sync", stq=None, engs=("vector", "gpsimd"), phase=None):
        if phase == 1:
            piece, i, eng = piece
        else:
            i = self.cv_i % self.cv_n
            eng = engs[self.cv_i % len(engs)]
            self.cv_i += 1
        loads, casts, stores, wB = piece
        stq = stq or self.store_q
        f, b, fB, bB = self.cv_f[i], self.cv_b[i], self.cv_fB[i], self.cv_bB[i]
        if phase != 1:
            for k, (dfn, src) in enumerate(loads):
                self.p.dma(ldq, dfn(f), src, (), **({"writes": [fB]} if k == 0 else {"adds": [fB]}))
        if phase == 0:
            return (piece, i, eng)
        for k, (ofn, ifn) in enumerate(casts):
            self.cp(eng, ofn(b), ifn(f), [fB], **({"W": [bB]} if k == 0 else {"A": [bB]}))
        for (dst, sfn) in stores:
            self.p.dma(stq, dst, sfn(b), [bB], (), [wB])

    def nat_pieces(self, src, K, M, dst, wB):
        out = []
        for k in range(K):
            for c0 in range(0, M, 2048):
                w = min(2048, M - c0)
                out.append(([(lambda f, w=w: f[:, 0:w], src[k * 128:(k + 1) * 128, c0:c0 + w])],
                            [(lambda b, w=w: b[:, 0:w], lambda f, w=w: f[:, 0:w])],
                            [(dst[:, k * M + c0:k * M + c0 + w], lambda b, w=w: b[:, 0:w])], wB))
        return out

    def w_pieces(self, key):
        wB = self.sbuf(("w", key))
        P = []
        kind = key[0]
        if kind == "ffn":
            _, l, i = key
            w13 = self.din("w13_%d_%d" % (l, i), [1024, 2 * DFF]).rearrange("(k p) n -> p k n", p=128)
            w2 = self.din("w2_%d_%d" % (l, i), [DFF, 1024])
            d13 = self.dtmp("w13s_%d_%d" % (l, i), [NJ, 128, 2048], BF16)
            d2 = self.dtmp("w2s_%d_%d" % (l, i), [128, NJ * 1024], BF16)
            for j in range(NJ):
                P.append((
                    [(lambda f: f.rearrange("p (k t c) -> p k t c", k=8, t=2)[:, :, 0, :], w13[:, :, j * 128:(j + 1) * 128]),
                     (lambda f: f.rearrange("p (k t c) -> p k t c", k=8, t=2)[:, :, 1, :], w13[:, :, DFF + j * 128:DFF + (j + 1) * 128])],
                    [(lambda b: b, lambda f: f)],
                    [(d13[j], lambda b: b)], wB))
            P += self.nat_pieces(w2, NJ, 1024, d2, wB)
        elif kind == "even":
            _, l = key
            win = self.din("w_in%d" % l, [1024, 1696])
            dwin = self.dtmp("wins%d" % l, [128, 8 * WIN_COLS], BF16)
            for k in range(8):
                P.append((
                    [(lambda f: f[:, 0:1696], win[k * 128:(k + 1) * 128, :])],
                    [(lambda b: b[:, 0:672], lambda f: f[:, 0:672]),
                     (lambda b: b[:, 672:704].rearrange("p (i t) -> p i t", t=2)[:, :, 0], lambda f: f[:, 640:672].rearrange("p (i t) -> p i t", t=2)[:, :, 1]),
                     (lambda b: b[:, 672:704].rearrange("p (i t) -> p i t", t=2)[:, :, 1], lambda f: f[:, 640:672].rearrange("p (i t) -> p i t", t=2)[:, :, 0]),
                     (lambda b: b[:, 704:1728], lambda f: f[:, 672:1696])],
                    [(dwin[:, k * WIN_COLS:(k + 1) * WIN_COLS], lambda b: b[:, 0:WIN_COLS])], wB))
            wq = self.din("w_q%d" % l, [384, 768])
            dwq = self.dtmp("wqs%d" % l, [128, 3 * 1024], BF16)
            fv = lambda f: f[:, 0:768].rearrange("p (h d) -> p h d", h=8)
            for k in range(3):
                P.append((
                    [(lambda f: f[:, 0:768], wq[k * 128:(k + 1) * 128, :])],
                    [(lambda b: b[:, 0:512].rearrange("p (h d) -> p h d", h=8), lambda f: fv(f)[:, :, 0:64]),
                     (lambda b: b[:, 512:768].rearrange("p (h d) -> p h d", h=8), lambda f: fv(f)[:, :, 64:96]),
                     (lambda b: b[:, 768:1024].rearrange("p (h i t) -> p h i t", h=8, t=2)[:, :, :, 0],
                      lambda f: fv(f)[:, :, 64:96].rearrange("p h (i t) -> p h i t", t=2)[:, :, :, 1]),
                     (lambda b: b[:, 768:1024].rearrange("p (h i t) -> p h i t", h=8, t=2)[:, :, :, 1],
                      lambda f: fv(f)[:, :, 64:96].rearrange("p h (i t) -> p h i t", t=2)[:, :, :, 0])],
                    [(dwq[:, k * 1024:(k + 1) * 1024], lambda b: b[:, 0:1024])], wB))
            wkv = self.din("w_kv%d" % l, [256, 1024])
            dwkv = self.dtmp("wkvs%d" % l, [128, 2 * 1024], BF16)
            fv2 = lambda f: f[:, 0:1024].rearrange("p (h d) -> p h d", h=8)
            for k in range(2):
                P.append((
                    [(lambda f: f[:, 0:1024], wkv[k * 128:(k + 1) * 128, :])],
                    [(lambda b: b[:, 0:512].rearrange("p (h d) -> p h d", h=8), lambda f: fv2(f)[:, :, 0:64]),
                     (lambda b: b[:, 512:1024].rearrange("p (h d) -> p h d", h=8), lambda f: fv2(f)[:, :, 64:128])],
                    [(dwkv[:, k * 1024:(k + 1) * 1024], lambda b: b[:, 0:1024])], wB))
            wo = self.din("w_out%d" % l, [1024, 1024])
            P += self.nat_pieces(wo, 8, 1024, self.dtmp("wos%d" % l, [128, 8 * 1024], BF16), wB)
        elif kind == "odd":
            _, l = key
            P += self.nat_pieces(self.din("w_pw1%d" % l, [1024, 2048]), 8, 2048, self.dtmp("wp1s%d" % l, [128, 8 * 2048], BF16), wB)
            P += self.nat_pieces(self.din("w_out%d" % l, [1024, 1024]), 8, 1024, self.dtmp("wos%d" % l, [128, 8 * 1024], BF16), wB)
        return P

    def need_w(self, key):
        if key in self.wdone:
            return
        self.wdone.add(key)
        mark = self.f_off
        self.cvt_setup(3)
        for piece in self.w_pieces(key):
            self.cvt_emit(piece)
        self.p.barrier()
        self.f_off = mark

    def res_ap(self, write):
        return self.state("RES", [1024, self.N], F32, write).rearrange("(c p) n -> p c n", p=128)

    def res_src(self, col0):
        if col0 in self.res_written:
            return self.res_ap(True)
        return self.din("RES_i", [1024, self.N], F32).rearrange("(c p) n -> p c n", p=128)

    def res_buf(self, col0):
        return self.sbuf(("RES", col0))

    def load_r(self, r, rB, grp):
        col0, T, w = grp
        src = self.res_src(col0)
        self.ld(r[:, :, :T], src[:, :, col0:col0 + T], [self.res_buf(col0)], W=rB)

    def store_r(self, r, rB, grp):
        col0, T, w = grp
        dst = self.res_ap(True)
        self.res_written.add(col0)
        self.st(dst[:, :, col0:col0 + T], r[:, :, :T], rB, W=[self.res_buf(col0)])

    def modulate(self, l, s, grp, r, rB, h, hB):
        col0, T, w = grp
        for c in range(8):
            eng = "vector" if c % 2 == 0 else "gpsimd"
            self.ts(eng, h[:, c, :T], r[:, c, :T], self.coef(l, "sc1p", s, w, c), self.coef(l, "sh", s, w, c), ALU.mult, ALU.add,
                    [rB[c], self.vecB], W=[hB[c]])

    def stats(self, srcs, T, F, eps_col, want_mean, tmp):
        C = len(srcs)
        xb, sq = tmp["xb"], tmp["sq"]
        xbB, sqB = tmp["xbB"], tmp["sqB"]
        for c, (x, xB) in enumerate(srcs):
            if want_mean:
                self.cp("gpsimd" if c % 2 else "vector", xb[:, c, :T], x, [xB], W=[xbB[c]])
            self.act(sq[:, c, :T], x, AF.Square, [xB], W=[sqB[c]])
        pm, pq = tmp["pm"], tmp["pq"]
        if want_mean:
            self.mmacc(pm, self.PS[pm][:, :T], [(self.ones_b, xb[:, c, :T]) for c in range(C)], list(xbB[:C]))
        self.mmacc(pq, self.PS[pq][:, :T], [(self.ones_b, sq[:, c, :T]) for c in range(C)], list(sqB[:C]))
        sB = tmp["sB"]
        rstd, nmr, mean, m2 = tmp["rstd"], tmp["nmr"], tmp["mean"], tmp["m2"]
        if want_mean:
            self.act(mean[:, :T], self.PS[pm][:, :T], AF.Copy, [self.PB[pm]], W=[sB], scale=1.0 / F)
            self.tt("vector", m2[:, :T], mean[:, :T], mean[:, :T], ALU.mult, [sB], W=[tmp["m2B"]])
            self.stt("vector", m2[:, :T], self.PS[pq][:, :T], 1.0 / F, m2[:, :T], ALU.mult, ALU.subtract, [self.PB[pq], tmp["m2B"]], W=[tmp["m2B"]])
            self.act(rstd[:, :T], m2[:, :T], AF.Sqrt, [tmp["m2B"]], W=[tmp["rsB"]], bias=self.eps[:, eps_col:eps_col + 1])
        else:
            self.act(rstd[:, :T], self.PS[pq][:, :T], AF.Sqrt, [self.PB[pq]], W=[tmp["rsB"]], bias=self.eps[:, eps_col:eps_col + 1], scale=1.0 / F)
        self.p.op("vector", lambda e: e.reciprocal(out=rstd[:, :T], in_=rstd[:, :T]), [tmp["rsB"]], [tmp["rsB"]])
        if want_mean:
            self.stt("vector", nmr[:, :T], mean[:, :T], -1.0, rstd[:, :T], ALU.mult, ALU.mult, [sB, tmp["rsB"]], W=[tmp["nmB"]])

    def stats_tmp(self, C, pm, pq, xb=None, sq=None):
        t = dict(xb=xb[0] if xb else self.ab(128, C, 512), sq=sq[0] if sq else self.ab(128, C, 512),
                 xbB=xb[1] if xb else bufs(C), sqB=sq[1] if sq else bufs(C), pm=pm, pq=pq, sB=Buf(), m2B=Buf(), rsB=Buf(), nmB=Buf(),
                 rstd=self.af(128, 512), nmr=self.af(128, 512), mean=self.af(128, 512), m2=self.af(128, 512),
                 t1=[self.af(128, 512) for _ in range(2)], t1B=bufs(2))
        return t

    def ln_apply(self, x, xB, C, T, gcol, bcol, out, outB, tmp, func=AF.Identity):
        for c in range(C):
            t1, t1B = tmp["t1"][c % 2], tmp["t1B"][c % 2]
            self.tt("vector", t1[:, :T], x[:, c, :T], tmp["rstd"][:, :T], ALU.mult, [xB[c], tmp["rsB"]], W=[t1B])
            self.tt("gpsimd", t1[:, :T], t1[:, :T], tmp["nmr"][:, :T], ALU.add, [t1B, tmp["nmB"]], W=[t1B])
            self.act(out[:, c, :T], t1[:, :T], func, [t1B, self.vecB], W=[outB[c]], bias=bcol(c), scale=gcol(c))

    def resid_ln_store(self, l, s, grp, r, rB, tmp):
        col0, T, w = grp
        V = self.vec[l]
        self.stats([(r[:, c, :T], rB[c]) for c in range(8)], T, 1024.0, 0, True, tmp)
        self.ln_apply(r, rB, 8, T, lambda c: V["lng"][:, s * 8 + c:s * 8 + c + 1], lambda c: V["lnb"][:, s * 8 + c:s * 8 + c + 1], r, rB, tmp)
        self.store_r(r, rB, grp)

    def epilogue(self, psi, m, T, r, rB, gwcol, ytmp, bias=None):
        y, yB = ytmp
        self.act(y[:, :T], self.PS[psi][:, :T], AF.Identity if bias is not None else AF.Copy, [self.PB[psi], self.vecB], W=[yB], scale=gwcol,
                 **({"bias": bias} if bias is not None else {}))
        self.stt("vector", r[:, m, :T], r[:, m, :T], ALPHA, y[:, :T], ALU.mult, ALU.add, [rB[m], yB], W=[rB[m]])

    def st_PRO(self):
        x = self.din("x", [self.nlat, 1024])
        ctx = self.din("ctx", [NCTX, 1024])
        res = self.res_ap(True)
        tin = [self.af(128, 4, 1024) for _ in range(2)]
        tinB = bufs(2)
        tout = [self.af(128, 8, 512) for _ in range(2)]
        toutB = bufs(2)
        pc = 0
        for gi, grp in enumerate(list(self.groups_lat) + [self.group_ctx]):
            col0, T, w = grp
            nt = T // 128
            b = gi % 2
            src = x[col0:col0 + T, :] if w == 0 else ctx
            self.ld(tin[b][:, 0:nt, :], src.rearrange("(t p) d -> p t d", p=128), (), W=[tinB[b]])
            first = True
            for t in range(nt):
                for hh in range(2):
                    psi = pc % 8
                    pc += 1
                    for q in range(4):
                        c = hh * 4 + q
                        self.mm(self.PS[psi][:, q * 128:(q + 1) * 128], tin[b][:, t, c * 128:(c + 1) * 128], self.ident, True, True, [tinB[b], self.Bc],
                                **({"W": [self.PB[psi]]} if q == 0 else {"A": [self.PB[psi]]}))
                    self.cp("vector" if hh == 0 else "scalar", tout[b][:, hh * 4:(hh + 1) * 4, t * 128:(t + 1) * 128],
                            self.PS[psi].rearrange("p (q t) -> p q t", q=4), [self.PB[psi]], **({"W": [toutB[b]]} if first else {"A": [toutB[b]]}))
                    first = False
            self.res_written.add(col0)
            self.st(res[:, :, col0:col0 + T], tout[b][:, :, :T], [toutB[b]], W=[self.res_buf(col0)])

    def st_EPI(self):
        out = self.dout("out", [self.nlat, 1024])
        tin = [self.af(128, 8, 512) for _ in range(2)]
        tinB = bufs(2)
        tout = [self.af(128, 1024) for _ in range(4)]
        toutB = bufs(4)
        self.outB = Buf("out")
        pc = 0
        tc = 0
        for gi, grp in enumerate(self.groups_lat):
            col0, T, w = grp
            b = gi % 2
            self.ld(tin[b], self.res_src(col0)[:, :, col0:col0 + T], [self.res_buf(col0)], W=[tinB[b]])
            for t in range(T // 128):
                ob = tc % 4
                tc += 1
                for hh in range(2):
                    psi = pc % 8
                    pc += 1
                    for q in range(4):
                        c = hh * 4 + q
                        self.mm(self.PS[psi][:, q * 128:(q + 1) * 128], tin[b][:, c, t * 128:(t + 1) * 128], self.ident, True, True, [tinB[b], self.Bc],
                                **({"W": [self.PB[psi]]} if q == 0 else {"A": [self.PB[psi]]}))
                    self.cp("vector" if hh == 0 else "scalar", tout[ob][:, hh * 512:(hh + 1) * 512], self.PS[psi], [self.PB[psi]],
                            **({"W": [toutB[ob]]} if hh == 0 else {"A": [toutB[ob]]}))
                self.st(out[col0 + t * 128:col0 + (t + 1) * 128, :], tout[ob], [toutB[ob]], A=[self.outB])

    def st_F(self, l, s, with_ctx):
        i = 0 if s == 0 else 1
        self.need_layer(l)
        self.need_w(("ffn", l, i))
        V = self.vec[l]
        d13 = self.dram["w13s_%d_%d" % (l, i)]
        d2 = self.dram["w2s_%d_%d" % (l, i)]
        wB = self.sbuf(("w", ("ffn", l, i)))
        W2 = self.ab(128, NJ, 1024)
        W2B = Buf()

        def load_W2():
            for q in range(2):
                self.ld(W2[:, q * 11:(q + 1) * 11, :], d2[:, q * 11 * 1024:(q + 1) * 11 * 1024].rearrange("p (j m) -> p j m", j=11), [wB],
                        **({"W": [W2B]} if q == 0 else {"A": [W2B]}))
        w13 = [self.ab(128, 8, 256) for _ in range(2)]
        w13B = bufs(2)
        S = 2
        r = [self.af(128, 8, 512) for _ in range(3)]
        rB = [bufs(8) for _ in range(3)]
        h = [self.ab(128, 8, 512) for _ in range(S)]
        hB = [bufs(8) for _ in range(S)]
        actt = [self.ab(128, NJ, 512) for _ in range(S)]
        actB = [Buf() for _ in range(S)]
        sg = [self.af(128, 512) for _ in range(2)]
        sgB = bufs(2)
        ytmp = [(self.af(128, 512), Buf()) for _ in range(2)]
        tmp = self.stats_tmp(8, 0, 1, xb=(h[0], hB[0]), sq=(h[1], hB[1]))
        groups = list(self.groups_lat)
        passes = [groups[a:a + S] for a in range(0, len(groups), S)]
        if with_ctx:
            passes.append([self.group_ctx])
        ridx = {}
        for pi, ps_ in enumerate(passes):
            if pi == 0:
                for si in range(len(ps_)):
                    ridx[(pi, si)] = si
            else:
                used = [ridx[(pi - 1, si)] for si in range(len(passes[pi - 1]))]
                free = [x for x in range(3) if x not in used]
                ridx[(pi, 0)] = free[0]
                if len(ps_) > 1:
                    ridx[(pi, 1)] = ridx[(pi - 1, 0)]

        def ln_stats(grp, ri):
            col0, T, w = grp
            self.stats([(r[ri][:, c, :T], rB[ri][c]) for c in range(8)], T, 1024.0, 0, True, tmp)

        def ln_apply_store(grp, ri):
            col0, T, w = grp
            self.ln_apply(r[ri], rB[ri], 8, T, lambda c: V["lng"][:, s * 8 + c:s * 8 + c + 1], lambda c: V["lnb"][:, s * 8 + c:s * 8 + c + 1],
                          r[ri], rB[ri], tmp)
            self.store_r(r[ri], rB[ri], grp)

        for si, grp in enumerate(passes[0]):
            self.load_r(r[ridx[(0, si)]], rB[ridx[(0, si)]], grp)
            self.modulate(l, s, grp, r[ridx[(0, si)]], rB[ridx[(0, si)]], h[si], hB[si])
        cnt = 0
        for pi, ps_ in enumerate(passes):
            nxt = passes[pi + 1] if pi + 1 < len(passes) else []
            for j in range(NJ):
                wb = j % 2
                self.ld(w13[wb], d13[j].rearrange("p (k c) -> p k c", k=8), [wB], W=[w13B[wb]])
                if pi == 0 and j == 1:
                    load_W2()
                for si, grp in enumerate(ps_):
                    col0, T, w = grp
                    pg = (cnt % 3) * 2
                    pu = pg + 1
                    cnt += 1
                    self.mmacc(pg, self.PS[pg][:, :T], [(w13[wb][:, k, 0:128], h[si][:, k, :T]) for k in range(8)], [w13B[wb]] + hB[si])
                    self.mmacc(pu, self.PS[pu][:, :T], [(w13[wb][:, k, 128:256], h[si][:, k, :T]) for k in range(8)], [w13B[wb]] + hB[si])
                    sgi = cnt % 2
                    self.act(sg[sgi][:, :T], self.PS[pg][:, :T], AF.Silu, [self.PB[pg]], W=[sgB[sgi]])
                    self.tt("vector", actt[si][:, j, :T], self.PS[pu][:, :T], sg[sgi][:, :T], ALU.mult, [self.PB[pu], sgB[sgi]],
                            **({"W": [actB[si]]} if j == 0 else {"A": [actB[si]]}))
            for si, grp in enumerate(ps_):
                col0, T, w = grp
                ri = ridx[(pi, si)]
                for m in range(8):
                    py = 6 + (m % 2)
                    self.mmacc(py, self.PS[py][:, :T], [(W2[:, j, m * 128:(m + 1) * 128], actt[si][:, j, :T]) for j in range(NJ)], [W2B, actB[si]])
                    self.epilogue(py, m, T, r[ri], rB[ri], self.coef(l, "gw", s, w, m), ytmp[m % 2])
                    if si > 0 and m == 3:
                        pr = ridx[(pi, si - 1)]
                        ln_stats(ps_[si - 1], pr)
                        ln_apply_store(ps_[si - 1], pr)
                        if len(nxt) > 1:
                            rn = ridx[(pi + 1, 1)]
                            self.load_r(r[rn], rB[rn], nxt[1])
                if si == 0 and nxt:
                    rn = ridx[(pi + 1, 0)]
                    self.load_r(r[rn], rB[rn], nxt[0])
            last = len(ps_) - 1
            rl = ridx[(pi, last)]
            ln_stats(ps_[last], rl)
            for si, grp in enumerate(nxt):
                rn = ridx[(pi + 1, si)]
                if len(ps_) == 1 and si == 1:
                    self.load_r(r[rn], rB[rn], grp)
                self.modulate(l, s, grp, r[rn], rB[rn], h[si], hB[si])
            ln_apply_store(ps_[last], rl)

    def st_E2(self, l, with_ctx, ctx_out):
        self.need_layer(l)
        self.need_w(("even", l))
        V = self.vec[l]
        wB = self.sbuf(("w", ("even", l)))
        N = self.N
        WIN = self.ab(128, 8, WIN_COLS)
        WQ = self.ab(128, 3, 1024)
        WB_ = Buf()
        self.ld(WIN, self.dram["wins%d" % l].rearrange("p (k c) -> p k c", k=8), [wB], W=[WB_])
        self.ld(WQ, self.dram["wqs%d" % l].rearrange("p (k c) -> p k c", k=3), [wB], A=[WB_])
        ropec = self.din("ropec", [128, self.nlat])
        ropes = self.din("ropes", [128, self.nlat])
        QTN = self.state("QTN", [512, N], BF16, True).rearrange("(c p) n -> p c n", p=128)
        QTR = self.state("QTR", [256, N], BF16, True).rearrange("(c p) n -> p c n", p=128)
        BL = self.state("BL", [512, N], BF16, True).rearrange("(c p) n -> p c n", p=128)
        CKV = self.state("CKV", [256, N], BF16, True).rearrange("(c p) n -> p c n", p=128)
        KR = self.state("KR", [32, N], BF16, True)
        r2 = [self.af(128, 8, 512) for _ in range(2)]
        rB2 = [bufs(8) for _ in range(2)]
        h2 = [self.ab(128, 8, 512) for _ in range(2)]
        hB2 = [bufs(8) for _ in range(2)]
        cosT = self.af(128, 512)
        sinT = self.af(128, 512)
        rpB = Buf()
        cq = self.af(128, 3, 512)
        cqB = bufs(3)
        cqn = self.ab(128, 3, 512)
        cqnB = bufs(3)
        ckv = self.af(128, 2, 512)
        ckvB = bufs(2)
        ckvn = self.ab(128, 2, 512)
        ckvnB = bufs(2)
        qn = self.ab(128, 4, 512)
        qnB = Buf()
        qr = self.ab(128, 2, 512)
        qrB = Buf()
        krt = self.ab(128, 512)
        krB = Buf()
        u = self.af(128, 4, 512)
        uB = bufs(4)
        vg = [self.af(128, 512) for _ in range(2)]
        vgB = bufs(2)
        vb = [self.ab(128, 512) for _ in range(2)]
        vbB = bufs(2)
        bnst2 = [self.af(128, 8) for _ in range(2)]
        bnB2 = bufs(2)
        mx2 = [self.af(128, 512) for _ in range(2)]
        mxB2 = bufs(2)
        bl = self.ab(128, 4, 512)
        blB = Buf()
        t1 = [self.af(128, 512) for _ in range(2)]
        t1B = bufs(2)
        tmp = self.stats_tmp(3, 0, 1)
        groups = list(self.groups_lat) + ([self.group_ctx] if with_ctx else [])
        pc = 0

        def nextps():
            nonlocal pc
            v = 2 + (pc % 6)
            pc += 1
            return v

        def x_after_group(gi):
            col0, T, w = groups[gi]
            cw = min(self.nlat, 1024)
            self.x_flush()
            if w == 1:
                self.x_even_ctx(l)
            elif (col0 + T) % cw == 0:
                self.x_even_chunk(l, (col0 + T) // cw - 1)
            if gi == len(groups) - 1:
                self.x_flush()
                self.x_done.add(l)

        self.load_r(r2[0], rB2[0], groups[0])
        self.modulate(l, 1, groups[0], r2[0], rB2[0], h2[0], hB2[0])
        for gi, grp in enumerate(groups):
            col0, T, w = grp
            full = (w == 0) or ctx_out
            h, hB = h2[gi % 2], hB2[gi % 2]
            if gi + 1 < len(groups):
                nb = (gi + 1) % 2
                self.load_r(r2[nb], rB2[nb], groups[gi + 1])
                self.modulate(l, 1, groups[gi + 1], r2[nb], rB2[nb], h2[nb], hB2[nb])
            if w == 0:
                self.ld(cosT[:, :T], ropec[:, col0:col0 + T], (), W=[rpB])
                self.ld(sinT[:, :T], ropes[:, col0:col0 + T], (), A=[rpB])
            hR = [WB_] + hB
            for c in range(2):
                psi = nextps()
                self.mmacc(psi, self.PS[psi][:, :T], [(WIN[:, k, 384 + c * 128:384 + (c + 1) * 128], h[:, k, :T]) for k in range(8)], hR)
                self.cp("scalar", ckv[:, c, :T], self.PS[psi][:, :T], [self.PB[psi]], W=[ckvB[c]])
            if full:
                for c in range(3):
                    psi = nextps()
                    self.mmacc(psi, self.PS[psi][:, :T], [(WIN[:, k, c * 128:(c + 1) * 128], h[:, k, :T]) for k in range(8)], hR)
                    self.cp("scalar", cq[:, c, :T], self.PS[psi][:, :T], [self.PB[psi]], W=[cqB[c]])
            pa = nextps()
            self.mmacc(pa, self.PS[pa][0:32, :T], [(WIN[:, k, 640:672], h[:, k, :T]) for k in range(8)], hR)
            if w == 0:
                pb = nextps()
                self.mmacc(pb, self.PS[pb][0:32, :T], [(WIN[:, k, 672:704], h[:, k, :T]) for k in range(8)], hR)
                self.tt("vector", t1[0][0:32, :T], self.PS[pa][0:32, :T], cosT[0:32, :T], ALU.mult, [self.PB[pa], rpB], W=[t1B[0]])
                self.tt("vector", t1[1][0:32, :T], self.PS[pb][0:32, :T], sinT[0:32, :T], ALU.mult, [self.PB[pb], rpB], W=[t1B[1]])
                self.tt("gpsimd", krt[0:32, :T], t1[0][0:32, :T], t1[1][0:32, :T], ALU.add, [t1B[0], t1B[1]], W=[krB])
            else:
                self.cp("scalar", krt[0:32, :T], self.PS[pa][0:32, :T], [self.PB[pa]], W=[krB])
            self.st(KR[:, col0:col0 + T], krt[0:32, :T], [krB], W=[self.sbuf(("KR", col0))])
            if full:
                for c in range(4):
                    psi = nextps()
                    self.mmacc(psi, self.PS[psi][:, :T], [(WIN[:, k, 704 + c * 128:704 + (c + 1) * 128], h[:, k, :T]) for k in range(8)], hR)
                    self.act(u[:, c, :T], self.PS[psi][:, :T], AF.Gelu, [self.PB[psi]], W=[uB[c]])
            self.stats([(ckv[:, c, :T], ckvB[c]) for c in range(2)], T, 256.0, 1, False, tmp)
            for c in range(2):
                self.tt("vector", t1[c][:, :T], ckv[:, c, :T], tmp["rstd"][:, :T], ALU.mult, [ckvB[c], tmp["rsB"]], W=[t1B[c]])
                self.act(ckvn[:, c, :T], t1[c][:, :T], AF.Copy, [t1B[c], self.vecB], W=[ckvnB[c]], scale=V["kvng"][:, c:c + 1])
            self.st(CKV[:, :, col0:col0 + T], ckvn[:, :, :T], ckvnB, W=[self.sbuf(("CKV", col0))])
            if not full:
                if self.fused and X_OVERLAP:
                    x_after_group(gi)
                continue
            self.stats([(cq[:, c, :T], cqB[c]) for c in range(3)], T, 384.0, 1, False, tmp)
            for c in range(3):
                self.tt("vector", t1[c % 2][:, :T], cq[:, c, :T], tmp["rstd"][:, :T], ALU.mult, [cqB[c], tmp["rsB"]], W=[t1B[c % 2]])
                self.act(cqn[:, c, :T], t1[c % 2][:, :T], AF.Copy, [t1B[c % 2], self.vecB], W=[cqnB[c]], scale=V["qng"][:, c:c + 1])

            def q_part():
                qR = [WB_] + cqnB
                for c in range(4):
                    psi = nextps()
                    self.mmacc(psi, self.PS[psi][:, :T], [(WQ[:, k, c * 128:(c + 1) * 128], cqn[:, k, :T]) for k in range(3)], qR)
                    self.cp("scalar", qn[:, c, :T], self.PS[psi][:, :T], [self.PB[psi]], **({"W": [qnB]} if c == 0 else {"A": [qnB]}))
                self.st(QTN[:, :, col0:col0 + T], qn[:, :, :T], [qnB], W=[self.sbuf(("QTN", col0))])
                for c in range(2):
                    pa = nextps()
                    self.mmacc(pa, self.PS[pa][:, :T], [(WQ[:, k, 512 + c * 128:512 + (c + 1) * 128], cqn[:, k, :T]) for k in range(3)], qR)
                    wa = {"W": [qrB]} if c == 0 else {"A": [qrB]}
                    if w == 0:
                        pb = nextps()
                        self.mmacc(pb, self.PS[pb][:, :T], [(WQ[:, k, 768 + c * 128:768 + (c + 1) * 128], cqn[:, k, :T]) for k in range(3)], qR)
                        self.tt("vector", t1[0][:, :T], self.PS[pa][:, :T], cosT[:, :T], ALU.mult, [self.PB[pa], rpB], W=[t1B[0]])
                        self.tt("vector", t1[1][:, :T], self.PS[pb][:, :T], sinT[:, :T], ALU.mult, [self.PB[pb], rpB], W=[t1B[1]])
                        self.tt("gpsimd", qr[:, c, :T], t1[0][:, :T], t1[1][:, :T], ALU.add, [t1B[0], t1B[1]], **wa)
                    else:
                        self.cp("scalar", qr[:, c, :T], self.PS[pa][:, :T], [self.PB[pa]], **wa)
                self.st(QTR[:, :, col0:col0 + T], qr[:, :, :T], [qrB], W=[self.sbuf(("QTR", col0))])

            nsub = T // 128
            for ci in range(nsub):
                if ci == nsub // 2:
                    q_part()
                tk = slice(ci * 128, (ci + 1) * 128)
                b2 = ci % 2
                bn, bnB_, mx_, mxB_ = bnst2[b2], bnB2[b2], mx2[b2], mxB2[b2]
                psi = nextps()
                self.mmacc(psi, self.PS[psi], [(h[:, k, tk], WIN[:, k, 1216:1728]) for k in range(8)], hR)
                self.act(vg[b2], self.PS[psi], AF.Gelu, [self.PB[psi]], W=[vgB[b2]])
                self.p.op("vector", lambda e, b2=b2, bn=bn: e.bn_stats(out=bn[:, 0:6], in_=vg[b2]), [vgB[b2]], [bnB_])
                self.p.op("vector", lambda e, bn=bn: e.bn_aggr(out=bn[:, 6:8], in_=bn[:, 0:6]), [bnB_], [bnB_])
                self.act(bn[:, 7:8], bn[:, 7:8], AF.Sqrt, [bnB_], W=[bnB_], bias=self.eps[:, 0:1])
                self.p.op("vector", lambda e, bn=bn: e.reciprocal(out=bn[:, 7:8], in_=bn[:, 7:8]), [bnB_], [bnB_])
                self.ts("vector", vg[b2], vg[b2], bn[:, 6:7], bn[:, 7:8], ALU.subtract, ALU.mult, [vgB[b2], bnB_], W=[vgB[b2]])
                self.tt("gpsimd", vg[b2], vg[b2], V["sg"], ALU.mult, [vgB[b2], self.vecB], W=[vgB[b2]])
                self.tt("vector", vb[b2], vg[b2], V["sb"], ALU.add, [vgB[b2], self.vecB], W=[vbB[b2]])
                psm = nextps()
                for g in range(4):
                    self.mm(self.PS[psm][:, g * 128:(g + 1) * 128], vb[b2][:, g * 128:(g + 1) * 128], V["wst"][:, g * 128:(g + 1) * 128], True, True,
                            [vbB[b2], self.vecB], **({"W": [self.PB[psm]]} if g == 0 else {"A": [self.PB[psm]]}))
                self.tt("vector", mx_, self.PS[psm], V["bs"], ALU.add, [self.PB[psm], self.vecB], W=[mxB_])
                self.tt("gpsimd", bl[:, :, tk], u[:, :, tk], mx_.rearrange("p (g i) -> p g i", g=4), ALU.mult, uB + [mxB_],
                        **({"W": [blB]} if ci == 0 else {"A": [blB]}))
            self.st(BL[:, :, col0:col0 + T], bl[:, :, :T], [blB], W=[self.sbuf(("BL", col0))])
            if self.fused and X_OVERLAP:
                x_after_group(gi)

    def x_even_ctx(self, l):
        nlat, N, nk = self.nlat, self.N, self.nk
        CKV = self.state("CKV", [256, N], BF16, False)
        KR = self.state("KR", [32, N], BF16, False)
        CKVA = self.state("CKVA", [256, nk], BF16, True)
        KRA = self.state("KRA", [32, nk], BF16, True)
        Ba = self.sbuf(("KVA",))
        self.ld(CKVA[:, 0:NCTX], CKV[:, nlat:N], [self.sbuf(("CKV", nlat))], A=[Ba])
        self.ld(KRA[:, 0:NCTX], KR[:, nlat:N], [self.sbuf(("KR", nlat))], A=[Ba])

    def x_even_chunk(self, l, k):
        nlat, N, nk = self.nlat, self.N, self.nk
        pairs = [[2 * i, 2 * i + 1] for i in range(self.ncores // 2)]
        cw = min(nlat, 1024)
        CKV = self.state("CKV", [256, N], BF16, False)
        KR = self.state("KR", [32, N], BF16, False)
        CKVA = self.state("CKVA", [256, nk], BF16, True)
        KRA = self.state("KRA", [32, nk], BF16, True)
        Ba = self.sbuf(("KVA",))
        xin = self.dtmp("xin%d_%d" % (l, k), [288, cw], BF16)
        xout = self.dtmp("xout%d_%d" % (l, k), [576, cw], BF16)
        Bi, Bo = Buf(), Buf()
        srcs = [self.sbuf((nm, c0)) for nm in ("CKV", "KR") for c0 in range(k * cw, (k + 1) * cw, 512)]
        self.ld(xin[0:256, :], CKV[:, k * cw:(k + 1) * cw], srcs, W=[Bi])
        self.ld(xin[256:288, :], KR[:, k * cw:(k + 1) * cw], srcs, A=[Bi])
        self.p.coll(lambda e, xin=xin, xout=xout: e.collective_compute("AllGather", ALU.bypass, replica_groups=pairs, ins=[xin], outs=[xout]), [Bi], [Bo])

        def post():
            for r in range(2):
                c0 = NCTX + r * nlat + k * cw
                self.ld(CKVA[:, c0:c0 + cw], xout[r * 288:r * 288 + 256, :], [Bo], A=[Ba])
                self.ld(KRA[:, c0:c0 + cw], xout[r * 288 + 256:(r + 1) * 288, :], [Bo], A=[Ba])
        self.x_pending.append(post)

    def x_flush(self):
        while self.x_pending:
            self.x_pending.pop(0)()

    def st_X(self, l):
        nlat, N, nk = self.nlat, self.N, self.nk
        pairs = [[2 * i, 2 * i + 1] for i in range(self.ncores // 2)]
        if l in self.x_done:
            return
        if l % 2 == 0:
            self.x_even_ctx(l)
            for k in range(nlat // min(nlat, 1024)):
                self.x_even_chunk(l, k)
            self.x_flush()
        else:
            self.x_odd_pre(l)
            self.x_flush()

    def x_odd_pre(self, l):
        nlat, N, nk = self.nlat, self.N, self.nk
        pairs = [[2 * i, 2 * i + 1] for i in range(self.ncores // 2)]
        GLH = self.state("GLH", [1024, nlat + 30], BF16, False)
        GLC = self.state("GLC", [1024, NCTX + 30], BF16, False)
        xin = self.dtmp("xin%d" % l, [1024, 32], BF16)
        xout = self.dtmp("xout%d" % l, [2048, 32], BF16)
        hm = self.din("hmask", [128, 2])
        Bi, Bo, Ba = Buf(), Buf(), self.sbuf(("GLH",))
        srcs = [self.sbuf(("GL", 0)), self.sbuf(("GL", nlat - 512))]
        self.ld(xin[:, 0:15], GLH[:, 15:30], srcs, W=[Bi])
        self.ld(xin[:, 15:30], GLH[:, nlat:nlat + 15], srcs, A=[Bi])
        self.p.coll(lambda e: e.collective_compute("AllGather", ALU.bypass, replica_groups=pairs, ins=[xin], outs=[xout]), [Bi], [Bo])
        hl = self.ab(128, 8, 32)
        hmt = self.af(128, 2)
        h2 = self.ab(128, 8, 32)
        z = self.ab(128, 8, 16)

        def post():
            hB = Buf()
            self.ld(hmt, hm, (), W=[hB])
            xo = xout.rearrange("(r c p) n -> r p c n", r=2, p=128)
            self.ld(hl[:, :, 0:15], xo[0][:, :, 15:30], [Bo], A=[hB])
            self.ld(hl[:, :, 15:30], xo[1][:, :, 0:15], [Bo], A=[hB])
            h2B = Buf()
            self.ts1("vector", h2[:, :, 0:15], hl[:, :, 0:15], hmt[:, 0:1], ALU.mult, [hB], W=[h2B])
            self.ts1("vector", h2[:, :, 15:30], hl[:, :, 15:30], hmt[:, 1:2], ALU.mult, [hB], A=[h2B])
            zB = Buf()
            self.memset("vector", z, 0.0, [zB])
            GLHv = GLH.rearrange("(c p) n -> p c n", p=128)
            GLCv = GLC.rearrange("(c p) n -> p c n", p=128)
            self.st(GLHv[:, :, 0:15], h2[:, :, 0:15], [h2B], A=[Ba])
            self.st(GLHv[:, :, nlat + 15:nlat + 30], h2[:, :, 15:30], [h2B], A=[Ba])
            self.st(GLCv[:, :, 0:15], z[:, :, 0:15], [zB], A=[Ba])
            self.st(GLCv[:, :, NCTX + 15:NCTX + 30], z[:, :, 0:15], [zB], A=[Ba])
        self.x_pending.append(post)

    def st_E4(self, l, ctx_out):
        self.need_layer(l)
        self.need_w(("even", l))
        wB = self.sbuf(("w", ("even", l)))
        N, nk = self.N, self.nk
        nkt = nk // 128
        CKVA = self.state("CKVA", [256, nk], BF16, False).rearrange("(c p) n -> p c n", p=128)
        KRA = self.state("KRA", [32, nk], BF16, False)
        QTN = self.state("QTN", [512, N], BF16, False)
        QTR = self.state("QTR", [256, N], BF16, False)
        AT = self.state("AT", [512, N], BF16, True)
        KN = self.dtmp("KN%d" % l, [512, nk], BF16)
        Bin = self.sbuf(("KVA",))
        mark_b = self.f_off
        ck = self.ab(128, 2, nk)
        ck_end = self.f_off
        ckB = Buf()
        self.ld(ck, CKVA, [Bin], W=[ckB])
        WKV = self.ab(128, 2, 1024)
        WKVB = Buf()
        self.ld(WKV, self.dram["wkvs%d" % l].rearrange("p (k c) -> p k c", k=2), [wB], W=[WKVB])
        Vall = self.ab(128, 8, nkt, 65)
        VB = Buf()
        self.memset("gpsimd", Vall, 1.0, [VB])
        kst = [self.ab(128, 512) for _ in range(2)]
        kstB = bufs(2)
        KNB = Buf()
        cnt = 0
        for c in range(4):
            for k0 in range(0, nk, 512):
                kw = min(512, nk - k0)
                psi = cnt % 4
                b = cnt % 2
                cnt += 1
                self.mmacc(psi, self.PS[psi][:, :kw], [(WKV[:, k2, c * 128:(c + 1) * 128], ck[:, k2, k0:k0 + kw]) for k2 in range(2)], [WKVB, ckB])
                self.cp("scalar" if b else "vector", kst[b][:, :kw], self.PS[psi][:, :kw], [self.PB[psi]], W=[kstB[b]])
                self.st(KN[c * 128:(c + 1) * 128, k0:k0 + kw], kst[b][:, :kw], [kstB[b]], A=[KNB])
        for kt in range(nkt):
            psi = cnt % 4
            cnt += 1
            self.mmacc(psi, self.PS[psi], [(ck[:, k2, kt * 128:(kt + 1) * 128], WKV[:, k2, 512:1024]) for k2 in range(2)], [WKVB, ckB])
            self.cp("scalar" if kt % 2 else "vector", Vall[:, :, kt, 1:65], self.PS[psi].rearrange("p (h d) -> p h d", h=8), [self.PB[psi]], A=[VB])
        self.p.barrier()
        save = self.f_off
        self.f_off = mark_b
        KH = [self.ab(128, nk) for _ in range(2)]
        assert self.f_off <= ck_end
        self.f_off = save
        KHB = bufs(2)
        Q = [self.ab(128, 512) for _ in range(2)]
        QB = bufs(2)
        PT = [self.ab(128, 512) for _ in range(4)]
        PTB = bufs(4)
        rec = self.af(128, 512)
        recB = Buf()
        recb = self.af(128, 512)
        recbB = Buf()
        an = [self.ab(128, 512) for _ in range(2)]
        anB = bufs(2)
        qgroups = list(self.groups_lat) + ([self.group_ctx] if ctx_out else [])
        blocks = [(hd, grp) for hd in range(8) for grp in qgroups]
        steps = []
        for bi, (hd, grp) in enumerate(blocks):
            kts = list(range(nkt)) if grp[2] == 0 else list(range(NCTX // 128))
            for ii, kt in enumerate(kts):
                steps.append((bi, kt, ii, len(kts)))

        def load_K(hd):
            kb = hd % 2
            self.ld(KH[kb][0:64, :], KN[hd * 64:(hd + 1) * 64, :], [KNB], W=[KHB[kb]])
            self.ld(KH[kb][64:96, :], KRA, [Bin], A=[KHB[kb]])

        def load_Q(bi):
            hd, (col0, T, w) = blocks[bi]
            qb = bi % 2
            self.ld(Q[qb][0:64, :T], QTN[hd * 64:(hd + 1) * 64, col0:col0 + T], [self.sbuf(("QTN", col0))], W=[QB[qb]])
            self.ld(Q[qb][64:96, :T], QTR[hd * 32:(hd + 1) * 32, col0:col0 + T], [self.sbuf(("QTR", col0))], A=[QB[qb]])

        load_K(0)
        load_Q(0)
        SK = 3
        BG_PLAN = {0: [("ffn", 0, 1), ("ffn", 1, 0), ("odd", 1), ("ffn", 1, 1), ("ffn", 2, 0), ("even", 2)],
                   2: [("ffn", 2, 1), ("ffn", 3, 0), ("odd", 3), ("ffn", 3, 1)]}
        jobs = []
        for key in BG_PLAN.get(l, []):
            if key not in self.wdone:
                self.wdone.add(key)
                jobs += self.w_pieces(key)
        self.cvt_setup(3)
        jobs.reverse()

        pend = []

        def hook():
            if jobs:
                pend.append(self.cvt_emit(jobs.pop(), ldq="sync", stq="sync", engs=("gpsimd",), phase=0))
            if pend and (len(pend) > 1 or not jobs):
                self.cvt_emit(pend.pop(0), ldq="sync", stq="sync", engs=("gpsimd",), phase=1)
        ns = len(steps)
        for idx in range(ns + SK):
            if idx < ns:
                bi, kt, ii, nkk = steps[idx]
                hd, (col0, T, w) = blocks[bi]
                if ii == 0:
                    if bi + 1 < len(blocks):
                        if blocks[bi + 1][0] != hd:
                            load_K(hd + 1)
                        load_Q(bi + 1)
                psi = idx % 4
                self.mm(self.PS[psi][:, :T], KH[hd % 2][0:96, kt * 128:(kt + 1) * 128], Q[bi % 2][0:96, :T], True, True,
                        [KHB[hd % 2], QB[bi % 2]], W=[self.PB[psi]])
                self.act(PT[psi][:, :T], self.PS[psi][:, :T], AF.Exp, [self.PB[psi]], W=[PTB[psi]], scale=ATT_SCALE)
                if idx % 16 == 4:
                    hook()
            j = idx - SK
            if j < 0:
                continue
            bi, kt, ii, nkk = steps[j]
            hd, (col0, T, w) = blocks[bi]
            po = 4 + (bi % 2)
            pb = j % 4
            self.mm(self.PS[po][0:65, :T], Vall[:, hd, kt, :], PT[pb][:, :T], ii == 0, ii == nkk - 1, [VB, PTB[pb]],
                    **({"W": [self.PB[po]]} if ii == 0 else {"A": [self.PB[po]]}))
            if ii != nkk - 1:
                continue
            self.p.op("vector", lambda e, po=po, T=T: e.reciprocal(out=rec[0:1, :T], in_=self.PS[po][0:1, :T]), [self.PB[po]], [recB])
            self.mm(self.PS[6][0:65, :T], self.ones_f[0:1, 0:65], rec[0:1, :T], True, True, [recB], W=[self.PB[6]])
            self.cp("scalar", recb[0:65, :T], self.PS[6][0:65, :T], [self.PB[6]], W=[recbB])
            ab_ = bi % 2
            self.tt("vector", an[ab_][0:65, :T], self.PS[po][0:65, :T], recb[0:65, :T], ALU.mult, [self.PB[po], recbB], W=[anB[ab_]])
            self.p.dma("sync", AT[hd * 64:(hd + 1) * 64, col0:col0 + T], an[ab_][1:65, :T], [anB[ab_]], (), [self.sbuf(("AT", col0))])
        while jobs or pend:
            hook()

    def st_E5(self, l, ctx_out):
        self.need_layer(l)
        self.need_w(("even", l))
        wB = self.sbuf(("w", ("even", l)))
        N = self.N
        AT = self.state("AT", [512, N], BF16, False).rearrange("(c p) n -> p c n", p=128)
        BL = self.state("BL", [512, N], BF16, False).rearrange("(c p) n -> p c n", p=128)
        WO = self.ab(128, 8, 1024)
        WOB = Buf()
        self.ld(WO, self.dram["wos%d" % l].rearrange("p (k c) -> p k c", k=8), [wB], W=[WOB])
        r = [self.af(128, 8, 512) for _ in range(2)]
        rB = [bufs(8) for _ in range(2)]
        ab_ = [self.ab(128, 8, 512) for _ in range(2)]
        abB = bufs(2)
        ytmp = [(self.af(128, 512), Buf()) for _ in range(2)]
        tmp = self.stats_tmp(8, 0, 1)
        groups = list(self.groups_lat) + ([self.group_ctx] if ctx_out else [])
        for gi, grp in enumerate(groups):
            col0, T, w = grp
            b = gi % 2
            self.load_r(r[b], rB[b], grp)
            self.ld(ab_[b][:, 0:4, :T], AT[:, :, col0:col0 + T], [self.sbuf(("AT", col0))], W=[abB[b]])
            self.ld(ab_[b][:, 4:8, :T], BL[:, :, col0:col0 + T], [self.sbuf(("BL", col0))], A=[abB[b]])
            for m in range(8):
                py = 2 + (m % 4)
                self.mmacc(py, self.PS[py][:, :T], [(WO[:, k, m * 128:(m + 1) * 128], ab_[b][:, k, :T]) for k in range(8)], [WOB, abB[b]])
                self.epilogue(py, m, T, r[b], rB[b], self.coef(l, "gw", 1, w, m), ytmp[m % 2])
            self.resid_ln_store(l, 1, grp, r[b], rB[b], tmp)

    def st_O1(self, l, ctx_out):
        self.need_layer(l)
        self.need_w(("odd", l))
        V = self.vec[l]
        wB = self.sbuf(("w", ("odd", l)))
        N = self.N
        if self.fused:
            GLHw = self.state("GLH", [1024, self.nlat + 30], BF16, True).rearrange("(c p) n -> p c n", p=128)
            GLCw = self.state("GLC", [1024, NCTX + 30], BF16, True).rearrange("(c p) n -> p c n", p=128)
        else:
            GL = self.state("GL", [1024, N], BF16, True).rearrange("(c p) n -> p c n", p=128)
        WP = self.ab(128, 8, 2048)
        WPB = Buf()
        self.ld(WP, self.dram["wp1s%d" % l].rearrange("p (k c) -> p k c", k=8), [wB], W=[WPB])
        r = [self.af(128, 8, 512)] * 2
        rB = [bufs(8)] * 2
        h = [self.ab(128, 8, 512) for _ in range(2)]
        hB = [bufs(8) for _ in range(2)]
        gl = [self.ab(128, 8, 512) for _ in range(2)]
        glB = bufs(2)
        sg = [self.af(128, 512) for _ in range(2)]
        sgB = bufs(2)
        groups = list(self.groups_lat) + ([self.group_ctx] if ctx_out else [])
        cnt = 0
        for gi, grp in enumerate(groups):
            col0, T, w = grp
            b = gi % 2
            self.load_r(r[b], rB[b], grp)
            self.modulate(l, 1, grp, r[b], rB[b], h[b], hB[b])
            for m in range(8):
                pa = (cnt % 4) * 2
                pg = pa + 1
                cnt += 1
                self.mmacc(pa, self.PS[pa][:, :T], [(WP[:, k, m * 128:(m + 1) * 128], h[b][:, k, :T]) for k in range(8)], [WPB] + hB[b])
                self.mmacc(pg, self.PS[pg][:, :T], [(WP[:, k, 1024 + m * 128:1024 + (m + 1) * 128], h[b][:, k, :T]) for k in range(8)], [WPB] + hB[b])
                si = cnt % 2
                self.act(sg[si][:, :T], self.PS[pg][:, :T], AF.Sigmoid, [self.PB[pg], self.vecB], W=[sgB[si]], bias=V["bpw1"][:, 8 + m:9 + m])
                self.stt("vector", gl[b][:, m, :T], self.PS[pa][:, :T], V["bpw1"][:, m:m + 1], sg[si][:, :T], ALU.add, ALU.mult,
                         [self.PB[pa], sgB[si], self.vecB], **({"W": [glB[b]]} if m == 0 else {"A": [glB[b]]}))
            if not self.fused:
                dstv = GL[:, :, col0:col0 + T]
            elif w == 0:
                dstv = GLHw[:, :, 15 + col0:15 + col0 + T]
            else:
                dstv = GLCw[:, :, 15:15 + T]
            self.st(dstv, gl[b][:, :, :T], [glB[b]], W=[self.sbuf(("GL", col0))])
            if self.fused and X_OVERLAP and w == 0 and col0 + T == self.nlat:
                self.x_odd_pre(l)
        if self.fused and X_OVERLAP:
            self.x_flush()
            self.x_done.add(l)

    def st_O2(self, l, ctx_out):
        self.need_layer(l)
        self.need_w(("odd", l))
        V = self.vec[l]
        wB = self.sbuf(("w", ("odd", l)))
        GLH = self.state("GLH", [1024, self.nlat + 30], BF16, False).rearrange("(c p) n -> p c n", p=128)
        Bin = self.sbuf(("GLH",))
        WO = self.ab(128, 8, 1024)
        WOB = Buf()
        self.ld(WO, self.dram["wos%d" % l].rearrange("p (k c) -> p k c", k=8), [wB], W=[WOB])
        xh = [self.ab(128, 8, 542) for _ in range(2)]
        xhB = bufs(2)
        diag = self.ab(128, 248, 128)
        dB = Buf()
        for wc in range(248):
            self.ts1("vector", diag[:, wc, :], self.ident_b, V["wdw"][:, wc:wc + 1], ALU.mult, [self.vecB, self.Bc],
                     **({"W": [dB]} if wc == 0 else {"A": [dB]}))
        acc2 = [self.af(128, 8, 512) for _ in range(2)]
        accB2 = [bufs(8) for _ in range(2)]
        hs2 = [self.ab(128, 8, 512)] * 2
        hsB2 = [bufs(8)] * 2
        r2 = [self.af(128, 8, 512)] * 2
        rB2 = [bufs(8)] * 2
        NPE = 8
        ytmp = [(self.af(128, 512), Buf()) for _ in range(2)]
        tmp = self.stats_tmp(8, 0, 1, xb=(hs2[0], hsB2[0]))
        groups = list(self.groups_lat)
        if ctx_out:
            GLC = self.state("GLC", [1024, NCTX + 30], BF16, False).rearrange("(c p) n -> p c n", p=128)
            groups.append(self.group_ctx)
        for gi, grp in enumerate(groups):
            col0, T, w = grp
            b = gi % 2
            if w == 0:
                self.ld(xh[b][:, :, :T + 30], GLH[:, :, col0:col0 + T + 30], [Bin], W=[xhB[b]])
            else:
                self.ld(xh[b][:, :, :T + 30], GLC[:, :, 0:T + 30], [Bin], W=[xhB[b]])
            acc, accB, hs, hsB, r, rB = acc2[b], accB2[b], hs2[b], hsB2[b], r2[b], rB2[b]
            self.load_r(r, rB, grp)
            for c in range(NPE, 8):
                self.ts("vector", acc[:, c, :T], xh[b][:, c, 0:T], V["wdw"][:, c:c + 1], V["bdw"][:, c:c + 1], ALU.mult, ALU.add,
                        [xhB[b], self.vecB], W=[accB[c]])
                for wi in range(1, 31):
                    self.stt("vector", acc[:, c, :T], xh[b][:, c, wi:wi + T], V["wdw"][:, wi * 8 + c:wi * 8 + c + 1], acc[:, c, :T], ALU.mult, ALU.add,
                             [xhB[b], accB[c], self.vecB], W=[accB[c]])
            for c in range(NPE):
                psi = 2 + (c % 4)
                self.mmacc(psi, self.PS[psi][:, :T], [(diag[:, wi * 8 + c, :], xh[b][:, c, wi:wi + T]) for wi in range(31)], [dB, xhB[b]])
                self.act(acc[:, c, :T], self.PS[psi][:, :T], AF.Identity, [self.PB[psi], self.vecB], W=[accB[c]], bias=V["bdw"][:, c:c + 1])
            self.stats([(acc[:, c, :T], accB[c]) for c in range(8)], T, 1024.0, 0, True, tmp)
            self.ln_apply(acc, accB, 8, T, lambda c: V["olng"][:, c:c + 1], lambda c: V["olnb"][:, c:c + 1], hs, hsB, tmp, func=AF.Silu)
            for m in range(8):
                py = 2 + (m % 4)
                self.mmacc(py, self.PS[py][:, :T], [(WO[:, k, m * 128:(m + 1) * 128], hs[:, k, :T]) for k in range(8)], [WOB] + hsB)
                self.epilogue(py, m, T, r, rB, self.coef(l, "gw", 1, w, m), ytmp[m % 2], bias=V["bg"][:, w * 8 + m:w * 8 + m + 1])
            self.resid_ln_store(l, 1, grp, r, rB, tmp)


STATE_SHAPES = None


def rope_tables(nlat, half):
    t = np.arange(half * nlat, (half + 1) * nlat)
    rows = (t // GRID_W).astype(np.float32)
    cols = (t % GRID_W).astype(np.float32)
    hf = 16
    inv = (1.0 / (np.float32(10000.0) ** (np.arange(0, hf, 2, dtype=np.float32) / np.float32(hf)))).astype(np.float32)
    ang = np.concatenate([rows[:, None] * inv, cols[:, None] * inv], axis=-1).astype(np.float32)
    c = np.cos(ang).astype(np.float32)
    s = np.sin(ang).astype(np.float32)
    C = np.repeat(c, 2, axis=1)
    S = np.repeat(s, 2, axis=1)
    S[:, 0::2] *= -1.0
    C = np.tile(C, (1, 4)).T.copy()
    S = np.tile(S, (1, 4)).T.copy()
    return C, S


def core_inputs(inp, core, nlat, names):
    b, half = core // 2, core % 2
    f = lambda a: np.ascontiguousarray(a, dtype=np.float32)
    d = {}
    for n in names:
        if n == "ident":
            d[n] = np.eye(128, dtype=np.float32)
        elif n == "hmask":
            d[n] = np.tile(np.array([[half, 1 - half]], np.float32), (128, 1))
        elif n == "x":
            d[n] = f(inp["x"][b, half * nlat:(half + 1) * nlat, :])
        elif n == "ctx":
            d[n] = f(inp["ctx"][b])
        elif n == "cc":
            d[n] = f(np.stack([inp["c"][b], inp["c_ctx"]]).reshape(16, 128))
        elif n == "ropec" or n == "ropes":
            C, S = rope_tables(nlat, half)
            d[n] = C if n == "ropec" else S
        elif n.startswith("w_mod"):
            d[n] = f(inp["w_mod"][int(n[5:])])
        elif n.startswith("b_mod"):
            d[n] = f(inp["b_mod"][int(n[5:])][None, :])
        elif n.startswith("ln_g"):
            d[n] = f(inp["ln_g"][int(n[4:])].reshape(24, 128))
        elif n.startswith("ln_b"):
            d[n] = f(inp["ln_b"][int(n[4:])].reshape(24, 128))
        elif n.startswith("w13_"):
            l, i = int(n[4]), int(n[6])
            d[n] = f(inp["ffn_w13"][l, i])
        elif n.startswith("w2_"):
            l, i = int(n[3]), int(n[5])
            d[n] = f(inp["ffn_w2"][l, i])
        elif n.startswith("w_in"):
            d[n] = f(inp["e_w_in"][int(n[4:]) // 2])
        elif n.startswith("w_q"):
            d[n] = f(inp["e_w_q_up"][int(n[3:]) // 2])
        elif n.startswith("w_kv"):
            d[n] = f(inp["e_w_kv_up"][int(n[4:]) // 2])
        elif n.startswith("w_out"):
            l = int(n[5:])
            d[n] = f(inp["e_w_out"][l // 2] if l % 2 == 0 else inp["o_w_out"][l // 2])
        elif n.startswith("qn"):
            d[n] = f(inp["e_q_norm"][int(n[2:]) // 2].reshape(3, 128))
        elif n.startswith("kvn"):
            d[n] = f(inp["e_kv_norm"][int(n[3:]) // 2].reshape(2, 128))
        elif n.startswith("sgu_g"):
            d[n] = f(inp["e_sgu_g"][int(n[5:]) // 2][None, :])
        elif n.startswith("sgu_b"):
            d[n] = f(inp["e_sgu_b"][int(n[5:]) // 2][None, :])
        elif n.startswith("b_s"):
            d[n] = f(inp["e_b_s"][int(n[3:]) // 2].reshape(1, 512))
        elif n.startswith("w_s"):
            d[n] = f(inp["e_w_s"][int(n[3:]) // 2])
        elif n.startswith("w_pw1"):
            d[n] = f(inp["o_w_pw1"][int(n[5:]) // 2])
        elif n.startswith("b_pw1"):
            d[n] = f(inp["o_b_pw1"][int(n[5:]) // 2].reshape(16, 128))
        elif n.startswith("w_dw"):
            d[n] = f(inp["o_w_dw"][int(n[4:]) // 2].reshape(248, 128))
        elif n.startswith("b_dw"):
            d[n] = f(inp["o_b_dw"][int(n[4:]) // 2].reshape(8, 128))
        elif n.startswith("o_ln_g"):
            d[n] = f(inp["o_ln_g"][int(n[6:]) // 2].reshape(8, 128))
        elif n.startswith("o_ln_b"):
            d[n] = f(inp["o_ln_b"][int(n[6:]) // 2].reshape(8, 128))
        elif n.startswith("b_out"):
            d[n] = f(inp["o_b_out"][int(n[5:]) // 2].reshape(8, 128))
        else:
            raise KeyError(n)
    return d


SEG_STATE = [
    ((), ("RES", "QTN", "QTR", "BL", "CKV", "KR")),
    (("RES", "QTN", "QTR", "BL", "CKVA", "KRA"), ("RES", "GL")),
    (("RES", "GLH", "GLC"), ("RES", "QTN", "QTR", "BL", "CKV", "KR")),
    (("RES", "QTN", "QTR", "BL", "CKVA", "KRA"), ("RES", "GL")),
    (("RES", "GLH"), ()),
]

_prog_cache = {}


def get_prog(nlat, si):
    key = (nlat, si)
    if key not in _prog_cache:
        segs = segments()
        b = Builder(nlat, segs[si], SEG_STATE[si][0], SEG_STATE[si][1])
        b.build()
        _prog_cache[key] = b
    return _prog_cache[key]


def exchange(si, states, nlat, ncores):
    new = [dict() for _ in range(ncores)]
    for c in range(ncores):
        st = states[c]
        mate = states[c ^ 1]
        half = c % 2
        lo, hi = (st, mate) if half == 0 else (mate, st)
        n = new[c]
        n["RES_i"] = st["RES_o"]
        if "CKV_o" in st:
            for nm in ("QTN", "QTR", "BL"):
                n[nm + "_i"] = st[nm + "_o"]
            n["CKVA_i"] = np.concatenate([st["CKV_o"][:, nlat:], lo["CKV_o"][:, :nlat], hi["CKV_o"][:, :nlat]], axis=1)
            n["KRA_i"] = np.concatenate([st["KR_o"][:, nlat:], lo["KR_o"][:, :nlat], hi["KR_o"][:, :nlat]], axis=1)
        if "GL_o" in st:
            gl = st["GL_o"]
            z = np.zeros((1024, 15), gl.dtype)
            left = z if half == 0 else mate["GL_o"][:, nlat - 15:nlat]
            right = mate["GL_o"][:, 0:15] if half == 0 else z
            n["GLH_i"] = np.concatenate([left, gl[:, :nlat], right], axis=1)
            n["GLC_i"] = np.concatenate([z, gl[:, nlat:], z], axis=1)
    return new


def run_all(inp, nlat, ncores, seg_range=None, debug=None):
    states = [dict() for _ in range(ncores)]
    nseg = len(SEG_STATE)
    rng = range(nseg) if seg_range is None else seg_range
    for si in rng:
        b = get_prog(nlat, si)
        in_maps = []
        for c in range(ncores):
            names = [n for n in b.inputs if not n.endswith("_i")]
            d = core_inputs(inp, c, nlat, names)
            for n in b.inputs:
                if n.endswith("_i"):
                    d[n] = np.ascontiguousarray(states[c][n])
            in_maps.append(d)
        res = run_bass_kernel_spmd(b.nc, in_maps, core_ids=list(range(ncores)))
        outs = res.results
        if debug is not None:
            debug.append(outs)
        if si == nseg - 1:
            return outs
        states = exchange(si, outs, nlat, ncores)
    return states


def get_fused(nlat, ncores=8):
    key = (nlat, "fused", ncores)
    if key not in _prog_cache:
        b = Builder(nlat, stage_list(), (), (), fused=True, ncores=ncores)
        b.build()
        _prog_cache[key] = b
    return _prog_cache[key]


def run_fused(inp, nlat, ncores):
    b = get_fused(nlat, ncores)
    in_maps = [core_inputs(inp, c, nlat, b.inputs) for c in range(ncores)]
    res = run_bass_kernel_spmd(b.nc, in_maps, core_ids=list(range(ncores)))
    return res.results


FUSED = True


def kernel(**inputs):
    inp = {k: np.asarray(v) for k, v in inputs.items()}
    B, S, _ = inp["x"].shape
    nlat = S // 2
    ncores = 2 * B
    outs = run_fused(inp, nlat, ncores) if FUSED else run_all(inp, nlat, ncores)
    out = np.empty((B, S, D), np.float32)
    for c in range(ncores):
        out[c // 2, (c % 2) * nlat:(c % 2 + 1) * nlat, :] = outs[c]["out"]
    return out
```

```python
import math
import contextlib
import numpy as np
import concourse.bass as bass
import concourse.mybir as mybir
from concourse.bass_utils import run_bass_kernel_spmd

F32 = mybir.dt.float32
BF16 = mybir.dt.bfloat16
AF = mybir.ActivationFunctionType
ALU = mybir.AluOpType
ENGS = ["sync", "scalar", "vector", "gpsimd", "tensor"]
NDMA = 56
NDMA_HW = 36

D = 1024
NCTX = 256
DFF = 2816
NJ = 22
ALPHA = 8.0 ** 0.25
LN_EPS = 1e-5
RMS_EPS = 1e-6
ATT_SCALE = 1.0 / math.sqrt(96.0)
GRID_W = 64
WIN_COLS = 1728
X_OVERLAP = True


class Buf:
    __slots__ = ("name", "w", "r", "p")

    def __init__(self, name=""):
        self.name = name
        self.w = {}
        self.r = {}
        self.p = {}


def bufs(n, name=""):
    return [Buf("%s%d" % (name, i)) for i in range(n)]


class Op:
    __slots__ = ("eng", "fn", "waits", "flag", "seq", "semval", "dma", "cc")

    def __init__(self, eng, fn):
        self.eng = eng
        self.fn = fn
        self.waits = []
        self.flag = False
        self.semval = 0
        self.dma = None
        self.cc = None


class Prog:
    def __init__(self, nc):
        self.nc = nc
        self.streams = {e: [] for e in ENGS}
        self.waited = {e: {} for e in ENGS}
        self.dma_val = [0] * NDMA
        self.dma_rr = {"sync": 0, "gpsimd": 0}
        self.dma_pool = {"sync": list(range(0, NDMA_HW)), "gpsimd": list(range(NDMA_HW, NDMA))}
        self.ncc = 0

    @staticmethod
    def _merge(dst, src):
        for k, v in src.items():
            if dst.get(k, -1) < v:
                dst[k] = v

    def _finish(self, op, deps, reads, writes, adds, ev_key, ev_val):
        eng = op.eng
        wd = self.waited[eng]
        for k, v in deps.items():
            if k[0] == "e" and k[1] == eng and eng in ("tensor", "sync"):
                continue
            if wd.get(k, -1) >= v:
                continue
            wd[k] = v
            op.waits.append((k, v))
            if k[0] == "e":
                self.streams[k[1]][v].flag = True
        for b in reads:
            if b.r.get(ev_key, -1) < ev_val:
                b.r[ev_key] = ev_val
        for b in writes:
            pg = dict(b.w)
            self._merge(pg, b.r)
            b.p = pg
            b.w = {ev_key: ev_val}
            b.r = {}
        for b in adds:
            if b.w.get(ev_key, -1) < ev_val:
                b.w[ev_key] = ev_val

    def _deps(self, reads, writes, adds):
        deps = {}
        for b in reads:
            self._merge(deps, b.w)
        for b in writes:
            self._merge(deps, b.w)
            self._merge(deps, b.r)
        for b in adds:
            self._merge(deps, b.r)
            self._merge(deps, b.p)
        return deps

    def op(self, eng, fn, reads=(), writes=(), adds=()):
        o = Op(eng, fn)
        st = self.streams[eng]
        o.seq = len(st)
        deps = self._deps(reads, writes, adds)
        st.append(o)
        self._finish(o, deps, reads, writes, adds, ("e", eng), o.seq)
        return o

    def dma(self, q, out, in_, reads=(), writes=(), adds=()):
        pool = self.dma_pool[q]
        s = pool[self.dma_rr[q] % len(pool)]
        self.dma_rr[q] += 1
        o = Op(q, lambda e: e.dma_start(out=out, in_=in_))
        st = self.streams[q]
        o.seq = len(st)
        deps = self._deps(reads, writes, adds)
        prev = self.dma_val[s]
        if prev > 0:
            k = ("d", s)
            if deps.get(k, -1) < prev:
                deps[k] = prev
        self.dma_val[s] = prev + 16
        o.dma = s
        st.append(o)
        self._finish(o, deps, reads, writes, adds, ("d", s), prev + 16)
        return o

    def coll(self, fn, reads=(), writes=()):
        idx = self.ncc
        self.ncc += 1
        o = Op("gpsimd", fn)
        st = self.streams["gpsimd"]
        o.seq = len(st)
        deps = self._deps(reads, writes, ())
        o.cc = idx
        o.dma = -1
        st.append(o)
        self._finish(o, deps, reads, writes, (), ("c", idx), 1)
        return o

    def barrier(self):
        deps = {}
        for e in ENGS:
            n = len(self.streams[e])
            if n and e != "sync":
                for i in range(n - 1, -1, -1):
                    if self.streams[e][i].dma is None and self.streams[e][i].fn is not None:
                        deps[("e", e)] = i
                        break
        for s in range(NDMA):
            if self.dma_val[s] > 0:
                deps[("d", s)] = self.dma_val[s]
        for i in range(self.ncc):
            deps[("c", i)] = 1
        for e in ENGS:
            o = Op(e, None)
            o.seq = len(self.streams[e])
            self.streams[e].append(o)
            self._finish(o, dict(deps), (), (), (), ("e", e), o.seq)

    def emit(self):
        nc = self.nc
        self.barrier()
        for e in ENGS:
            c = 0
            for o in self.streams[e]:
                if o.flag and o.dma is None:
                    c += 1
                    o.semval = c
        with contextlib.ExitStack() as es:
            esem = {e: es.enter_context(nc.semaphore("es_" + e)) for e in ENGS}
            dsem = [es.enter_context(nc.semaphore("ds_%d" % i)) for i in range(NDMA)]
            csem = [es.enter_context(nc.semaphore("cs_%d" % i)) for i in range(self.ncc)]
            block = es.enter_context(nc.Block())

            def run(engname):
                def body(eng):
                    for o in self.streams[engname]:
                        for k, v in o.waits:
                            if k[0] == "e":
                                eng.wait_ge(esem[k[1]], self.streams[k[1]][v].semval)
                            elif k[0] == "c":
                                eng.wait_ge(csem[k[1]], v)
                            else:
                                eng.wait_ge(dsem[k[1]], v)
                        if o.fn is None:
                            if o.flag:
                                eng.nop().then_inc(esem[engname], 1)
                            continue
                        ins = o.fn(eng)
                        if o.cc is not None:
                            ins.then_inc(csem[o.cc], 1)
                        elif o.dma is not None:
                            ins.then_inc(dsem[o.dma], 16)
                        elif o.flag:
                            ins.then_inc(esem[engname], 1)
                return body

            block.sync(run("sync"))
            block.scalar(run("scalar"))
            block.vector(run("vector"))
            block.gpsimd(run("gpsimd"))
            block.tensor(run("tensor"))


def stage_list():
    L = [("PRO",)]
    for l in range(4):
        ci = l <= 2
        co = l <= 1
        L.append(("F", l, 0, ci))
        if l % 2 == 0:
            L.append(("E2", l, ci, co))
            L.append(("X", l))
            L.append(("E4", l, co))
            L.append(("E5", l, co))
        else:
            L.append(("O1", l, co))
            L.append(("X", l))
            L.append(("O2", l, co))
        L.append(("F", l, 2, co))
    L.append(("EPI",))
    return L


def segments():
    segs = [[]]
    for st in stage_list():
        if st[0] == "X":
            segs.append([])
        else:
            segs[-1].append(st)
    return segs


class Builder:
    def __init__(self, nlat, stages, state_in, state_out, fused=False, ncores=8):
        self.fused = fused
        self.ncores = ncores
        self.nc = bass.Bass("TRN2", target_bir_lowering=False)
        self.p = Prog(self.nc)
        self.nlat = nlat
        self.N = nlat + NCTX
        self.nk = NCTX + 2 * nlat
        self.stages = stages
        self.state_in = set(state_in)
        self.state_out = set(state_out)
        self.dram = {}
        self.inputs = []
        self.outputs = []
        self.wdone = set()
        self.groups_lat = [(g * 512, 512, 0) for g in range(nlat // 512)]
        self.group_ctx = (nlat, NCTX, 1)
        self.res_written = set()
        self.x_done = set()
        self.x_pending = []

    def din(self, name, shape, dt=F32):
        if name not in self.dram:
            self.dram[name] = self.nc.dram_tensor(name, list(shape), dt, kind="ExternalInput").ap()
            self.inputs.append(name)
        return self.dram[name]

    def dtmp(self, name, shape, dt):
        if name not in self.dram:
            self.dram[name] = self.nc.dram_tensor(name, list(shape), dt).ap()
        return self.dram[name]

    def dout(self, name, shape, dt=F32):
        if name not in self.dram:
            self.dram[name] = self.nc.dram_tensor(name, list(shape), dt, kind="ExternalOutput").ap()
            self.outputs.append(name)
        return self.dram[name]

    def state(self, name, shape, dt, write):
        if write:
            if name in self.state_out:
                return self.dout(name + "_o", shape, dt)
            return self.dtmp(name + "_t", shape, dt)
        if (name + "_o") in self.dram:
            return self.dram[name + "_o"]
        if (name + "_t") in self.dram:
            return self.dram[name + "_t"]
        assert name in self.state_in, name
        return self.din(name + "_i", shape, dt)

    def sbuf(self, key):
        if key not in self.dbufs:
            self.dbufs[key] = Buf(str(key))
        return self.dbufs[key]

    def reset_arena(self):
        self.f_off = self.f_base

    def af(self, *shape):
        n = int(np.prod(shape[1:]))
        off = self.f_off
        self.f_off += n
        assert self.f_off <= self.f_size, ("arena overflow", self.f_off)
        ap = self.AF[:, off:off + n]
        if len(shape) == 3:
            ap = ap.rearrange("p (a b) -> p a b", a=shape[1])
        elif len(shape) == 4:
            ap = ap.rearrange("p (a b c) -> p a b c", a=shape[1], b=shape[2])
        return ap

    def ab(self, *shape):
        n = int(np.prod(shape[1:]))
        nf = (n + 1) // 2
        off = self.f_off
        self.f_off += nf
        assert self.f_off <= self.f_size, ("arena overflow", self.f_off)
        ap = self.AF[:, off:off + nf].bitcast(BF16)[:, 0:n]
        if len(shape) == 3:
            ap = ap.rearrange("p (a b) -> p a b", a=shape[1])
        elif len(shape) == 4:
            ap = ap.rearrange("p (a b c) -> p a b c", a=shape[1], b=shape[2])
        return ap

    def act(self, out, in_, func, R, W=(), A=(), bias=None, scale=1.0):
        if bias is None:
            self.p.op("scalar", lambda e: e.activation(out=out, in_=in_, func=func, scale=scale), R, W, A)
        else:
            self.p.op("scalar", lambda e: e.activation(out=out, in_=in_, func=func, bias=bias, scale=scale), R, W, A)

    def tt(self, eng, out, in0, in1, op, R, W=(), A=()):
        self.p.op(eng, lambda e: e.tensor_tensor(out=out, in0=in0, in1=in1, op=op), R, W, A)

    def ts(self, eng, out, in0, s1, s2, op0, op1, R, W=(), A=()):
        self.p.op(eng, lambda e: e.tensor_scalar(out=out, in0=in0, scalar1=s1, scalar2=s2, op0=op0, op1=op1), R, W, A)

    def ts1(self, eng, out, in0, s1, op0, R, W=(), A=()):
        self.p.op(eng, lambda e: e.tensor_single_scalar(out=out, in_=in0, scalar=s1, op=op0), R, W, A)

    def stt(self, eng, out, in0, scalar, in1, op0, op1, R, W=(), A=()):
        self.p.op(eng, lambda e: e.scalar_tensor_tensor(out=out, in0=in0, scalar=scalar, in1=in1, op0=op0, op1=op1), R, W, A)

    def cp(self, eng, out, in_, R, W=(), A=()):
        if eng == "scalar":
            self.p.op(eng, lambda e: e.copy(out=out, in_=in_), R, W, A)
        else:
            self.p.op(eng, lambda e: e.tensor_copy(out=out, in_=in_), R, W, A)

    def mm(self, out, lhsT, rhs, start, stop, R, W=(), A=()):
        self.p.op("tensor", lambda e: e.matmul(out, lhsT=lhsT, rhs=rhs, start=start, stop=stop), R, W, A)

    def mmacc(self, psi, out, pairs, R):
        n = len(pairs)
        for i, (l, r) in enumerate(pairs):
            if i == 0:
                self.mm(out, l, r, True, n == 1, R, W=[self.PB[psi]])
            else:
                self.mm(out, l, r, False, i == n - 1, R, A=[self.PB[psi]])

    def memset(self, eng, ap, val, W):
        self.p.op(eng, lambda e: e.memset(ap, val), (), W)

    def ld(self, out, in_, R, W=(), A=()):
        self.p.dma("sync", out, in_, R, W, A)

    def st(self, out, in_, R, W=(), A=()):
        self.p.dma(self.store_q, out, in_, R, W, A)

    def build(self):
        nc = self.nc
        self.store_q = "gpsimd"
        self.dbufs = {}
        with contextlib.ExitStack() as es:
            self.f_size = 51 * 1024 + 512
            self.AF = es.enter_context(nc.sbuf_tensor("arena_f", [128, self.f_size], F32))[:]
            self.PS = [es.enter_context(nc.psum_tensor("ps%d" % i, [128, 512], F32))[:] for i in range(8)]
            self.PB = bufs(8, "ps")
            self.f_off = 0
            self.consts()
            self.f_base = self.f_off
            self.marks = []
            for stg in self.stages:
                self.reset_arena()
                getattr(self, "st_" + stg[0])(*stg[1:])
                self.p.barrier()
                self.marks.append((stg, sum(1 for o in self.p.streams["tensor"] if o.fn is not None)))
            self.p.emit()
        return nc

    def consts(self):
        self.ident = self.af(128, 128)
        self.Bc = Buf("consts")
        idn = self.din("ident", [128, 128])
        self.ld(self.ident, idn, (), [self.Bc])
        self.ones_b = self.ab(128, 128)
        self.memset("vector", self.ones_b, 1.0, [Buf()])
        self.ident_b = self.ab(128, 128)
        self.cp("vector", self.ident_b, self.ident, [self.Bc], A=[self.Bc])
        self.ones_f = self.af(128, 128)
        self.memset("vector", self.ones_f, 1.0, [Buf()])
        self.eps = self.af(128, 2)
        self.memset("vector", self.eps[:, 0:1], LN_EPS, [Buf()])
        self.memset("vector", self.eps[:, 1:2], RMS_EPS, [Buf()])
        self.p.barrier()
        self.vec = {}
        self.vecB = Buf("vec")
        self.layers_mod = set()
        self.layer_vec = set()

    def load_cols(self, rows_ap, R, dst, stgq):
        stg, stgB = stgq
        self.ld(stg[0:R, :], rows_ap, (), [stgB])
        self.mm(self.PS[7][:, 0:R], stg[0:R, :], self.ident[0:R, 0:R], True, True, [stgB, self.Bc], W=[self.PB[7]])
        self.cp("vector", dst, self.PS[7][:, 0:R], [self.PB[7]], A=[self.vecB])

    def need_layer(self, l):
        if l in self.layer_vec:
            return
        self.layer_vec.add(l)
        self.f_off = self.f_base
        V = {}
        lng = self.af(128, 24)
        lnb = self.af(128, 24)
        modt = self.af(128, 72, 2)
        sc1p = self.af(128, 48)
        sh = self.af(128, 48)
        gw = self.af(128, 48)
        V.update(lng=lng, lnb=lnb, sc1p=sc1p, sh=sh, gw=gw)
        if l % 2 == 0:
            V["qng"] = self.af(128, 3)
            V["kvng"] = self.af(128, 2)
            V["sg"] = self.af(128, 512)
            V["sb"] = self.af(128, 512)
            V["bs"] = self.af(128, 512)
            V["wst"] = self.ab(128, 512)
        else:
            V["bpw1"] = self.af(128, 16)
            V["wdw"] = self.af(128, 31 * 8)
            V["bdw"] = self.af(128, 8)
            V["olng"] = self.af(128, 8)
            V["olnb"] = self.af(128, 8)
            V["bout"] = self.af(128, 8)
            V["bg"] = self.af(128, 16)
        self.f_base = self.f_off
        self.vec[l] = V
        stg = self.af(128, 128)
        stgB = Buf("stg")
        sq = (stg, stgB)
        lnga = self.din("ln_g%d" % l, [24, 128])
        lnba = self.din("ln_b%d" % l, [24, 128])
        self.load_cols(lnga, 24, lng, sq)
        self.load_cols(lnba, 24, lnb, sq)
        if l % 2 == 0:
            self.load_cols(self.din("qn%d" % l, [3, 128]), 3, V["qng"], sq)
            self.load_cols(self.din("kvn%d" % l, [2, 128]), 2, V["kvng"], sq)
            sgr = self.din("sgu_g%d" % l, [1, 512])
            sbr = self.din("sgu_b%d" % l, [1, 512])
            bsr = self.din("b_s%d" % l, [1, 512])
            for nm, src in (("sg", sgr), ("sb", sbr), ("bs", bsr)):
                self.ld(V[nm], bass.AP(src.tensor, 0, [[0, 128], [1, 512]]), (), A=[self.vecB])
            wsr = self.din("w_s%d" % l, [4, 128, 128])
            wtmp = self.af(128, 4, 128)
            wB = Buf()
            self.ld(wtmp, wsr.rearrange("g i j -> i g j"), (), [wB])
            for g in range(4):
                self.mm(self.PS[6][:, g * 128:(g + 1) * 128], wtmp[:, g, :], self.ident, True, True, [wB, self.Bc],
                        **({"W": [self.PB[6]]} if g == 0 else {"A": [self.PB[6]]}))
            self.cp("vector", V["wst"], self.PS[6], [self.PB[6]], A=[self.vecB])
        else:
            self.load_cols(self.din("b_pw1%d" % l, [16, 128]), 16, V["bpw1"], sq)
            wd = self.din("w_dw%d" % l, [248, 128])
            self.load_cols(wd[0:128, :], 128, V["wdw"][:, 0:128], sq)
            self.load_cols(wd[128:248, :], 120, V["wdw"][:, 128:248], sq)
            self.load_cols(self.din("b_dw%d" % l, [8, 128]), 8, V["bdw"], sq)
            self.load_cols(self.din("o_ln_g%d" % l, [8, 128]), 8, V["olng"], sq)
            self.load_cols(self.din("o_ln_b%d" % l, [8, 128]), 8, V["olnb"], sq)
            self.load_cols(self.din("b_out%d" % l, [8, 128]), 8, V["bout"], sq)
        cc = self.din("cc", [16, 128])
        scT = self.af(128, 16)
        scB = Buf()
        self.ld(stg[0:16, :], cc, (), [stgB])
        self.mm(self.PS[7][:, 0:16], stg[0:16, :], self.ident[0:16, 0:16], True, True, [stgB, self.Bc], W=[self.PB[7]])
        self.act(scT, self.PS[7][:, 0:16], AF.Silu, [self.PB[7]], W=[scB])
        scK = self.af(128, 8, 2)
        self.cp("vector", scK, scT.rearrange("p (w k) -> p k w", w=2), [scB], W=[scB])
        wm = self.din("w_mod%d" % l, [1024, 9216]).rearrange("(k p) n -> p k n", p=128)
        bmr = self.din("b_mod%d" % l, [1, 9216])
        bmt = [self.af(128, 512) for _ in range(2)]
        bmB = bufs(2)
        wbuf = [self.af(128, 4, 512) for _ in range(4)]
        wbB = bufs(4)
        piece = self.af(128, 512)
        pB = Buf()
        mB = Buf()
        for n in range(18):
            self.ld(bmt[n % 2][0:1, :], bmr[:, n * 512:(n + 1) * 512], (), [bmB[n % 2]])
            for hk in range(2):
                wi = (n * 2 + hk) % 4
                self.ld(wbuf[wi], wm[:, hk * 4:(hk + 1) * 4, n * 512:(n + 1) * 512], (), [wbB[wi]])
                for k4 in range(4):
                    k = hk * 4 + k4
                    self.mm(self.PS[5][0:2, :], scK[:, k, :], wbuf[wi][:, k4, :], k == 0, False, [scB, wbB[wi]],
                            **({"W": [self.PB[5]]} if k == 0 else {"A": [self.PB[5]]}))
            self.mm(self.PS[5][0:2, :], self.ones_f[0:1, 0:2], bmt[n % 2][0:1, :], False, True, [bmB[n % 2]], A=[self.PB[5]])
            self.cp("vector", piece[0:2, :], self.PS[5][0:2, :], [self.PB[5]], W=[pB])
            for q in range(4):
                self.mm(self.PS[6][:, 2 * q:2 * q + 2], piece[0:2, q * 128:(q + 1) * 128], self.ident[0:2, 0:2], True, True,
                        [pB, self.Bc], **({"W": [self.PB[6]]} if q == 0 else {"A": [self.PB[6]]}))
            self.cp("vector", modt[:, n * 4:(n + 1) * 4, :], self.PS[6][:, 0:8].rearrange("p (q w) -> p q w", w=2), [self.PB[6]], A=[mB])
        for s in range(3):
            wt = 1.0 if s == 1 else 0.5
            for w in range(2):
                o = (s * 2 + w) * 8
                self.ts1("vector", sc1p[:, o:o + 8], modt[:, (3 * s + 1) * 8:(3 * s + 1) * 8 + 8, w], 1.0, ALU.add, [mB], A=[self.vecB])
                self.cp("vector", sh[:, o:o + 8], modt[:, (3 * s) * 8:(3 * s) * 8 + 8, w], [mB], A=[self.vecB])
                self.ts1("vector", gw[:, o:o + 8], modt[:, (3 * s + 2) * 8:(3 * s + 2) * 8 + 8, w], wt, ALU.mult, [mB], A=[self.vecB])
        self.p.barrier()
        if l % 2 == 1:
            for w in range(2):
                o = (1 * 2 + w) * 8
                self.tt("vector", V["bg"][:, w * 8:w * 8 + 8], V["bout"], gw[:, o:o + 8], ALU.mult, [self.vecB], A=[self.vecB])
        self.p.barrier()
        self.f_off = self.f_base

    def coef(self, l, name, s, w, c):
        o = (s * 2 + w) * 8 + c
        return self.vec[l][name][:, o:o + 1]

    def cvt_setup(self, nb=2):
        self.cv_f = [self.af(128, 2048) for _ in range(nb)]
        self.cv_b = [self.ab(128, 2048) for _ in range(nb)]
        self.cv_fB = bufs(nb)
        self.cv_bB = bufs(nb)
        self.cv_i = 0
        self.cv_n = nb

    def cvt_emit(self, piece, ldq="sync", stq=None, engs=("vector", "gpsimd"), phase=None):
        if phase == 1:
            piece, i, eng = piece
        else:
            i = self.cv_i % self.cv_n
            eng = engs[self.cv_i % len(engs)]
            self.cv_i += 1
        loads, casts, stores, wB = piece
        stq = stq or self.store_q
        f, b, fB, bB = self.cv_f[i], self.cv_b[i], self.cv_fB[i], self.cv_bB[i]
        if phase != 1:
            for k, (dfn, src) in enumerate(loads):
                self.p.dma(ldq, dfn(f), src, (), **({"writes": [fB]} if k == 0 else {"adds": [fB]}))
        if phase == 0:
            return (piece, i, eng)
        for k, (ofn, ifn) in enumerate(casts):
            self.cp(eng, ofn(b), ifn(f), [fB], **({"W": [bB]} if k == 0 else {"A": [bB]}))
        for (dst, sfn) in stores:
            self.p.dma(stq, dst, sfn(b), [bB], (), [wB])

    def nat_pieces(self, src, K, M, dst, wB):
        out = []
        for k in range(K):
            for c0 in range(0, M, 2048):
                w = min(2048, M - c0)
                out.append(([(lambda f, w=w: f[:, 0:w], src[k * 128:(k + 1) * 128, c0:c0 + w])],
                            [(lambda b, w=w: b[:, 0:w], lambda f, w=w: f[:, 0:w])],
                            [(dst[:, k * M + c0:k * M + c0 + w], lambda b, w=w: b[:, 0:w])], wB))
        return out

    def w_pieces(self, key):
        wB = self.sbuf(("w", key))
        P = []
        kind = key[0]
        if kind == "ffn":
            _, l, i = key
            w13 = self.din("w13_%d_%d" % (l, i), [1024, 2 * DFF]).rearrange("(k p) n -> p k n", p=128)
            w2 = self.din("w2_%d_%d" % (l, i), [DFF, 1024])
            d13 = self.dtmp("w13s_%d_%d" % (l, i), [NJ, 128, 2048], BF16)
            d2 = self.dtmp("w2s_%d_%d" % (l, i), [128, NJ * 1024], BF16)
            for j in range(NJ):
                P.append((
                    [(lambda f: f.rearrange("p (k t c) -> p k t c", k=8, t=2)[:, :, 0, :], w13[:, :, j * 128:(j + 1) * 128]),
                     (lambda f: f.rearrange("p (k t c) -> p k t c", k=8, t=2)[:, :, 1, :], w13[:, :, DFF + j * 128:DFF + (j + 1) * 128])],
                    [(lambda b: b, lambda f: f)],
                    [(d13[j], lambda b: b)], wB))
            P += self.nat_pieces(w2, NJ, 1024, d2, wB)
        elif kind == "even":
            _, l = key
            win = self.din("w_in%d" % l, [1024, 1696])
            dwin = self.dtmp("wins%d" % l, [128, 8 * WIN_COLS], BF16)
            for k in range(8):
                P.append((
                    [(lambda f: f[:, 0:1696], win[k * 128:(k + 1) * 128, :])],
                    [(lambda b: b[:, 0:672], lambda f: f[:, 0:672]),
                     (lambda b: b[:, 672:704].rearrange("p (i t) -> p i t", t=2)[:, :, 0], lambda f: f[:, 640:672].rearrange("p (i t) -> p i t", t=2)[:, :, 1]),
                     (lambda b: b[:, 672:704].rearrange("p (i t) -> p i t", t=2)[:, :, 1], lambda f: f[:, 640:672].rearrange("p (i t) -> p i t", t=2)[:, :, 0]),
                     (lambda b: b[:, 704:1728], lambda f: f[:, 672:1696])],
                    [(dwin[:, k * WIN_COLS:(k + 1) * WIN_COLS], lambda b: b[:, 0:WIN_COLS])], wB))
            wq = self.din("w_q%d" % l, [384, 768])
            dwq = self.dtmp("wqs%d" % l, [128, 3 * 1024], BF16)
            fv = lambda f: f[:, 0:768].rearrange("p (h d) -> p h d", h=8)
            for k in range(3):
                P.append((
                    [(lambda f: f[:, 0:768], wq[k * 128:(k + 1) * 128, :])],
                    [(lambda b: b[:, 0:512].rearrange("p (h d) -> p h d", h=8), lambda f: fv(f)[:, :, 0:64]),
                     (lambda b: b[:, 512:768].rearrange("p (h d) -> p h d", h=8), lambda f: fv(f)[:, :, 64:96]),
                     (lambda b: b[:, 768:1024].rearrange("p (h i t) -> p h i t", h=8, t=2)[:, :, :, 0],
                      lambda f: fv(f)[:, :, 64:96].rearrange("p h (i t) -> p h i t", t=2)[:, :, :, 1]),
                     (lambda b: b[:, 768:1024].rearrange("p (h i t) -> p h i t", h=8, t=2)[:, :, :, 1],
                      lambda f: fv(f)[:, :, 64:96].rearrange("p h (i t) -> p h i t", t=2)[:, :, :, 0])],
                    [(dwq[:, k * 1024:(k + 1) * 1024], lambda b: b[:, 0:1024])], wB))
            wkv = self.din("w_kv%d" % l, [256, 1024])
            dwkv = self.dtmp("wkvs%d" % l, [128, 2 * 1024], BF16)
            fv2 = lambda f: f[:, 0:1024].rearrange("p (h d) -> p h d", h=8)
            for k in range(2):
                P.append((
                    [(lambda f: f[:, 0:1024], wkv[k * 128:(k + 1) * 128, :])],
                    [(lambda b: b[:, 0:512].rearrange("p (h d) -> p h d", h=8), lambda f: fv2(f)[:, :, 0:64]),
                     (lambda b: b[:, 512:1024].rearrange("p (h d) -> p h d", h=8), lambda f: fv2(f)[:, :, 64:128])],
                    [(dwkv[:, k * 1024:(k + 1) * 1024], lambda b: b[:, 0:1024])], wB))
            wo = self.din("w_out%d" % l, [1024, 1024])
            P += self.nat_pieces(wo, 8, 1024, self.dtmp("wos%d" % l, [128, 8 * 1024], BF16), wB)
        elif kind == "odd":
            _, l = key
            P += self.nat_pieces(self.din("w_pw1%d" % l, [1024, 2048]), 8, 2048, self.dtmp("wp1s%d" % l, [128, 8 * 2048], BF16), wB)
            P += self.nat_pieces(self.din("w_out%d" % l, [1024, 1024]), 8, 1024, self.dtmp("wos%d" % l, [128, 8 * 1024], BF16), wB)
        return P

    def need_w(self, key):
        if key in self.wdone:
            return
        self.wdone.add(key)
        mark = self.f_off
        self.cvt_setup(3)
        for piece in self.w_pieces(key):
            self.cvt_emit(piece)
        self.p.barrier()
        self.f_off = mark

    def res_ap(self, write):
        return self.state("RES", [1024, self.N], F32, write).rearrange("(c p) n -> p c n", p=128)

    def res_src(self, col0):
        if col0 in self.res_written:
            return self.res_ap(True)
        return self.din("RES_i", [1024, self.N], F32).rearrange("(c p) n -> p c n", p=128)

    def res_buf(self, col0):
        return self.sbuf(("RES", col0))

    def load_r(self, r, rB, grp):
        col0, T, w = grp
        src = self.res_src(col0)
        self.ld(r[:, :, :T], src[:, :, col0:col0 + T], [self.res_buf(col0)], W=rB)

    def store_r(self, r, rB, grp):
        col0, T, w = grp
        dst = self.res_ap(True)
        self.res_written.add(col0)
        self.st(dst[:, :, col0:col0 + T], r[:, :, :T], rB, W=[self.res_buf(col0)])

    def modulate(self, l, s, grp, r, rB, h, hB):
        col0, T, w = grp
        for c in range(8):
            eng = "vector" if c % 2 == 0 else "gpsimd"
            self.ts(eng, h[:, c, :T], r[:, c, :T], self.coef(l, "sc1p", s, w, c), self.coef(l, "sh", s, w, c), ALU.mult, ALU.add,
                    [rB[c], self.vecB], W=[hB[c]])

    def stats(self, srcs, T, F, eps_col, want_mean, tmp):
        C = len(srcs)
        xb, sq = tmp["xb"], tmp["sq"]
        xbB, sqB = tmp["xbB"], tmp["sqB"]
        for c, (x, xB) in enumerate(srcs):
            if want_mean:
                self.cp("gpsimd" if c % 2 else "vector", xb[:, c, :T], x, [xB], W=[xbB[c]])
            self.act(sq[:, c, :T], x, AF.Square, [xB], W=[sqB[c]])
        pm, pq = tmp["pm"], tmp["pq"]
        if want_mean:
            self.mmacc(pm, self.PS[pm][:, :T], [(self.ones_b, xb[:, c, :T]) for c in range(C)], list(xbB[:C]))
        self.mmacc(pq, self.PS[pq][:, :T], [(self.ones_b, sq[:, c, :T]) for c in range(C)], list(sqB[:C]))
        sB = tmp["sB"]
        rstd, nmr, mean, m2 = tmp["rstd"], tmp["nmr"], tmp["mean"], tmp["m2"]
        if want_mean:
            self.act(mean[:, :T], self.PS[pm][:, :T], AF.Copy, [self.PB[pm]], W=[sB], scale=1.0 / F)
            self.tt("vector", m2[:, :T], mean[:, :T], mean[:, :T], ALU.mult, [sB], W=[tmp["m2B"]])
            self.stt("vector", m2[:, :T], self.PS[pq][:, :T], 1.0 / F, m2[:, :T], ALU.mult, ALU.subtract, [self.PB[pq], tmp["m2B"]], W=[tmp["m2B"]])
            self.act(rstd[:, :T], m2[:, :T], AF.Sqrt, [tmp["m2B"]], W=[tmp["rsB"]], bias=self.eps[:, eps_col:eps_col + 1])
        else:
            self.act(rstd[:, :T], self.PS[pq][:, :T], AF.Sqrt, [self.PB[pq]], W=[tmp["rsB"]], bias=self.eps[:, eps_col:eps_col + 1], scale=1.0 / F)
        self.p.op("vector", lambda e: e.reciprocal(out=rstd[:, :T], in_=rstd[:, :T]), [tmp["rsB"]], [tmp["rsB"]])
        if want_mean:
            self.stt("vector", nmr[:, :T], mean[:, :T], -1.0, rstd[:, :T], ALU.mult, ALU.mult, [sB, tmp["rsB"]], W=[tmp["nmB"]])

    def stats_tmp(self, C, pm, pq, xb=None, sq=None):
        t = dict(xb=xb[0] if xb else self.ab(128, C, 512), sq=sq[0] if sq else self.ab(128, C, 512),
                 xbB=xb[1] if xb else bufs(C), sqB=sq[1] if sq else bufs(C), pm=pm, pq=pq, sB=Buf(), m2B=Buf(), rsB=Buf(), nmB=Buf(),
                 rstd=self.af(128, 512), nmr=self.af(128, 512), mean=self.af(128, 512), m2=self.af(128, 512),
                 t1=[self.af(128, 512) for _ in range(2)], t1B=bufs(2))
        return t

    def ln_apply(self, x, xB, C, T, gcol, bcol, out, outB, tmp, func=AF.Identity):
        for c in range(C):
            t1, t1B = tmp["t1"][c % 2], tmp["t1B"][c % 2]
            self.tt("vector", t1[:, :T], x[:, c, :T], tmp["rstd"][:, :T], ALU.mult, [xB[c], tmp["rsB"]], W=[t1B])
            self.tt("gpsimd", t1[:, :T], t1[:, :T], tmp["nmr"][:, :T], ALU.add, [t1B, tmp["nmB"]], W=[t1B])
            self.act(out[:, c, :T], t1[:, :T], func, [t1B, self.vecB], W=[outB[c]], bias=bcol(c), scale=gcol(c))

    def resid_ln_store(self, l, s, grp, r, rB, tmp):
        col0, T, w = grp
        V = self.vec[l]
        self.stats([(r[:, c, :T], rB[c]) for c in range(8)], T, 1024.0, 0, True, tmp)
        self.ln_apply(r, rB, 8, T, lambda c: V["lng"][:, s * 8 + c:s * 8 + c + 1], lambda c: V["lnb"][:, s * 8 + c:s * 8 + c + 1], r, rB, tmp)
        self.store_r(r, rB, grp)

    def epilogue(self, psi, m, T, r, rB, gwcol, ytmp, bias=None):
        y, yB = ytmp
        self.act(y[:, :T], self.PS[psi][:, :T], AF.Identity if bias is not None else AF.Copy, [self.PB[psi], self.vecB], W=[yB], scale=gwcol,
                 **({"bias": bias} if bias is not None else {}))
        self.stt("vector", r[:, m, :T], r[:, m, :T], ALPHA, y[:, :T], ALU.mult, ALU.add, [rB[m], yB], W=[rB[m]])

    def st_PRO(self):
        x = self.din("x", [self.nlat, 1024])
        ctx = self.din("ctx", [NCTX, 1024])
        res = self.res_ap(True)
        tin = [self.af(128, 4, 1024) for _ in range(2)]
        tinB = bufs(2)
        tout = [self.af(128, 8, 512) for _ in range(2)]
        toutB = bufs(2)
        pc = 0
        for gi, grp in enumerate(list(self.groups_lat) + [self.group_ctx]):
            col0, T, w = grp
            nt = T // 128
            b = gi % 2
            src = x[col0:col0 + T, :] if w == 0 else ctx
            self.ld(tin[b][:, 0:nt, :], src.rearrange("(t p) d -> p t d", p=128), (), W=[tinB[b]])
            first = True
            for t in range(nt):
                for hh in range(2):
                    psi = pc % 8
                    pc += 1
                    for q in range(4):
                        c = hh * 4 + q
                        self.mm(self.PS[psi][:, q * 128:(q + 1) * 128], tin[b][:, t, c * 128:(c + 1) * 128], self.ident, True, True, [tinB[b], self.Bc],
                                **({"W": [self.PB[psi]]} if q == 0 else {"A": [self.PB[psi]]}))
                    self.cp("vector" if hh == 0 else "scalar", tout[b][:, hh * 4:(hh + 1) * 4, t * 128:(t + 1) * 128],
                            self.PS[psi].rearrange("p (q t) -> p q t", q=4), [self.PB[psi]], **({"W": [toutB[b]]} if first else {"A": [toutB[b]]}))
                    first = False
            self.res_written.add(col0)
            self.st(res[:, :, col0:col0 + T], tout[b][:, :, :T], [toutB[b]], W=[self.res_buf(col0)])

    def st_EPI(self):
        out = self.dout("out", [self.nlat, 1024])
        tin = [self.af(128, 8, 512) for _ in range(2)]
        tinB = bufs(2)
        tout = [self.af(128, 1024) for _ in range(4)]
        toutB = bufs(4)
        self.outB = Buf("out")
        pc = 0
        tc = 0
        for gi, grp in enumerate(self.groups_lat):
            col0, T, w = grp
            b = gi % 2
            self.ld(tin[b], self.res_src(col0)[:, :, col0:col0 + T], [self.res_buf(col0)], W=[tinB[b]])
            for t in range(T // 128):
                ob = tc % 4
                tc += 1
                for hh in range(2):
                    psi = pc % 8
                    pc += 1
                    for q in range(4):
                        c = hh * 4 + q
                        self.mm(self.PS[psi][:, q * 128:(q + 1) * 128], tin[b][:, c, t * 128:(t + 1) * 128], self.ident, True, True, [tinB[b], self.Bc],
                                **({"W": [self.PB[psi]]} if q == 0 else {"A": [self.PB[psi]]}))
                    self.cp("vector" if hh == 0 else "scalar", tout[ob][:, hh * 512:(hh + 1) * 512], self.PS[psi], [self.PB[psi]],
                            **({"W": [toutB[ob]]} if hh == 0 else {"A": [toutB[ob]]}))
                self.st(out[col0 + t * 128:col0 + (t + 1) * 128, :], tout[ob], [toutB[ob]], A=[self.outB])

    def st_F(self, l, s, with_ctx):
        i = 0 if s == 0 else 1
        self.need_layer(l)
        self.need_w(("ffn", l, i))
        V = self.vec[l]
        d13 = self.dram["w13s_%d_%d" % (l, i)]
        d2 = self.dram["w2s_%d_%d" % (l, i)]
        wB = self.sbuf(("w", ("ffn", l, i)))
        W2 = self.ab(128, NJ, 1024)
        W2B = Buf()

        def load_W2():
            for q in range(2):
                self.ld(W2[:, q * 11:(q + 1) * 11, :], d2[:, q * 11 * 1024:(q + 1) * 11 * 1024].rearrange("p (j m) -> p j m", j=11), [wB],
                        **({"W": [W2B]} if q == 0 else {"A": [W2B]}))
        w13 = [self.ab(128, 8, 256) for _ in range(2)]
        w13B = bufs(2)
        S = 2
        r = [self.af(128, 8, 512) for _ in range(3)]
        rB = [bufs(8) for _ in range(3)]
        h = [self.ab(128, 8, 512) for _ in range(S)]
        hB = [bufs(8) for _ in range(S)]
        actt = [self.ab(128, NJ, 512) for _ in range(S)]
        actB = [Buf() for _ in range(S)]
        sg = [self.af(128, 512) for _ in range(2)]
        sgB = bufs(2)
        ytmp = [(self.af(128, 512), Buf()) for _ in range(2)]
        tmp = self.stats_tmp(8, 0, 1, xb=(h[0], hB[0]), sq=(h[1], hB[1]))
        groups = list(self.groups_lat)
        passes = [groups[a:a + S] for a in range(0, len(groups), S)]
        if with_ctx:
            passes.append([self.group_ctx])
        ridx = {}
        for pi, ps_ in enumerate(passes):
            if pi == 0:
                for si in range(len(ps_)):
                    ridx[(pi, si)] = si
            else:
                used = [ridx[(pi - 1, si)] for si in range(len(passes[pi - 1]))]
                free = [x for x in range(3) if x not in used]
                ridx[(pi, 0)] = free[0]
                if len(ps_) > 1:
                    ridx[(pi, 1)] = ridx[(pi - 1, 0)]

        def ln_stats(grp, ri):
            col0, T, w = grp
            self.stats([(r[ri][:, c, :T], rB[ri][c]) for c in range(8)], T, 1024.0, 0, True, tmp)

        def ln_apply_store(grp, ri):
            col0, T, w = grp
            self.ln_apply(r[ri], rB[ri], 8, T, lambda c: V["lng"][:, s * 8 + c:s * 8 + c + 1], lambda c: V["lnb"][:, s * 8 + c:s * 8 + c + 1],
                          r[ri], rB[ri], tmp)
            self.store_r(r[ri], rB[ri], grp)

        for si, grp in enumerate(passes[0]):
            self.load_r(r[ridx[(0, si)]], rB[ridx[(0, si)]], grp)
            self.modulate(l, s, grp, r[ridx[(0, si)]], rB[ridx[(0, si)]], h[si], hB[si])
        cnt = 0
        for pi, ps_ in enumerate(passes):
            nxt = passes[pi + 1] if pi + 1 < len(passes) else []
            for j in range(NJ):
                wb = j % 2
                self.ld(w13[wb], d13[j].rearrange("p (k c) -> p k c", k=8), [wB], W=[w13B[wb]])
                if pi == 0 and j == 1:
                    load_W2()
                for si, grp in enumerate(ps_):
                    col0, T, w = grp
                    pg = (cnt % 3) * 2
                    pu = pg + 1
                    cnt += 1
                    self.mmacc(pg, self.PS[pg][:, :T], [(w13[wb][:, k, 0:128], h[si][:, k, :T]) for k in range(8)], [w13B[wb]] + hB[si])
                    self.mmacc(pu, self.PS[pu][:, :T], [(w13[wb][:, k, 128:256], h[si][:, k, :T]) for k in range(8)], [w13B[wb]] + hB[si])
                    sgi = cnt % 2
                    self.act(sg[sgi][:, :T], self.PS[pg][:, :T], AF.Silu, [self.PB[pg]], W=[sgB[sgi]])
                    self.tt("vector", actt[si][:, j, :T], self.PS[pu][:, :T], sg[sgi][:, :T], ALU.mult, [self.PB[pu], sgB[sgi]],
                            **({"W": [actB[si]]} if j == 0 else {"A": [actB[si]]}))
            for si, grp in enumerate(ps_):
                col0, T, w = grp
                ri = ridx[(pi, si)]
                for m in range(8):
                    py = 6 + (m % 2)
                    self.mmacc(py, self.PS[py][:, :T], [(W2[:, j, m * 128:(m + 1) * 128], actt[si][:, j, :T]) for j in range(NJ)], [W2B, actB[si]])
                    self.epilogue(py, m, T, r[ri], rB[ri], self.coef(l, "gw", s, w, m), ytmp[m % 2])
                    if si > 0 and m == 3:
                        pr = ridx[(pi, si - 1)]
                        ln_stats(ps_[si - 1], pr)
                        ln_apply_store(ps_[si - 1], pr)
                        if len(nxt) > 1:
                            rn = ridx[(pi + 1, 1)]
                            self.load_r(r[rn], rB[rn], nxt[1])
                if si == 0 and nxt:
                    rn = ridx[(pi + 1, 0)]
                    self.load_r(r[rn], rB[rn], nxt[0])
            last = len(ps_) - 1
            rl = ridx[(pi, last)]
            ln_stats(ps_[last], rl)
            for si, grp in enumerate(nxt):
                rn = ridx[(pi + 1, si)]
                if len(ps_) == 1 and si == 1:
                    self.load_r(r[rn], rB[rn], grp)
                self.modulate(l, s, grp, r[rn], rB[rn], h[si], hB[si])
            ln_apply_store(ps_[last], rl)

    def st_E2(self, l, with_ctx, ctx_out):
        self.need_layer(l)
        self.need_w(("even", l))
        V = self.vec[l]
        wB = self.sbuf(("w", ("even", l)))
        N = self.N
        WIN = self.ab(128, 8, WIN_COLS)
        WQ = self.ab(128, 3, 1024)
        WB_ = Buf()
        self.ld(WIN, self.dram["wins%d" % l].rearrange("p (k c) -> p k c", k=8), [wB], W=[WB_])
        self.ld(WQ, self.dram["wqs%d" % l].rearrange("p (k c) -> p k c", k=3), [wB], A=[WB_])
        ropec = self.din("ropec", [128, self.nlat])
        ropes = self.din("ropes", [128, self.nlat])
        QTN = self.state("QTN", [512, N], BF16, True).rearrange("(c p) n -> p c n", p=128)
        QTR = self.state("QTR", [256, N], BF16, True).rearrange("(c p) n -> p c n", p=128)
        BL = self.state("BL", [512, N], BF16, True).rearrange("(c p) n -> p c n", p=128)
        CKV = self.state("CKV", [256, N], BF16, True).rearrange("(c p) n -> p c n", p=128)
        KR = self.state("KR", [32, N], BF16, True)
        r2 = [self.af(128, 8, 512) for _ in range(2)]
        rB2 = [bufs(8) for _ in range(2)]
        h2 = [self.ab(128, 8, 512) for _ in range(2)]
        hB2 = [bufs(8) for _ in range(2)]
        cosT = self.af(128, 512)
        sinT = self.af(128, 512)
        rpB = Buf()
        cq = self.af(128, 3, 512)
        cqB = bufs(3)
        cqn = self.ab(128, 3, 512)
        cqnB = bufs(3)
        ckv = self.af(128, 2, 512)
        ckvB = bufs(2)
        ckvn = self.ab(128, 2, 512)
        ckvnB = bufs(2)
        qn = self.ab(128, 4, 512)
        qnB = Buf()
        qr = self.ab(128, 2, 512)
        qrB = Buf()
        krt = self.ab(128, 512)
        krB = Buf()
        u = self.af(128, 4, 512)
        uB = bufs(4)
        vg = [self.af(128, 512) for _ in range(2)]
        vgB = bufs(2)
        vb = [self.ab(128, 512) for _ in range(2)]
        vbB = bufs(2)
        bnst2 = [self.af(128, 8) for _ in range(2)]
        bnB2 = bufs(2)
        mx2 = [self.af(128, 512) for _ in range(2)]
        mxB2 = bufs(2)
        bl = self.ab(128, 4, 512)
        blB = Buf()
        t1 = [self.af(128, 512) for _ in range(2)]
        t1B = bufs(2)
        tmp = self.stats_tmp(3, 0, 1)
        groups = list(self.groups_lat) + ([self.group_ctx] if with_ctx else [])
        pc = 0

        def nextps():
            nonlocal pc
            v = 2 + (pc % 6)
            pc += 1
            return v

        def x_after_group(gi):
            col0, T, w = groups[gi]
            cw = min(self.nlat, 1024)
            self.x_flush()
            if w == 1:
                self.x_even_ctx(l)
            elif (col0 + T) % cw == 0:
                self.x_even_chunk(l, (col0 + T) // cw - 1)
            if gi == len(groups) - 1:
                self.x_flush()
                self.x_done.add(l)

        self.load_r(r2[0], rB2[0], groups[0])
        self.modulate(l, 1, groups[0], r2[0], rB2[0], h2[0], hB2[0])
        for gi, grp in enumerate(groups):
            col0, T, w = grp
            full = (w == 0) or ctx_out
            h, hB = h2[gi % 2], hB2[gi % 2]
            if gi + 1 < len(groups):
                nb = (gi + 1) % 2
                self.load_r(r2[nb], rB2[nb], groups[gi + 1])
                self.modulate(l, 1, groups[gi + 1], r2[nb], rB2[nb], h2[nb], hB2[nb])
            if w == 0:
                self.ld(cosT[:, :T], ropec[:, col0:col0 + T], (), W=[rpB])
                self.ld(sinT[:, :T], ropes[:, col0:col0 + T], (), A=[rpB])
            hR = [WB_] + hB
            for c in range(2):
                psi = nextps()
                self.mmacc(psi, self.PS[psi][:, :T], [(WIN[:, k, 384 + c * 128:384 + (c + 1) * 128], h[:, k, :T]) for k in range(8)], hR)
                self.cp("scalar", ckv[:, c, :T], self.PS[psi][:, :T], [self.PB[psi]], W=[ckvB[c]])
            if full:
                for c in range(3):
                    psi = nextps()
                    self.mmacc(psi, self.PS[psi][:, :T], [(WIN[:, k, c * 128:(c + 1) * 128], h[:, k, :T]) for k in range(8)], hR)
                    self.cp("scalar", cq[:, c, :T], self.PS[psi][:, :T], [self.PB[psi]], W=[cqB[c]])
            pa = nextps()
            self.mmacc(pa, self.PS[pa][0:32, :T], [(WIN[:, k, 640:672], h[:, k, :T]) for k in range(8)], hR)
            if w == 0:
                pb = nextps()
                self.mmacc(pb, self.PS[pb][0:32, :T], [(WIN[:, k, 672:704], h[:, k, :T]) for k in range(8)], hR)
                self.tt("vector", t1[0][0:32, :T], self.PS[pa][0:32, :T], cosT[0:32, :T], ALU.mult, [self.PB[pa], rpB], W=[t1B[0]])
                self.tt("vector", t1[1][0:32, :T], self.PS[pb][0:32, :T], sinT[0:32, :T], ALU.mult, [self.PB[pb], rpB], W=[t1B[1]])
                self.tt("gpsimd", krt[0:32, :T], t1[0][0:32, :T], t1[1][0:32, :T], ALU.add, [t1B[0], t1B[1]], W=[krB])
            else:
                self.cp("scalar", krt[0:32, :T], self.PS[pa][0:32, :T], [self.PB[pa]], W=[krB])
            self.st(KR[:, col0:col0 + T], krt[0:32, :T], [krB], W=[self.sbuf(("KR", col0))])
            if full:
                for c in range(4):
                    psi = nextps()
                    self.mmacc(psi, self.PS[psi][:, :T], [(WIN[:, k, 704 + c * 128:704 + (c + 1) * 128], h[:, k, :T]) for k in range(8)], hR)
                    self.act(u[:, c, :T], self.PS[psi][:, :T], AF.Gelu, [self.PB[psi]], W=[uB[c]])
            self.stats([(ckv[:, c, :T], ckvB[c]) for c in range(2)], T, 256.0, 1, False, tmp)
            for c in range(2):
                self.tt("vector", t1[c][:, :T], ckv[:, c, :T], tmp["rstd"][:, :T], ALU.mult, [ckvB[c], tmp["rsB"]], W=[t1B[c]])
                self.act(ckvn[:, c, :T], t1[c][:, :T], AF.Copy, [t1B[c], self.vecB], W=[ckvnB[c]], scale=V["kvng"][:, c:c + 1])
            self.st(CKV[:, :, col0:col0 + T], ckvn[:, :, :T], ckvnB, W=[self.sbuf(("CKV", col0))])
            if not full:
                if self.fused and X_OVERLAP:
                    x_after_group(gi)
                continue
            self.stats([(cq[:, c, :T], cqB[c]) for c in range(3)], T, 384.0, 1, False, tmp)
            for c in range(3):
                self.tt("vector", t1[c % 2][:, :T], cq[:, c, :T], tmp["rstd"][:, :T], ALU.mult, [cqB[c], tmp["rsB"]], W=[t1B[c % 2]])
                self.act(cqn[:, c, :T], t1[c % 2][:, :T], AF.Copy, [t1B[c % 2], self.vecB], W=[cqnB[c]], scale=V["qng"][:, c:c + 1])

            def q_part():
                qR = [WB_] + cqnB
                for c in range(4):
                    psi = nextps()
                    self.mmacc(psi, self.PS[psi][:, :T], [(WQ[:, k, c * 128:(c + 1) * 128], cqn[:, k, :T]) for k in range(3)], qR)
                    self.cp("scalar", qn[:, c, :T], self.PS[psi][:, :T], [self.PB[psi]], **({"W": [qnB]} if c == 0 else {"A": [qnB]}))
                self.st(QTN[:, :, col0:col0 + T], qn[:, :, :T], [qnB], W=[self.sbuf(("QTN", col0))])
                for c in range(2):
                    pa = nextps()
                    self.mmacc(pa, self.PS[pa][:, :T], [(WQ[:, k, 512 + c * 128:512 + (c + 1) * 128], cqn[:, k, :T]) for k in range(3)], qR)
                    wa = {"W": [qrB]} if c == 0 else {"A": [qrB]}
                    if w == 0:
                        pb = nextps()
                        self.mmacc(pb, self.PS[pb][:, :T], [(WQ[:, k, 768 + c * 128:768 + (c + 1) * 128], cqn[:, k, :T]) for k in range(3)], qR)
                        self.tt("vector", t1[0][:, :T], self.PS[pa][:, :T], cosT[:, :T], ALU.mult, [self.PB[pa], rpB], W=[t1B[0]])
                        self.tt("vector", t1[1][:, :T], self.PS[pb][:, :T], sinT[:, :T], ALU.mult, [self.PB[pb], rpB], W=[t1B[1]])
                        self.tt("gpsimd", qr[:, c, :T], t1[0][:, :T], t1[1][:, :T], ALU.add, [t1B[0], t1B[1]], **wa)
                    else:
                        self.cp("scalar", qr[:, c, :T], self.PS[pa][:, :T], [self.PB[pa]], **wa)
                self.st(QTR[:, :, col0:col0 + T], qr[:, :, :T], [qrB], W=[self.sbuf(("QTR", col0))])

            nsub = T // 128
            for ci in range(nsub):
                if ci == nsub // 2:
                    q_part()
                tk = slice(ci * 128, (ci + 1) * 128)
                b2 = ci % 2
                bn, bnB_, mx_, mxB_ = bnst2[b2], bnB2[b2], mx2[b2], mxB2[b2]
                psi = nextps()
                self.mmacc(psi, self.PS[psi], [(h[:, k, tk], WIN[:, k, 1216:1728]) for k in range(8)], hR)
                self.act(vg[b2], self.PS[psi], AF.Gelu, [self.PB[psi]], W=[vgB[b2]])
                self.p.op("vector", lambda e, b2=b2, bn=bn: e.bn_stats(out=bn[:, 0:6], in_=vg[b2]), [vgB[b2]], [bnB_])
                self.p.op("vector", lambda e, bn=bn: e.bn_aggr(out=bn[:, 6:8], in_=bn[:, 0:6]), [bnB_], [bnB_])
                self.act(bn[:, 7:8], bn[:, 7:8], AF.Sqrt, [bnB_], W=[bnB_], bias=self.eps[:, 0:1])
                self.p.op("vector", lambda e, bn=bn: e.reciprocal(out=bn[:, 7:8], in_=bn[:, 7:8]), [bnB_], [bnB_])
                self.ts("vector", vg[b2], vg[b2], bn[:, 6:7], bn[:, 7:8], ALU.subtract, ALU.mult, [vgB[b2], bnB_], W=[vgB[b2]])
                self.tt("gpsimd", vg[b2], vg[b2], V["sg"], ALU.mult, [vgB[b2], self.vecB], W=[vgB[b2]])
                self.tt("vector", vb[b2], vg[b2], V["sb"], ALU.add, [vgB[b2], self.vecB], W=[vbB[b2]])
                psm = nextps()
                for g in range(4):
                    self.mm(self.PS[psm][:, g * 128:(g + 1) * 128], vb[b2][:, g * 128:(g + 1) * 128], V["wst"][:, g * 128:(g + 1) * 128], True, True,
                            [vbB[b2], self.vecB], **({"W": [self.PB[psm]]} if g == 0 else {"A": [self.PB[psm]]}))
                self.tt("vector", mx_, self.PS[psm], V["bs"], ALU.add, [self.PB[psm], self.vecB], W=[mxB_])
                self.tt("gpsimd", bl[:, :, tk], u[:, :, tk], mx_.rearrange("p (g i) -> p g i", g=4), ALU.mult, uB + [mxB_],
                        **({"W": [blB]} if ci == 0 else {"A": [blB]}))
            self.st(BL[:, :, col0:col0 + T], bl[:, :, :T], [blB], W=[self.sbuf(("BL", col0))])
            if self.fused and X_OVERLAP:
                x_after_group(gi)

    def x_even_ctx(self, l):
        nlat, N, nk = self.nlat, self.N, self.nk
        CKV = self.state("CKV", [256, N], BF16, False)
        KR = self.state("KR", [32, N], BF16, False)
        CKVA = self.state("CKVA", [256, nk], BF16, True)
        KRA = self.state("KRA", [32, nk], BF16, True)
        Ba = self.sbuf(("KVA",))
        self.ld(CKVA[:, 0:NCTX], CKV[:, nlat:N], [self.sbuf(("CKV", nlat))], A=[Ba])
        self.ld(KRA[:, 0:NCTX], KR[:, nlat:N], [self.sbuf(("KR", nlat))], A=[Ba])

    def x_even_chunk(self, l, k):
        nlat, N, nk = self.nlat, self.N, self.nk
        pairs = [[2 * i, 2 * i + 1] for i in range(self.ncores // 2)]
        cw = min(nlat, 1024)
        CKV = self.state("CKV", [256, N], BF16, False)
        KR = self.state("KR", [32, N], BF16, False)
        CKVA = self.state("CKVA", [256, nk], BF16, True)
        KRA = self.state("KRA", [32, nk], BF16, True)
        Ba = self.sbuf(("KVA",))
        xin = self.dtmp("xin%d_%d" % (l, k), [288, cw], BF16)
        xout = self.dtmp("xout%d_%d" % (l, k), [576, cw], BF16)
        Bi, Bo = Buf(), Buf()
        srcs = [self.sbuf((nm, c0)) for nm in ("CKV", "KR") for c0 in range(k * cw, (k + 1) * cw, 512)]
        self.ld(xin[0:256, :], CKV[:, k * cw:(k + 1) * cw], srcs, W=[Bi])
        self.ld(xin[256:288, :], KR[:, k * cw:(k + 1) * cw], srcs, A=[Bi])
        self.p.coll(lambda e, xin=xin, xout=xout: e.collective_compute("AllGather", ALU.bypass, replica_groups=pairs, ins=[xin], outs=[xout]), [Bi], [Bo])

        def post():
            for r in range(2):
                c0 = NCTX + r * nlat + k * cw
                self.ld(CKVA[:, c0:c0 + cw], xout[r * 288:r * 288 + 256, :], [Bo], A=[Ba])
                self.ld(KRA[:, c0:c0 + cw], xout[r * 288 + 256:(r + 1) * 288, :], [Bo], A=[Ba])
        self.x_pending.append(post)

    def x_flush(self):
        while self.x_pending:
            self.x_pending.pop(0)()

    def st_X(self, l):
        nlat, N, nk = self.nlat, self.N, self.nk
        pairs = [[2 * i, 2 * i + 1] for i in range(self.ncores // 2)]
        if l in self.x_done:
            return
        if l % 2 == 0:
            self.x_even_ctx(l)
            for k in range(nlat // min(nlat, 1024)):
                self.x_even_chunk(l, k)
            self.x_flush()
        else:
            self.x_odd_pre(l)
            self.x_flush()

    def x_odd_pre(self, l):
        nlat, N, nk = self.nlat, self.N, self.nk
        pairs = [[2 * i, 2 * i + 1] for i in range(self.ncores // 2)]
        GLH = self.state("GLH", [1024, nlat + 30], BF16, False)
        GLC = self.state("GLC", [1024, NCTX + 30], BF16, False)
        xin = self.dtmp("xin%d" % l, [1024, 32], BF16)
        xout = self.dtmp("xout%d" % l, [2048, 32], BF16)
        hm = self.din("hmask", [128, 2])
        Bi, Bo, Ba = Buf(), Buf(), self.sbuf(("GLH",))
        srcs = [self.sbuf(("GL", 0)), self.sbuf(("GL", nlat - 512))]
        self.ld(xin[:, 0:15], GLH[:, 15:30], srcs, W=[Bi])
        self.ld(xin[:, 15:30], GLH[:, nlat:nlat + 15], srcs, A=[Bi])
        self.p.coll(lambda e: e.collective_compute("AllGather", ALU.bypass, replica_groups=pairs, ins=[xin], outs=[xout]), [Bi], [Bo])
        hl = self.ab(128, 8, 32)
        hmt = self.af(128, 2)
        h2 = self.ab(128, 8, 32)
        z = self.ab(128, 8, 16)

        def post():
            hB = Buf()
            self.ld(hmt, hm, (), W=[hB])
            xo = xout.rearrange("(r c p) n -> r p c n", r=2, p=128)
            self.ld(hl[:, :, 0:15], xo[0][:, :, 15:30], [Bo], A=[hB])
            self.ld(hl[:, :, 15:30], xo[1][:, :, 0:15], [Bo], A=[hB])
            h2B = Buf()
            self.ts1("vector", h2[:, :, 0:15], hl[:, :, 0:15], hmt[:, 0:1], ALU.mult, [hB], W=[h2B])
            self.ts1("vector", h2[:, :, 15:30], hl[:, :, 15:30], hmt[:, 1:2], ALU.mult, [hB], A=[h2B])
            zB = Buf()
            self.memset("vector", z, 0.0, [zB])
            GLHv = GLH.rearrange("(c p) n -> p c n", p=128)
            GLCv = GLC.rearrange("(c p) n -> p c n", p=128)
            self.st(GLHv[:, :, 0:15], h2[:, :, 0:15], [h2B], A=[Ba])
            self.st(GLHv[:, :, nlat + 15:nlat + 30], h2[:, :, 15:30], [h2B], A=[Ba])
            self.st(GLCv[:, :, 0:15], z[:, :, 0:15], [zB], A=[Ba])
            self.st(GLCv[:, :, NCTX + 15:NCTX + 30], z[:, :, 0:15], [zB], A=[Ba])
        self.x_pending.append(post)

    def st_E4(self, l, ctx_out):
        self.need_layer(l)
        self.need_w(("even", l))
        wB = self.sbuf(("w", ("even", l)))
        N, nk = self.N, self.nk
        nkt = nk // 128
        CKVA = self.state("CKVA", [256, nk], BF16, False).rearrange("(c p) n -> p c n", p=128)
        KRA = self.state("KRA", [32, nk], BF16, False)
        QTN = self.state("QTN", [512, N], BF16, False)
        QTR = self.state("QTR", [256, N], BF16, False)
        AT = self.state("AT", [512, N], BF16, True)
        KN = self.dtmp("KN%d" % l, [512, nk], BF16)
        Bin = self.sbuf(("KVA",))
        mark_b = self.f_off
        ck = self.ab(128, 2, nk)
        ck_end = self.f_off
        ckB = Buf()
        self.ld(ck, CKVA, [Bin], W=[ckB])
        WKV = self.ab(128, 2, 1024)
        WKVB = Buf()
        self.ld(WKV, self.dram["wkvs%d" % l].rearrange("p (k c) -> p k c", k=2), [wB], W=[WKVB])
        Vall = self.ab(128, 8, nkt, 65)
        VB = Buf()
        self.memset("gpsimd", Vall, 1.0, [VB])
        kst = [self.ab(128, 512) for _ in range(2)]
        kstB = bufs(2)
        KNB = Buf()
        cnt = 0
        for c in range(4):
            for k0 in range(0, nk, 512):
                kw = min(512, nk - k0)
                psi = cnt % 4
                b = cnt % 2
                cnt += 1
                self.mmacc(psi, self.PS[psi][:, :kw], [(WKV[:, k2, c * 128:(c + 1) * 128], ck[:, k2, k0:k0 + kw]) for k2 in range(2)], [WKVB, ckB])
                self.cp("scalar" if b else "vector", kst[b][:, :kw], self.PS[psi][:, :kw], [self.PB[psi]], W=[kstB[b]])
                self.st(KN[c * 128:(c + 1) * 128, k0:k0 + kw], kst[b][:, :kw], [kstB[b]], A=[KNB])
        for kt in range(nkt):
            psi = cnt % 4
            cnt += 1
            self.mmacc(psi, self.PS[psi], [(ck[:, k2, kt * 128:(kt + 1) * 128], WKV[:, k2, 512:1024]) for k2 in range(2)], [WKVB, ckB])
            self.cp("scalar" if kt % 2 else "vector", Vall[:, :, kt, 1:65], self.PS[psi].rearrange("p (h d) -> p h d", h=8), [self.PB[psi]], A=[VB])
        self.p.barrier()
        save = self.f_off
        self.f_off = mark_b
        KH = [self.ab(128, nk) for _ in range(2)]
        assert self.f_off <= ck_end
        self.f_off = save
        KHB = bufs(2)
        Q = [self.ab(128, 512) for _ in range(2)]
        QB = bufs(2)
        PT = [self.ab(128, 512) for _ in range(4)]
        PTB = bufs(4)
        rec = self.af(128, 512)
        recB = Buf()
        recb = self.af(128, 512)
        recbB = Buf()
        an = [self.ab(128, 512) for _ in range(2)]
        anB = bufs(2)
        qgroups = list(self.groups_lat) + ([self.group_ctx] if ctx_out else [])
        blocks = [(hd, grp) for hd in range(8) for grp in qgroups]
        steps = []
        for bi, (hd, grp) in enumerate(blocks):
            kts = list(range(nkt)) if grp[2] == 0 else list(range(NCTX // 128))
            for ii, kt in enumerate(kts):
                steps.append((bi, kt, ii, len(kts)))

        def load_K(hd):
            kb = hd % 2
            self.ld(KH[kb][0:64, :], KN[hd * 64:(hd + 1) * 64, :], [KNB], W=[KHB[kb]])
            self.ld(KH[kb][64:96, :], KRA, [Bin], A=[KHB[kb]])

        def load_Q(bi):
            hd, (col0, T, w) = blocks[bi]
            qb = bi % 2
            self.ld(Q[qb][0:64, :T], QTN[hd * 64:(hd + 1) * 64, col0:col0 + T], [self.sbuf(("QTN", col0))], W=[QB[qb]])
            self.ld(Q[qb][64:96, :T], QTR[hd * 32:(hd + 1) * 32, col0:col0 + T], [self.sbuf(("QTR", col0))], A=[QB[qb]])

        load_K(0)
        load_Q(0)
        SK = 3
        BG_PLAN = {0: [("ffn", 0, 1), ("ffn", 1, 0), ("odd", 1), ("ffn", 1, 1), ("ffn", 2, 0), ("even", 2)],
                   2: [("ffn", 2, 1), ("ffn", 3, 0), ("odd", 3), ("ffn", 3, 1)]}
        jobs = []
        for key in BG_PLAN.get(l, []):
            if key not in self.wdone:
                self.wdone.add(key)
                jobs += self.w_pieces(key)
        self.cvt_setup(3)
        jobs.reverse()

        pend = []

        def hook():
            if jobs:
                pend.append(self.cvt_emit(jobs.pop(), ldq="sync", stq="sync", engs=("gpsimd",), phase=0))
            if pend and (len(pend) > 1 or not jobs):
                self.cvt_emit(pend.pop(0), ldq="sync", stq="sync", engs=("gpsimd",), phase=1)
        ns = len(steps)
        for idx in range(ns + SK):
            if idx < ns:
                bi, kt, ii, nkk = steps[idx]
                hd, (col0, T, w) = blocks[bi]
                if ii == 0:
                    if bi + 1 < len(blocks):
                        if blocks[bi + 1][0] != hd:
                            load_K(hd + 1)
                        load_Q(bi + 1)
                psi = idx % 4
                self.mm(self.PS[psi][:, :T], KH[hd % 2][0:96, kt * 128:(kt + 1) * 128], Q[bi % 2][0:96, :T], True, True,
                        [KHB[hd % 2], QB[bi % 2]], W=[self.PB[psi]])
                self.act(PT[psi][:, :T], self.PS[psi][:, :T], AF.Exp, [self.PB[psi]], W=[PTB[psi]], scale=ATT_SCALE)
                if idx % 16 == 4:
                    hook()
            j = idx - SK
            if j < 0:
                continue
            bi, kt, ii, nkk = steps[j]
            hd, (col0, T, w) = blocks[bi]
            po = 4 + (bi % 2)
            pb = j % 4
            self.mm(self.PS[po][0:65, :T], Vall[:, hd, kt, :], PT[pb][:, :T], ii == 0, ii == nkk - 1, [VB, PTB[pb]],
                    **({"W": [self.PB[po]]} if ii == 0 else {"A": [self.PB[po]]}))
            if ii != nkk - 1:
                continue
            self.p.op("vector", lambda e, po=po, T=T: e.reciprocal(out=rec[0:1, :T], in_=self.PS[po][0:1, :T]), [self.PB[po]], [recB])
            self.mm(self.PS[6][0:65, :T], self.ones_f[0:1, 0:65], rec[0:1, :T], True, True, [recB], W=[self.PB[6]])
            self.cp("scalar", recb[0:65, :T], self.PS[6][0:65, :T], [self.PB[6]], W=[recbB])
            ab_ = bi % 2
            self.tt("vector", an[ab_][0:65, :T], self.PS[po][0:65, :T], recb[0:65, :T], ALU.mult, [self.PB[po], recbB], W=[anB[ab_]])
            self.p.dma("sync", AT[hd * 64:(hd + 1) * 64, col0:col0 + T], an[ab_][1:65, :T], [anB[ab_]], (), [self.sbuf(("AT", col0))])
        while jobs or pend:
            hook()

    def st_E5(self, l, ctx_out):
        self.need_layer(l)
        self.need_w(("even", l))
        wB = self.sbuf(("w", ("even", l)))
        N = self.N
        AT = self.state("AT", [512, N], BF16, False).rearrange("(c p) n -> p c n", p=128)
        BL = self.state("BL", [512, N], BF16, False).rearrange("(c p) n -> p c n", p=128)
        WO = self.ab(128, 8, 1024)
        WOB = Buf()
        self.ld(WO, self.dram["wos%d" % l].rearrange("p (k c) -> p k c", k=8), [wB], W=[WOB])
        r = [self.af(128, 8, 512) for _ in range(2)]
        rB = [bufs(8) for _ in range(2)]
        ab_ = [self.ab(128, 8, 512) for _ in range(2)]
        abB = bufs(2)
        ytmp = [(self.af(128, 512), Buf()) for _ in range(2)]
        tmp = self.stats_tmp(8, 0, 1)
        groups = list(self.groups_lat) + ([self.group_ctx] if ctx_out else [])
        for gi, grp in enumerate(groups):
            col0, T, w = grp
            b = gi % 2
            self.load_r(r[b], rB[b], grp)
            self.ld(ab_[b][:, 0:4, :T], AT[:, :, col0:col0 + T], [self.sbuf(("AT", col0))], W=[abB[b]])
            self.ld(ab_[b][:, 4:8, :T], BL[:, :, col0:col0 + T], [self.sbuf(("BL", col0))], A=[abB[b]])
            for m in range(8):
                py = 2 + (m % 4)
                self.mmacc(py, self.PS[py][:, :T], [(WO[:, k, m * 128:(m + 1) * 128], ab_[b][:, k, :T]) for k in range(8)], [WOB, abB[b]])
                self.epilogue(py, m, T, r[b], rB[b], self.coef(l, "gw", 1, w, m), ytmp[m % 2])
            self.resid_ln_store(l, 1, grp, r[b], rB[b], tmp)

    def st_O1(self, l, ctx_out):
        self.need_layer(l)
        self.need_w(("odd", l))
        V = self.vec[l]
        wB = self.sbuf(("w", ("odd", l)))
        N = self.N
        if self.fused:
            GLHw = self.state("GLH", [1024, self.nlat + 30], BF16, True).rearrange("(c p) n -> p c n", p=128)
            GLCw = self.state("GLC", [1024, NCTX + 30], BF16, True).rearrange("(c p) n -> p c n", p=128)
        else:
            GL = self.state("GL", [1024, N], BF16, True).rearrange("(c p) n -> p c n", p=128)
        WP = self.ab(128, 8, 2048)
        WPB = Buf()
        self.ld(WP, self.dram["wp1s%d" % l].rearrange("p (k c) -> p k c", k=8), [wB], W=[WPB])
        r = [self.af(128, 8, 512)] * 2
        rB = [bufs(8)] * 2
        h = [self.ab(128, 8, 512) for _ in range(2)]
        hB = [bufs(8) for _ in range(2)]
        gl = [self.ab(128, 8, 512) for _ in range(2)]
        glB = bufs(2)
        sg = [self.af(128, 512) for _ in range(2)]
        sgB = bufs(2)
        groups = list(self.groups_lat) + ([self.group_ctx] if ctx_out else [])
        cnt = 0
        for gi, grp in enumerate(groups):
            col0, T, w = grp
            b = gi % 2
            self.load_r(r[b], rB[b], grp)
            self.modulate(l, 1, grp, r[b], rB[b], h[b], hB[b])
            for m in range(8):
                pa = (cnt % 4) * 2
                pg = pa + 1
                cnt += 1
                self.mmacc(pa, self.PS[pa][:, :T], [(WP[:, k, m * 128:(m + 1) * 128], h[b][:, k, :T]) for k in range(8)], [WPB] + hB[b])
                self.mmacc(pg, self.PS[pg][:, :T], [(WP[:, k, 1024 + m * 128:1024 + (m + 1) * 128], h[b][:, k, :T]) for k in range(8)], [WPB] + hB[b])
                si = cnt % 2
                self.act(sg[si][:, :T], self.PS[pg][:, :T], AF.Sigmoid, [self.PB[pg], self.vecB], W=[sgB[si]], bias=V["bpw1"][:, 8 + m:9 + m])
                self.stt("vector", gl[b][:, m, :T], self.PS[pa][:, :T], V["bpw1"][:, m:m + 1], sg[si][:, :T], ALU.add, ALU.mult,
                         [self.PB[pa], sgB[si], self.vecB], **({"W": [glB[b]]} if m == 0 else {"A": [glB[b]]}))
            if not self.fused:
                dstv = GL[:, :, col0:col0 + T]
            elif w == 0:
                dstv = GLHw[:, :, 15 + col0:15 + col0 + T]
            else:
                dstv = GLCw[:, :, 15:15 + T]
            self.st(dstv, gl[b][:, :, :T], [glB[b]], W=[self.sbuf(("GL", col0))])
            if self.fused and X_OVERLAP and w == 0 and col0 + T == self.nlat:
                self.x_odd_pre(l)
        if self.fused and X_OVERLAP:
            self.x_flush()
            self.x_done.add(l)

    def st_O2(self, l, ctx_out):
        self.need_layer(l)
        self.need_w(("odd", l))
        V = self.vec[l]
        wB = self.sbuf(("w", ("odd", l)))
        GLH = self.state("GLH", [1024, self.nlat + 30], BF16, False).rearrange("(c p) n -> p c n", p=128)
        Bin = self.sbuf(("GLH",))
        WO = self.ab(128, 8, 1024)
        WOB = Buf()
        self.ld(WO, self.dram["wos%d" % l].rearrange("p (k c) -> p k c", k=8), [wB], W=[WOB])
        xh = [self.ab(128, 8, 542) for _ in range(2)]
        xhB = bufs(2)
        diag = self.ab(128, 248, 128)
        dB = Buf()
        for wc in range(248):
            self.ts1("vector", diag[:, wc, :], self.ident_b, V["wdw"][:, wc:wc + 1], ALU.mult, [self.vecB, self.Bc],
                     **({"W": [dB]} if wc == 0 else {"A": [dB]}))
        acc2 = [self.af(128, 8, 512) for _ in range(2)]
        accB2 = [bufs(8) for _ in range(2)]
        hs2 = [self.ab(128, 8, 512)] * 2
        hsB2 = [bufs(8)] * 2
        r2 = [self.af(128, 8, 512)] * 2
        rB2 = [bufs(8)] * 2
        NPE = 8
        ytmp = [(self.af(128, 512), Buf()) for _ in range(2)]
        tmp = self.stats_tmp(8, 0, 1, xb=(hs2[0], hsB2[0]))
        groups = list(self.groups_lat)
        if ctx_out:
            GLC = self.state("GLC", [1024, NCTX + 30], BF16, False).rearrange("(c p) n -> p c n", p=128)
            groups.append(self.group_ctx)
        for gi, grp in enumerate(groups):
            col0, T, w = grp
            b = gi % 2
            if w == 0:
                self.ld(xh[b][:, :, :T + 30], GLH[:, :, col0:col0 + T + 30], [Bin], W=[xhB[b]])
            else:
                self.ld(xh[b][:, :, :T + 30], GLC[:, :, 0:T + 30], [Bin], W=[xhB[b]])
            acc, accB, hs, hsB, r, rB = acc2[b], accB2[b], hs2[b], hsB2[b], r2[b], rB2[b]
            self.load_r(r, rB, grp)
            for c in range(NPE, 8):
                self.ts("vector", acc[:, c, :T], xh[b][:, c, 0:T], V["wdw"][:, c:c + 1], V["bdw"][:, c:c + 1], ALU.mult, ALU.add,
                        [xhB[b], self.vecB], W=[accB[c]])
                for wi in range(1, 31):
                    self.stt("vector", acc[:, c, :T], xh[b][:, c, wi:wi + T], V["wdw"][:, wi * 8 + c:wi * 8 + c + 1], acc[:, c, :T], ALU.mult, ALU.add,
                             [xhB[b], accB[c], self.vecB], W=[accB[c]])
            for c in range(NPE):
                psi = 2 + (c % 4)
                self.mmacc(psi, self.PS[psi][:, :T], [(diag[:, wi * 8 + c, :], xh[b][:, c, wi:wi + T]) for wi in range(31)], [dB, xhB[b]])
                self.act(acc[:, c, :T], self.PS[psi][:, :T], AF.Identity, [self.PB[psi], self.vecB], W=[accB[c]], bias=V["bdw"][:, c:c + 1])
            self.stats([(acc[:, c, :T], accB[c]) for c in range(8)], T, 1024.0, 0, True, tmp)
            self.ln_apply(acc, accB, 8, T, lambda c: V["olng"][:, c:c + 1], lambda c: V["olnb"][:, c:c + 1], hs, hsB, tmp, func=AF.Silu)
            for m in range(8):
                py = 2 + (m % 4)
                self.mmacc(py, self.PS[py][:, :T], [(WO[:, k, m * 128:(m + 1) * 128], hs[:, k, :T]) for k in range(8)], [WOB] + hsB)
                self.epilogue(py, m, T, r, rB, self.coef(l, "gw", 1, w, m), ytmp[m % 2], bias=V["bg"][:, w * 8 + m:w * 8 + m + 1])
            self.resid_ln_store(l, 1, grp, r, rB, tmp)


STATE_SHAPES = None


def rope_tables(nlat, half):
    t = np.arange(half * nlat, (half + 1) * nlat)
    rows = (t // GRID_W).astype(np.float32)
    cols = (t % GRID_W).astype(np.float32)
    hf = 16
    inv = (1.0 / (np.float32(10000.0) ** (np.arange(0, hf, 2, dtype=np.float32) / np.float32(hf)))).astype(np.float32)
    ang = np.concatenate([rows[:, None] * inv, cols[:, None] * inv], axis=-1).astype(np.float32)
    c = np.cos(ang).astype(np.float32)
    s = np.sin(ang).astype(np.float32)
    C = np.repeat(c, 2, axis=1)
    S = np.repeat(s, 2, axis=1)
    S[:, 0::2] *= -1.0
    C = np.tile(C, (1, 4)).T.copy()
    S = np.tile(S, (1, 4)).T.copy()
    return C, S


def core_inputs(inp, core, nlat, names):
    b, half = core // 2, core % 2
    f = lambda a: np.ascontiguousarray(a, dtype=np.float32)
    d = {}
    for n in names:
        if n == "ident":
            d[n] = np.eye(128, dtype=np.float32)
        elif n == "hmask":
            d[n] = np.tile(np.array([[half, 1 - half]], np.float32), (128, 1))
        elif n == "x":
            d[n] = f(inp["x"][b, half * nlat:(half + 1) * nlat, :])
        elif n == "ctx":
            d[n] = f(inp["ctx"][b])
        elif n == "cc":
            d[n] = f(np.stack([inp["c"][b], inp["c_ctx"]]).reshape(16, 128))
        elif n == "ropec" or n == "ropes":
            C, S = rope_tables(nlat, half)
            d[n] = C if n == "ropec" else S
        elif n.startswith("w_mod"):
            d[n] = f(inp["w_mod"][int(n[5:])])
        elif n.startswith("b_mod"):
            d[n] = f(inp["b_mod"][int(n[5:])][None, :])
        elif n.startswith("ln_g"):
            d[n] = f(inp["ln_g"][int(n[4:])].reshape(24, 128))
        elif n.startswith("ln_b"):
            d[n] = f(inp["ln_b"][int(n[4:])].reshape(24, 128))
        elif n.startswith("w13_"):
            l, i = int(n[4]), int(n[6])
            d[n] = f(inp["ffn_w13"][l, i])
        elif n.startswith("w2_"):
            l, i = int(n[3]), int(n[5])
            d[n] = f(inp["ffn_w2"][l, i])
        elif n.startswith("w_in"):
            d[n] = f(inp["e_w_in"][int(n[4:]) // 2])
        elif n.startswith("w_q"):
            d[n] = f(inp["e_w_q_up"][int(n[3:]) // 2])
        elif n.startswith("w_kv"):
            d[n] = f(inp["e_w_kv_up"][int(n[4:]) // 2])
        elif n.startswith("w_out"):
            l = int(n[5:])
            d[n] = f(inp["e_w_out"][l // 2] if l % 2 == 0 else inp["o_w_out"][l // 2])
        elif n.startswith("qn"):
            d[n] = f(inp["e_q_norm"][int(n[2:]) // 2].reshape(3, 128))
        elif n.startswith("kvn"):
            d[n] = f(inp["e_kv_norm"][int(n[3:]) // 2].reshape(2, 128))
        elif n.startswith("sgu_g"):
            d[n] = f(inp["e_sgu_g"][int(n[5:]) // 2][None, :])
        elif n.startswith("sgu_b"):
            d[n] = f(inp["e_sgu_b"][int(n[5:]) // 2][None, :])
        elif n.startswith("b_s"):
            d[n] = f(inp["e_b_s"][int(n[3:]) // 2].reshape(1, 512))
        elif n.startswith("w_s"):
            d[n] = f(inp["e_w_s"][int(n[3:]) // 2])
        elif n.startswith("w_pw1"):
            d[n] = f(inp["o_w_pw1"][int(n[5:]) // 2])
        elif n.startswith("b_pw1"):
            d[n] = f(inp["o_b_pw1"][int(n[5:]) // 2].reshape(16, 128))
        elif n.startswith("w_dw"):
            d[n] = f(inp["o_w_dw"][int(n[4:]) // 2].reshape(248, 128))
        elif n.startswith("b_dw"):
            d[n] = f(inp["o_b_dw"][int(n[4:]) // 2].reshape(8, 128))
        elif n.startswith("o_ln_g"):
            d[n] = f(inp["o_ln_g"][int(n[6:]) // 2].reshape(8, 128))
        elif n.startswith("o_ln_b"):
            d[n] = f(inp["o_ln_b"][int(n[6:]) // 2].reshape(8, 128))
        elif n.startswith("b_out"):
            d[n] = f(inp["o_b_out"][int(n[5:]) // 2].reshape(8, 128))
        else:
            raise KeyError(n)
    return d


SEG_STATE = [
    ((), ("RES", "QTN", "QTR", "BL", "CKV", "KR")),
    (("RES", "QTN", "QTR", "BL", "CKVA", "KRA"), ("RES", "GL")),
    (("RES", "GLH", "GLC"), ("RES", "QTN", "QTR", "BL", "CKV", "KR")),
    (("RES", "QTN", "QTR", "BL", "CKVA", "KRA"), ("RES", "GL")),
    (("RES", "GLH"), ()),
]

_prog_cache = {}


def get_prog(nlat, si):
    key = (nlat, si)
    if key not in _prog_cache:
        segs = segments()
        b = Builder(nlat, segs[si], SEG_STATE[si][0], SEG_STATE[si][1])
        b.build()
        _prog_cache[key] = b
    return _prog_cache[key]


def exchange(si, states, nlat, ncores):
    new = [dict() for _ in range(ncores)]
    for c in range(ncores):
        st = states[c]
        mate = states[c ^ 1]
        half = c % 2
        lo, hi = (st, mate) if half == 0 else (mate, st)
        n = new[c]
        n["RES_i"] = st["RES_o"]
        if "CKV_o" in st:
            for nm in ("QTN", "QTR", "BL"):
                n[nm + "_i"] = st[nm + "_o"]
            n["CKVA_i"] = np.concatenate([st["CKV_o"][:, nlat:], lo["CKV_o"][:, :nlat], hi["CKV_o"][:, :nlat]], axis=1)
            n["KRA_i"] = np.concatenate([st["KR_o"][:, nlat:], lo["KR_o"][:, :nlat], hi["KR_o"][:, :nlat]], axis=1)
        if "GL_o" in st:
            gl = st["GL_o"]
            z = np.zeros((1024, 15), gl.dtype)
            left = z if half == 0 else mate["GL_o"][:, nlat - 15:nlat]
            right = mate["GL_o"][:, 0:15] if half == 0 else z
            n["GLH_i"] = np.concatenate([left, gl[:, :nlat], right], axis=1)
            n["GLC_i"] = np.concatenate([z, gl[:, nlat:], z], axis=1)
    return new


def run_all(inp, nlat, ncores, seg_range=None, debug=None):
    states = [dict() for _ in range(ncores)]
    nseg = len(SEG_STATE)
    rng = range(nseg) if seg_range is None else seg_range
    for si in rng:
        b = get_prog(nlat, si)
        in_maps = []
        for c in range(ncores):
            names = [n for n in b.inputs if not n.endswith("_i")]
            d = core_inputs(inp, c, nlat, names)
            for n in b.inputs:
                if n.endswith("_i"):
                    d[n] = np.ascontiguousarray(states[c][n])
            in_maps.append(d)
        res = run_bass_kernel_spmd(b.nc, in_maps, core_ids=list(range(ncores)))
        outs = res.results
        if debug is not None:
            debug.append(outs)
        if si == nseg - 1:
            return outs
        states = exchange(si, outs, nlat, ncores)
    return states


def get_fused(nlat, ncores=8):
    key = (nlat, "fused", ncores)
    if key not in _prog_cache:
        b = Builder(nlat, stage_list(), (), (), fused=True, ncores=ncores)
        b.build()
        _prog_cache[key] = b
    return _prog_cache[key]


def run_fused(inp, nlat, ncores):
    b = get_fused(nlat, ncores)
    in_maps = [core_inputs(inp, c, nlat, b.inputs) for c in range(ncores)]
    res = run_bass_kernel_spmd(b.nc, in_maps, core_ids=list(range(ncores)))
    return res.results


FUSED = True


def kernel(**inputs):
    inp = {k: np.asarray(v) for k, v in inputs.items()}
    B, S, _ = inp["x"].shape
    nlat = S // 2
    ncores = 2 * B
    outs = run_fused(inp, nlat, ncores) if FUSED else run_all(inp, nlat, ncores)
    out = np.empty((B, S, D), np.float32)
    for c in range(ncores):
        out[c // 2, (c % 2) * nlat:(c % 2 + 1) * nlat, :] = outs[c]["out"]
    return out
```
